# Optimizing a Trainium2 kernel written in Bass

```python
import jax, jax.numpy as jnp
from jax import lax
import numpy as np

D_MODEL = 4096
BATCH = 2
SEQ = 8192
DEPTH = 2

N_MIXERS = 2
N_A = (DEPTH + 1) // 2
N_B = DEPTH // 2

ML_HEADS = 8
ML_V_DIM = D_MODEL // ML_HEADS
ML_QK_DIM = ML_V_DIM // 2
ML_CHUNK = 64
GATE_SOFTCAP = 15.0
ML_IN_COLS = ML_HEADS * (2 * ML_QK_DIM + ML_V_DIM) + D_MODEL + 2 * ML_HEADS

MB_HEADS = 32
MB_HEAD_DIM = D_MODEL // MB_HEADS
MB_BLOCK = 256
MB_TOPK = 3
MB_QUERY_CHUNK = 16

D_FF = -(-(8 * D_MODEL // 3) // 256) * 256
CONV_WIDTH = 3
NORM_EPS = 1e-6

kernel_name = "hybrid_mlstm_moba_convffn"


def rmsnorm(x, g):
    xf = x.astype(jnp.float32)
    y = xf * lax.rsqrt(jnp.mean(xf * xf, axis=-1, keepdims=True) + NORM_EPS)
    return (y * g.astype(jnp.float32)).astype(x.dtype)


def mlstm_mixer(xn, w_in, gate_bias, head_gain, w_out):
    bsz, seq, _ = xn.shape
    H, dk, dv, L = ML_HEADS, ML_QK_DIM, ML_V_DIM, ML_CHUNK
    f32 = jnp.float32
    proj = xn @ w_in
    s1 = H * dk
    s2 = 2 * H * dk
    s3 = s2 + H * dv
    s4 = s3 + D_MODEL
    o = proj[..., s3:s4]

    def heads(t, d):
        return t.reshape(bsz, seq, H, d).transpose(0, 2, 1, 3).astype(f32)

    q = heads(proj[..., :s1], dk)
    k = heads(proj[..., s1:s2], dk) * (dk ** -0.5)
    v = heads(proj[..., s2:s3], dv)
    g = (proj[..., s4:] + gate_bias).astype(f32)
    g = GATE_SOFTCAP * jnp.tanh(g / GATE_SOFTCAP)
    i_pre = g[..., :H].transpose(0, 2, 1)
    log_f = jax.nn.log_sigmoid(g[..., H:]).transpose(0, 2, 1)

    nc = seq // L

    def to_chunks(t):
        t = t.reshape(t.shape[:2] + (nc, L) + t.shape[3:])
        return jnp.moveaxis(t, 2, 0)

    tril = jnp.tril(jnp.ones((L, L), dtype=bool))

    def step(carry, inp):
        C, n, m = carry
        qc, kc, vc, ic, lfc = inp
        b = jnp.cumsum(lfc, axis=-1)
        dmat = b[..., :, None] - b[..., None, :] + ic[..., None, :]
        dmat = jnp.where(tril, dmat, -jnp.inf)
        inter = b + m[..., None]
        m_row = jnp.maximum(jnp.max(dmat, axis=-1), inter)
        a_inter = jnp.exp(inter - m_row)
        s = jnp.einsum('bhjd,bhid->bhji', qc, kc) * jnp.exp(dmat - m_row[..., None])
        num = (a_inter[..., None] * jnp.einsum('bhjd,bhde->bhje', qc, C)
               + jnp.einsum('bhji,bhie->bhje', s, vc))
        den = a_inter * jnp.einsum('bhjd,bhd->bhj', qc, n) + jnp.sum(s, axis=-1)
        h = num / jnp.maximum(jnp.abs(den), jnp.exp(-m_row))[..., None]
        b_last = b[..., -1]
        dec_i = b_last[..., None] - b + ic
        m_new = jnp.maximum(b_last + m, jnp.max(dec_i, axis=-1))
        w_i = jnp.exp(dec_i - m_new[..., None])
        a_old = jnp.exp(b_last + m - m_new)
        kw = kc * w_i[..., None]
        C_new = a_old[..., None, None] * C + jnp.einsum('bhid,bhie->bhde', kw, vc)
        n_new = a_old[..., None] * n + jnp.sum(kw, axis=2)
        return (C_new, n_new, m_new), h

    init = (jnp.zeros((bsz, H, dk, dv), f32), jnp.zeros((bsz, H, dk), f32),
            jnp.zeros((bsz, H), f32))
    _, hs = lax.scan(step, init, (to_chunks(q), to_chunks(k), to_chunks(v),
                                  to_chunks(i_pre), to_chunks(log_f)))
    h = jnp.moveaxis(hs, 0, 2).reshape(bsz, H, seq, dv)
    h = h * lax.rsqrt(jnp.mean(h * h, axis=-1, keepdims=True) + NORM_EPS)
    h = h.transpose(0, 2, 1, 3).reshape(bsz, seq, D_MODEL) * head_gain.astype(f32)
    h = h * jax.nn.sigmoid(o.astype(f32))
    return h.astype(xn.dtype) @ w_out


def moba_mixer(xn, w_qkv, w_out):
    bsz, seq, _ = xn.shape
    H, dh, bs = MB_HEADS, MB_HEAD_DIM, MB_BLOCK
    f32 = jnp.float32
    q, k, v = jnp.split(xn @ w_qkv, 3, axis=-1)

    def heads(t):
        return t.reshape(bsz, seq, H, dh).transpose(0, 2, 1, 3)

    q, k, v = heads(q), heads(k), heads(v)
    n_blk = -(-seq // bs)
    pad = n_blk * bs - seq
    kp = jnp.pad(k, ((0, 0), (0, 0), (0, pad), (0, 0)))
    vp = jnp.pad(v, ((0, 0), (0, 0), (0, pad), (0, 0)))
    kb = kp.reshape(bsz, H, n_blk, bs, dh)
    vb = vp.reshape(bsz, H, n_blk, bs, dh)
    k_mean = jnp.mean(kb.astype(f32), axis=3)
    gate = jnp.einsum('bhsd,bhnd->bhsn', q.astype(f32), k_mean)
    q_blk = jnp.arange(seq) // bs
    fully_past = jnp.arange(n_blk)[None, :] < q_blk[:, None]
    gate = jnp.where(fully_past, gate, -jnp.inf)
    k_sel = min(MB_TOPK, n_blk)
    _, sel_idx = lax.top_k(gate, k_sel)
    sel_valid = jnp.arange(k_sel)[None, :] < q_blk[:, None]

    qc_len = MB_QUERY_CHUNK
    nq = seq // qc_len
    q_ch = jnp.moveaxis(q.reshape(bsz, H, nq, qc_len, dh), 2, 0)
    idx_ch = jnp.moveaxis(sel_idx.reshape(bsz, H, nq, qc_len, k_sel), 2, 0)
    valid_ch = sel_valid.reshape(nq, qc_len, k_sel)
    gather = jax.vmap(jax.vmap(lambda blocks, ix: blocks[ix]))
    scale = dh ** -0.5

    def attend(args):
        ci, qc, ic, vm = args
        t0 = ci * qc_len
        own = t0 // bs
        k_g = gather(kb, ic)
        v_g = gather(vb, ic)
        s_sel = jnp.einsum('bhqd,bhqjpd->bhqjp', qc, k_g).astype(f32) * scale
        s_sel = jnp.where(vm[None, None, :, :, None], s_sel, -jnp.inf)
        s_sel = s_sel.reshape(bsz, H, qc_len, k_sel * bs)
        k_own = lax.dynamic_slice_in_dim(kp, own * bs, bs, axis=2)
        v_own = lax.dynamic_slice_in_dim(vp, own * bs, bs, axis=2)
        s_own = jnp.einsum('bhqd,bhpd->bhqp', qc, k_own).astype(f32) * scale
        causal = (own * bs + jnp.arange(bs))[None, :] <= (t0 + jnp.arange(qc_len))[:, None]
        s_own = jnp.where(causal, s_own, -jnp.inf)
        p = jax.nn.softmax(jnp.concatenate([s_sel, s_own], axis=-1), axis=-1)
        p_sel = p[..., :k_sel * bs].reshape(bsz, H, qc_len, k_sel, bs)
        p_own = p[..., k_sel * bs:]
        out = (jnp.einsum('bhqjp,bhqjpe->bhqe', p_sel, v_g.astype(f32))
               + jnp.einsum('bhqp,bhpe->bhqe', p_own, v_own.astype(f32)))
        return out.astype(qc.dtype)

    out = lax.map(attend, (jnp.arange(nq), q_ch, idx_ch, valid_ch))
    out = jnp.moveaxis(out, 0, 2).reshape(bsz, H, seq, dh)
    out = out.transpose(0, 2, 1, 3).reshape(bsz, seq, D_MODEL)
    return out @ w_out


def conv_ffn(xn, w_up, conv_w, conv_b, w_down):
    seq = xn.shape[1]
    h = xn @ w_up
    hp = jnp.pad(h, ((0, 0), (CONV_WIDTH - 1, 0), (0, 0)))
    h = conv_b + sum(conv_w[j] * hp[:, j:j + seq] for j in range(CONV_WIDTH))
    gate, up = jnp.split(h, 2, axis=-1)
    return (jax.nn.silu(gate) * up) @ w_down


def setup_inputs(seed: int = 0) -> dict:
    key = jax.random.key(seed)
    ks = jax.random.split(key, 16)
    f32 = jnp.float32
    nrm = jax.random.normal
    D, F2 = D_MODEL, 2 * D_FF
    x = nrm(ks[0], (BATCH, SEQ, D), f32)
    norm_mix = 1.0 + 0.02 * nrm(ks[1], (DEPTH, D), f32)
    norm_ffn = 1.0 + 0.02 * nrm(ks[2], (DEPTH, D), f32)
    a_w_in = nrm(ks[3], (N_A, D, ML_IN_COLS), f32) * D ** -0.5
    kb1, kb2 = jax.random.split(ks[4])
    a_gate_bias = jnp.concatenate([
        0.1 * nrm(kb1, (N_A, ML_HEADS), f32),
        3.0 + 0.1 * nrm(kb2, (N_A, ML_HEADS), f32)], axis=-1)
    a_head_norm = 1.0 + 0.02 * nrm(ks[5], (N_A, D), f32)
    a_w_out = nrm(ks[6], (N_A, D, D), f32) * D ** -0.5
    b_w_qkv = nrm(ks[7], (N_B, D, 3 * D), f32) * D ** -0.5
    b_w_out = nrm(ks[8], (N_B, D, D), f32) * D ** -0.5
    ffn_w_up = nrm(ks[9], (DEPTH, D, F2), f32) * D ** -0.5
    ffn_conv_w = nrm(ks[10], (DEPTH, CONV_WIDTH, F2), f32) * CONV_WIDTH ** -0.5
    ffn_conv_b = 0.01 * nrm(ks[11], (DEPTH, F2), f32)
    ffn_w_down = nrm(ks[12], (DEPTH, D_FF, D), f32) * D_FF ** -0.5
    final_norm = 1.0 + 0.02 * nrm(ks[13], (D,), f32)
    return {"x": x, "norm_mix": norm_mix, "norm_ffn": norm_ffn,
            "a_w_in": a_w_in, "a_gate_bias": a_gate_bias, "a_head_norm": a_head_norm,
            "a_w_out": a_w_out, "b_w_qkv": b_w_qkv, "b_w_out": b_w_out,
            "ffn_w_up": ffn_w_up, "ffn_conv_w": ffn_conv_w, "ffn_conv_b": ffn_conv_b,
            "ffn_w_down": ffn_w_down, "final_norm": final_norm}


def reference(x, norm_mix, norm_ffn, a_w_in, a_gate_bias, a_head_norm, a_w_out,
              b_w_qkv, b_w_out, ffn_w_up, ffn_conv_w, ffn_conv_b, ffn_w_down, final_norm):
    for i in range(DEPTH):
        h = rmsnorm(x, norm_mix[i])
        j = i // N_MIXERS
        if i % N_MIXERS == 0:
            x = x + mlstm_mixer(h, a_w_in[j], a_gate_bias[j], a_head_norm[j], a_w_out[j])
        else:
            x = x + moba_mixer(h, b_w_qkv[j], b_w_out[j])
        h = rmsnorm(x, norm_ffn[i])
        x = x + conv_ffn(h, ffn_w_up[i], ffn_conv_w[i], ffn_conv_b[i], ffn_w_down[i])
    return rmsnorm(x, final_norm)
```

```python
import numpy as np
from contextlib import ExitStack
import concourse.bass as bass
import concourse.mybir as mybir
from concourse.bass_utils import run_bass_kernel_spmd

F32 = mybir.dt.float32
BF16 = mybir.dt.bfloat16
AF = mybir.ActivationFunctionType
ALU = mybir.AluOpType
AX = mybir.AxisListType

D_MODEL = 4096
NORM_EPS = 1e-6
N_CORES = 8


class Buf:
    __slots__ = ("name", "last_w", "readers", "dsem", "dcount")

    def __init__(self, name):
        self.name = name
        self.last_w = None
        self.readers = []
        self.dsem = None
        self.dcount = 0


class Op:
    __slots__ = ("eng", "fn", "waits", "signal", "idx", "is_dma", "dsem", "dcount", "count")

    def __init__(self, eng, fn, idx, is_dma):
        self.eng = eng
        self.fn = fn
        self.idx = idx
        self.is_dma = is_dma
        self.waits = []
        self.signal = False
        self.dsem = None
        self.dcount = 0
        self.count = 0


ENGS = ("pe", "act", "dve", "pool", "sp")


class Prog:
    def __init__(self, nc, stack, same_engine_sync=True):
        self.nc = nc
        self.stack = stack
        self.ops = {e: [] for e in ENGS}
        self.esem = {e: stack.enter_context(nc.semaphore("es_" + e)) for e in ENGS}
        self.seen_e = {e: {} for e in ENGS}
        self.seen_d = {e: {} for e in ENGS}
        self.same_engine_sync = same_engine_sync
        self.nbuf = 0
        self.ndsem = 0

    def buf(self, name="b"):
        self.nbuf += 1
        return Buf(f"{name}{self.nbuf}")

    def bufs(self, n, name="b"):
        return [self.buf(name) for _ in range(n)]

    def _get_dsem(self, b):
        if b.dsem is None:
            self.ndsem += 1
            b.dsem = self.stack.enter_context(self.nc.semaphore(f"ds{self.ndsem}"))
        return b.dsem

    def op(self, eng, fn, reads=(), writes=(), dma=False, sem_buf=None):
        lst = self.ops[eng]
        o = Op(eng, fn, len(lst), dma)
        deps = []
        for b in reads:
            if b.last_w is not None:
                deps.append(b.last_w)
        for b in writes:
            if b.last_w is not None:
                deps.append(b.last_w)
            deps.extend(b.readers)
        emax = {}
        dmax = {}
        for d in deps:
            if d.is_dma:
                k = id(d.dsem)
                if k not in dmax or dmax[k][1] < d.dcount:
                    dmax[k] = (d.dsem, d.dcount)
            else:
                if d.eng == eng and not dma:
                    if eng == "pe" or not self.same_engine_sync:
                        continue
                if d.eng not in emax or emax[d.eng].idx < d.idx:
                    emax[d.eng] = d
        se = self.seen_e[eng]
        for e2, d in emax.items():
            if se.get(e2, -1) >= d.idx:
                continue
            se[e2] = d.idx
            d.signal = True
            o.waits.append(("e", d))
        sd = self.seen_d[eng]
        for k, (sem, cnt) in dmax.items():
            if sd.get(k, -1) >= cnt:
                continue
            sd[k] = cnt
            o.waits.append(("d", sem, cnt))
        if dma:
            sb = sem_buf
            if sb is None:
                sb = writes[0] if writes else reads[0]
            o.dsem = self._get_dsem(sb)
            sb.dcount += 16
            o.dcount = sb.dcount
        for b in reads:
            b.readers.append(o)
        for b in writes:
            b.last_w = o
            b.readers = []
        lst.append(o)
        return o

    def finish(self, eng="sp"):
        o = Op(eng, None, len(self.ops[eng]), False)
        seen = {}
        for e in ENGS:
            for p in self.ops[e]:
                if p.is_dma:
                    k = id(p.dsem)
                    if k not in seen or seen[k][1] < p.dcount:
                        seen[k] = (p.dsem, p.dcount)
        for k, (sem, cnt) in seen.items():
            o.waits.append(("d", sem, cnt))
        for e in ENGS:
            if e != eng and self.ops[e]:
                last = None
                for p in reversed(self.ops[e]):
                    if not p.is_dma and p.fn is not None:
                        last = p
                        break
                if last is not None:
                    last.signal = True
                    o.waits.append(("e", last))
        self.ops[eng].append(o)

    def emit(self):
        nc = self.nc
        for e in ENGS:
            c = 0
            for o in self.ops[e]:
                if o.signal:
                    c += 1
                    o.count = c
        handles = {"pe": "tensor", "act": "scalar", "dve": "vector", "pool": "gpsimd", "sp": "sync"}
        with nc.Block() as block:
            for e in ENGS:
                if not self.ops[e]:
                    continue

                def body(engine, e=e):
                    for o in self.ops[e]:
                        for w in o.waits:
                            if w[0] == "e":
                                engine.wait_ge(self.esem[w[1].eng], w[1].count)
                            else:
                                engine.wait_ge(w[1], w[2])
                        if o.fn is None:
                            continue
                        ins = o.fn(engine)
                        if o.is_dma:
                            ins.then_inc(o.dsem, 16)
                        elif o.signal:
                            ins.then_inc(self.esem[e], 1)

                getattr(block, handles[e])(body)


def lay_w(W):
    K, N = W.shape
    NC = -(-N // 128)
    if NC * 128 != N:
        Wp = np.zeros((K, NC * 128), W.dtype)
        Wp[:, :N] = W
        W = Wp
    KC = K // 128
    return np.ascontiguousarray(W.reshape(KC, 128, NC, 128).transpose(2, 1, 0, 3))


def lay_vec(v, pad_to=None):
    n = v.shape[0]
    NC = -(-n // 128)
    if NC * 128 != n:
        vp = np.zeros((NC * 128,), v.dtype)
        vp[:n] = v
        v = vp
    return np.ascontiguousarray(v.reshape(NC, 128).T)


def _split_last(n, cap=2048):
    for b in range(min(n, cap), 0, -1):
        if n % b == 0:
            return b
    return 1


def build_dense(cfg):
    T, TT, HL, D = cfg["T"], cfg["TT"], cfg["HL"], cfg["D"]
    KC = D // 128
    NT = T // TT
    TW = TT + HL
    mix, ffn, nxt, final = cfg["mix"], cfg["ffn"], cfg["nxt"], cfg["final"]
    F = cfg.get("F", 0)
    FC = F // 128
    NG = cfg.get("NG", 1)
    nc = bass.Bass("TRN2", target_bir_lowering=False)

    def din(name, shape):
        return nc.dram_tensor(name, list(shape), F32, kind="ExternalInput")

    def dout(name, shape):
        return nc.dram_tensor(name, list(shape), F32, kind="ExternalOutput")

    xT = din("xT", (NT, D, TW))
    if mix:
        hT = din("hT", (NT, D, TW))
        w_mix = din("w_mix", (KC, 128, KC, 128))
    if ffn:
        g_ffn = din("g_ffn", (128, KC))
        w_up = din("w_up", (2 * FC, 128, KC, 128))
        cw = din("cw", (128, 3, 2 * FC))
        cb = din("cb", (128, 2 * FC))
        w_dn = din("w_dn", (KC, 128, FC, 128))
    if nxt:
        NOC = -(-nxt // 128)
        g_nxt = din("g_nxt", (128, KC))
        w_nxt = din("w_nxt", (NOC, 128, KC, 128))
        projT = dout("projT", (NOC * 128, T))
    if final:
        g_fin = din("g_fin", (128, KC))
    if ffn or final:
        xoT = dout("xoT", (D, T))

    with ExitStack() as st:
        P = Prog(nc, st)

        def sb(name, shape, dt):
            return st.enter_context(nc.sbuf_tensor(name, list(shape), dt))

        def ps(name, shape, dt=F32):
            return st.enter_context(nc.psum_tensor(name, list(shape), dt))

        x1 = sb("x1", (128, KC, TW), F32)
        x1_b = P.bufs(KC, "x1_")
        hn = sb("hn", (128, KC, TW), BF16)
        hn_b = P.bufs(KC, "hn_")
        ones = sb("ones", (128, 128), F32)
        ones_b = P.buf("ones")
        sq = [sb(f"sq{i}", (128, TW), F32) for i in range(2)]
        sq_b = P.bufs(2, "sq")
        rstd = sb("rstd", (128, TW), F32)
        rstd_b = P.buf("rstd")
        NWB = 3
        if ffn:
            gsz = [FC // NG + (1 if i < FC % NG else 0) for i in range(NG)]
            gst = [sum(gsz[:i]) for i in range(NG)]
            FG = max(gsz)
        KW = max(KC, FG if ffn else 0)
        wt = [sb(f"wt{i}", (128, KW, 128), BF16) for i in range(NWB)]
        wt_b = P.bufs(NWB, "wt")
        NEV = 3
        ev = [sb(f"ev{i}", (128, TT), F32) for i in range(NEV)]
        ev_b = P.bufs(NEV, "ev")
        gvec = {}
        if ffn:
            gt = sb("gt", (128, FG, TT), BF16)
            gt_b = P.bufs(FG, "gt_")
            ucat = [sb(f"uc{i}", (128, TW), F32) for i in range(4)]
            ucat_b = P.bufs(4, "uc")
            cacc = [sb(f"ca{i}", (128, TT), F32) for i in range(4)]
            cacc_b = P.bufs(4, "ca")
            cw_s = sb("cw_s", (128, 3, 2 * FC), F32)
            cb_s = sb("cb_s", (128, 2 * FC), F32)
            cwb = P.buf("cw")
            gvec["ffn"] = (sb("g_ffn_s", (128, KC), F32), P.buf("gffn"), g_ffn)
        if mix:
            hb = hn
            hb_b = hn_b
            hst = [sb(f"hst{i}", (128, TW), F32) for i in range(2)]
            hst_b = P.bufs(2, "hst")
        if nxt:
            gvec["nxt"] = (sb("g_nxt_s", (128, KC), F32), P.buf("gnxt"), g_nxt)
        if final:
            gvec["fin"] = (sb("g_fin_s", (128, KC), F32), P.buf("gfin"), g_fin)
        NPS = 4
        acc = [ps(f"acc{i}", (128, TT)) for i in range(NPS)]
        acc_b = P.bufs(NPS, "acc")
        ssp = ps("ssp", (128, 512))
        ssp_b = P.buf("ssp")
        if HL:
            hal = ps("hal", (128, 512))
            hal_b = P.bufs(8, "hal")

        P.op("pool", lambda e: e.memset(ones[:, :], 1.0), writes=[ones_b])
        for k, (t, b, d) in gvec.items():
            P.op("sp", lambda e, t=t, d=d: e.dma_start(out=t[:, :], in_=d.ap()), writes=[b], dma=True)
        if ffn:
            P.op("sp", lambda e: e.dma_start(out=cw_s[:, :, :], in_=cw.ap()), writes=[cwb], dma=True)
            P.op("sp", lambda e: e.dma_start(out=cb_s[:, :], in_=cb.ap()), writes=[cwb], dma=True)

        cnt = {"w": 0, "acc": 0, "ev": 0, "sq": 0, "hal": 0, "uc": 0, "ca": 0, "hst": 0, "alt": 0}

        def load_w(dram_ap_2d, nk):
            i = cnt["w"] % NWB
            cnt["w"] += 1
            n = nk * 128
            bsz = _split_last(n)
            src = dram_ap_2d.rearrange("p (a b) -> p a b", b=bsz)
            dst = wt[i][:, 0:nk, :].rearrange("p k j -> p (k j)").rearrange("p (a b) -> p a b", b=bsz)
            P.op("pool", lambda e: e.dma_start(out=dst, in_=src), writes=[wt_b[i]], dma=True)
            return i

        def next_acc():
            i = cnt["acc"] % NPS
            cnt["acc"] += 1
            return i

        def compute_rstd():
            hs = None
            if HL:
                hs = cnt["hal"] % 8
                cnt["hal"] += 1
            for kc in range(KC):
                i = cnt["sq"] % 2
                cnt["sq"] += 1
                P.op("act", lambda e, kc=kc, i=i: e.activation(out=sq[i][:, :], in_=x1[:, kc, :], func=AF.Square),
                     reads=[x1_b[kc]], writes=[sq_b[i]])
                P.op("pe", lambda e, kc=kc, i=i: e.matmul(ssp[:, 0:TT], ones[:, :], sq[i][:, HL:TW],
                                                          start=(kc == 0), stop=(kc == KC - 1)),
                     reads=[ones_b, sq_b[i]], writes=[ssp_b])
                if HL:
                    P.op("pe", lambda e, kc=kc, i=i, hs=hs: e.matmul(hal[:, hs * HL:(hs + 1) * HL], ones[:, :], sq[i][:, 0:HL],
                                                                     start=(kc == 0), stop=(kc == KC - 1)),
                         reads=[ones_b, sq_b[i]], writes=[hal_b[hs]])
            P.op("act", lambda e: e.activation(out=rstd[:, HL:TW], in_=ssp[:, 0:TT], func=AF.Sqrt,
                                               bias=eps_t[:, 0:1], scale=1.0 / D),
                 reads=[ssp_b, eps_b], writes=[rstd_b])
            if HL:
                P.op("act", lambda e, hs=hs: e.activation(out=rstd[:, 0:HL], in_=hal[:, hs * HL:(hs + 1) * HL], func=AF.Sqrt,
                                                          bias=eps_t[:, 0:1], scale=1.0 / D),
                     reads=[hal_b[hs], eps_b], writes=[rstd_b])
            P.op("dve", lambda e: e.reciprocal(out=rstd[:, :], in_=rstd[:, :]), reads=[rstd_b], writes=[rstd_b])

        def rmsnorm_to_hn(gkey):
            gt_, gb_, _ = gvec[gkey]
            compute_rstd()
            for kc in range(KC):
                P.op("dve", lambda e, kc=kc: e.scalar_tensor_tensor(
                    out=hn[:, kc, :], in0=x1[:, kc, :], scalar=gt_[:, kc:kc + 1], in1=rstd[:, :],
                    op0=ALU.mult, op1=ALU.mult),
                    reads=[x1_b[kc], gb_, rstd_b], writes=[hn_b[kc]])

        eps_t = sb("eps_t", (128, 1), F32)
        eps_b = P.buf("eps")
        P.op("pool", lambda e: e.memset(eps_t[:, :], NORM_EPS), writes=[eps_b])

        def gemm(wslot, nk, rhs_fn, rhs_bufs, acc_i, halo_slot=None, halo_rhs_fn=None):
            for k in range(nk):
                P.op("pe", lambda e, k=k: e.matmul(acc[acc_i][:, :], wt[wslot][:, k, :], rhs_fn(k),
                                                   start=(k == 0), stop=(k == nk - 1)),
                     reads=[wt_b[wslot], rhs_bufs[k]], writes=[acc_b[acc_i]])
            if halo_slot is not None:
                for k in range(nk):
                    P.op("pe", lambda e, k=k: e.matmul(hal[:, halo_slot * HL:(halo_slot + 1) * HL],
                                                       wt[wslot][:, k, :], halo_rhs_fn(k),
                                                       start=(k == 0), stop=(k == nk - 1)),
                         reads=[wt_b[wslot], rhs_bufs[k]], writes=[hal_b[halo_slot]])

        for tt in range(NT):
            t0 = tt * TT
            for kc in range(KC):
                P.op("sp", lambda e, kc=kc, tt=tt: e.dma_start(out=x1[:, kc, :], in_=xT[tt, kc * 128:(kc + 1) * 128, :]),
                     writes=[x1_b[kc]], dma=True)
            if mix:
                for kc in range(KC):
                    i = cnt["hst"] % 2
                    cnt["hst"] += 1
                    P.op("sp", lambda e, kc=kc, tt=tt, i=i: e.dma_start(out=hst[i][:, :], in_=hT[tt, kc * 128:(kc + 1) * 128, :]),
                         writes=[hst_b[i]], dma=True)
                    P.op("act", lambda e, kc=kc, i=i: e.copy(out=hb[:, kc, :], in_=hst[i][:, :]),
                         reads=[hst_b[i]], writes=[hb_b[kc]])
                for dc in range(KC):
                    ws = load_w(w_mix[dc].rearrange("p k j -> p (k j)"), KC)
                    ai = next_acc()
                    hs = None
                    if HL:
                        hs = cnt["hal"] % 8
                        cnt["hal"] += 1
                    gemm(ws, KC, lambda k: hb[:, k, HL:TW], hb_b, ai, hs, lambda k: hb[:, k, 0:HL])
                    P.op("dve", lambda e, dc=dc, ai=ai: e.tensor_add(out=x1[:, dc, HL:TW], in0=x1[:, dc, HL:TW], in1=acc[ai][:, :]),
                         reads=[acc_b[ai], x1_b[dc]], writes=[x1_b[dc]])
                    if HL:
                        P.op("dve", lambda e, dc=dc, hs=hs: e.tensor_add(out=x1[:, dc, 0:HL], in0=x1[:, dc, 0:HL],
                                                                        in1=hal[:, hs * HL:(hs + 1) * HL]),
                             reads=[hal_b[hs], x1_b[dc]], writes=[x1_b[dc]])
            if ffn:
                rmsnorm_to_hn("ffn")
                for g in range(NG):
                    for fl in range(gsz[g]):
                        fc = gst[g] + fl
                        res = []
                        for half in range(2):
                            col = fc + half * FC
                            ws = load_w(w_up[col].rearrange("p k j -> p (k j)"), KC)
                            ai = next_acc()
                            hs = cnt["hal"] % 8
                            cnt["hal"] += 1
                            gemm(ws, KC, lambda k: hn[:, k, HL:TW], hn_b, ai, hs, lambda k: hn[:, k, 0:HL])
                            ui = cnt["uc"] % 4
                            cnt["uc"] += 1
                            ci = cnt["ca"] % 4
                            cnt["ca"] += 1
                            P.op("act", lambda e, ui=ui, ai=ai: e.copy(out=ucat[ui][:, HL:TW], in_=acc[ai][:, :]),
                                 reads=[acc_b[ai]], writes=[ucat_b[ui]])
                            P.op("act", lambda e, ui=ui, hs=hs: e.copy(out=ucat[ui][:, 0:HL], in_=hal[:, hs * HL:(hs + 1) * HL]),
                                 reads=[hal_b[hs]], writes=[ucat_b[ui]])
                            P.op("act", lambda e, ai=ai, ci=ci, col=col: e.activation(
                                out=cacc[ci][:, :], in_=acc[ai][:, :], func=AF.Identity,
                                scale=cw_s[:, 2, col:col + 1], bias=cb_s[:, col:col + 1]),
                                reads=[acc_b[ai], cwb], writes=[cacc_b[ci]])
                            P.op("dve", lambda e, ui=ui, ci=ci, col=col: e.scalar_tensor_tensor(
                                out=cacc[ci][:, :], in0=ucat[ui][:, 1:TW - 1], scalar=cw_s[:, 1, col:col + 1],
                                in1=cacc[ci][:, :], op0=ALU.mult, op1=ALU.add),
                                reads=[ucat_b[ui], cwb, cacc_b[ci]], writes=[cacc_b[ci]])
                            P.op("dve", lambda e, ui=ui, ci=ci, col=col: e.scalar_tensor_tensor(
                                out=cacc[ci][:, :], in0=ucat[ui][:, 0:TW - 2], scalar=cw_s[:, 0, col:col + 1],
                                in1=cacc[ci][:, :], op0=ALU.mult, op1=ALU.add),
                                reads=[ucat_b[ui], cwb, cacc_b[ci]], writes=[cacc_b[ci]])
                            res.append(ci)
                        cg, cu = res
                        P.op("act", lambda e, cg=cg: e.activation(out=cacc[cg][:, :], in_=cacc[cg][:, :], func=AF.Silu),
                             reads=[cacc_b[cg]], writes=[cacc_b[cg]])
                        P.op("pool", lambda e, cg=cg, cu=cu, fl=fl: e.tensor_tensor(
                            out=gt[:, fl, :], in0=cacc[cg][:, :], in1=cacc[cu][:, :], op=ALU.mult),
                            reads=[cacc_b[cg], cacc_b[cu]], writes=[gt_b[fl]])
                    for dc in range(KC):
                        ws = load_w(w_dn[dc, :, gst[g]:gst[g] + gsz[g], :].rearrange("p k j -> p (k j)"), gsz[g])
                        ai = next_acc()
                        gemm(ws, gsz[g], lambda k: gt[:, k, :], gt_b, ai)
                        P.op("dve", lambda e, dc=dc, ai=ai: e.tensor_add(out=x1[:, dc, HL:TW], in0=x1[:, dc, HL:TW], in1=acc[ai][:, :]),
                             reads=[acc_b[ai], x1_b[dc]], writes=[x1_b[dc]])
                if not final:
                    for kc in range(KC):
                        P.op("sp", lambda e, kc=kc, t0=t0: e.dma_start(out=xoT[kc * 128:(kc + 1) * 128, t0:t0 + TT], in_=x1[:, kc, HL:TW]),
                             reads=[x1_b[kc]], dma=True)
            if nxt:
                rmsnorm_to_hn("nxt")
                for oc in range(NOC):
                    ws = load_w(w_nxt[oc].rearrange("p k j -> p (k j)"), KC)
                    ai = next_acc()
                    gemm(ws, KC, lambda k: hn[:, k, HL:TW], hn_b, ai)
                    i = cnt["ev"] % NEV
                    cnt["ev"] += 1
                    eng = "act" if oc % 2 == 0 else "dve"
                    if eng == "act":
                        P.op("act", lambda e, i=i, ai=ai: e.copy(out=ev[i][:, :], in_=acc[ai][:, :]),
                             reads=[acc_b[ai]], writes=[ev_b[i]])
                    else:
                        P.op("dve", lambda e, i=i, ai=ai: e.tensor_copy(out=ev[i][:, :], in_=acc[ai][:, :]),
                             reads=[acc_b[ai]], writes=[ev_b[i]])
                    P.op("sp", lambda e, i=i, oc=oc, t0=t0: e.dma_start(out=projT[oc * 128:(oc + 1) * 128, t0:t0 + TT], in_=ev[i][:, :]),
                         reads=[ev_b[i]], dma=True)
            if final:
                gt_, gb_, _ = gvec["fin"]
                compute_rstd()
                for kc in range(KC):
                    i = cnt["ev"] % NEV
                    cnt["ev"] += 1
                    P.op("dve", lambda e, kc=kc, i=i: e.scalar_tensor_tensor(
                        out=ev[i][:, :], in0=x1[:, kc, HL:TW], scalar=gt_[:, kc:kc + 1], in1=rstd[:, HL:TW],
                        op0=ALU.mult, op1=ALU.mult),
                        reads=[x1_b[kc], gb_, rstd_b], writes=[ev_b[i]])
                    P.op("sp", lambda e, kc=kc, i=i, t0=t0: e.dma_start(out=xoT[kc * 128:(kc + 1) * 128, t0:t0 + TT], in_=ev[i][:, :]),
                         reads=[ev_b[i]], dma=True)
        P.finish("sp")
        P.emit()
        print("dense sbuf bytes remaining", nc.sbuf_bytes_remaining, "ops", {e: len(P.ops[e]) for e in ENGS}, flush=True)
    return nc


def build_mlstm(cfg):
    S, NP = cfg["S"], cfg["NP"]
    L, DK, DV, GC = 64, 256, 512, 4
    NCH = S // L
    NGR = NCH // GC
    GT = GC * L
    KSC = DK ** -0.5
    CAP = 15.0
    assert NCH <= 128
    nc = bass.Bass("TRN2", target_bir_lowering=False)

    def din(name, shape):
        return nc.dram_tensor(name, list(shape), F32, kind="ExternalInput")

    qT = din("qT", (NP, DK, S))
    kT = din("kT", (NP, DK, S))
    ktm = din("ktm", (NP, S, DK))
    vtm = din("vtm", (NP, S, DV))
    otm = din("otm", (NP, S, DV))
    gi = din("gi", (NP, S))
    gf = din("gf", (NP, S))
    gbias = din("gbias", (128, NP * 2))
    gain = din("gain", (64, NP * DV))
    negmask = din("negmask", (64, 64))
    ident = din("ident", (128, 128))
    hout = nc.dram_tensor("hout", [NP, S, DV], F32, kind="ExternalOutput")
    gs_u = nc.dram_tensor("gs_u", [NP, S], F32, kind="Internal")
    gs_a = nc.dram_tensor("gs_a", [NP, S], F32, kind="Internal")
    gs_ao = nc.dram_tensor("gs_ao", [NP, NCH], F32, kind="Internal")

    with ExitStack() as st:
        P = Prog(nc, st)

        def sb(name, shape, dt=F32):
            return st.enter_context(nc.sbuf_tensor(name, list(shape), dt))

        def ps(name, shape, dt=F32):
            return st.enter_context(nc.psum_tensor(name, list(shape), dt))

        idt = sb("idt", (128, 128)); idt_b = P.buf("idt")
        nm = sb("nm", (64, 64)); nm_b = P.buf("nm")
        gb = sb("gb", (128, NP * 2)); gb_b = P.buf("gb")
        gn = sb("gn", (64, NP * DV)); gn_b = P.buf("gn")
        one = sb("one", (128, 64)); one_b = P.buf("one")
        onec = sb("onec", (64, 1), BF16); onec_b = P.buf("onec")
        epsc = sb("epsc", (128, 1)); epsc_b = P.buf("epsc")
        P.op("sp", lambda e: e.dma_start(out=idt[:, :], in_=ident.ap()), writes=[idt_b], dma=True)
        P.op("sp", lambda e: e.dma_start(out=nm[:, :], in_=negmask.ap()), writes=[nm_b], dma=True)
        P.op("sp", lambda e: e.dma_start(out=gb[:, :], in_=gbias.ap()), writes=[gb_b], dma=True)
        P.op("sp", lambda e: e.dma_start(out=gn[:, :], in_=gain.ap()), writes=[gn_b], dma=True)
        P.op("pool", lambda e: e.memset(one[:, :], 1.0), writes=[one_b])
        P.op("pool", lambda e: e.memset(onec[:, :], 1.0), writes=[onec_b])
        P.op("pool", lambda e: e.memset(epsc[:, :], NORM_EPS), writes=[epsc_b])

        ps_s = [ps(f"ps_s{i}", (64, 64)) for i in range(2)]; ps_s_b = P.bufs(2, "pss")
        ps_num = [ps(f"ps_num{i}", (64, 512)) for i in range(2)]; ps_num_b = P.bufs(2, "psn")
        ps_c = [ps(f"ps_c{i}", (128, 512)) for i in range(2)]; ps_c_b = P.bufs(2, "psc")
        ps_sm = ps("ps_sm", (128, 512))
        ps_den_b = P.bufs(2, "psd")
        ps_n_b = P.bufs(2, "psnn")

        gi_t = sb("gi_t", (NCH, NP, L)); gf_t = sb("gf_t", (NCH, NP, L))
        G = P.buf("gates")
        for p in range(NP):
            P.op("sp", lambda e, p=p: e.dma_start(out=gi_t[:, p, :], in_=gi[p].rearrange("(c j) -> c j", j=L)), writes=[G], dma=True)
            P.op("sp", lambda e, p=p: e.dma_start(out=gf_t[:, p, :], in_=gf[p].rearrange("(c j) -> c j", j=L)), writes=[G], dma=True)
        ti = sb("ti", (NCH, NP, L)); tf = sb("tf", (NCH, NP, L))
        e1 = sb("e1", (NCH, NP, L)); lfn = sb("lfn", (NCH, NP, L))
        bb = sb("bb", (NCH, NP, L)); ww = sb("ww", (NCH, NP, L)); cm = sb("cm", (NCH, NP, L))
        uu = sb("uu", (NCH, NP, L)); aa = sb("aa", (NCH, NP, L)); ub = sb("ub", (NCH, NP, L))
        enm = sb("enm", (NCH, NP, L)); wi = sb("wi", (NCH, NP, L))
        bl = sb("bl", (NCH, NP)); md = sb("md", (NCH, NP)); mn = sb("mn", (NCH, NP)); mp = sb("mp", (NCH, NP))
        bmn = sb("bmn", (NCH, NP)); aot = sb("aot", (NCH, NP)); ao = sb("ao", (NCH, NP))
        blT = sb("blT", (NP, NCH)); mdT = sb("mdT", (NP, NCH)); mnT = sb("mnT", (NP, NCH)); mpT = sb("mpT", (NP, NCH))
        w_col = sb("w_col", (64, NP, NCH)); wik_col = sb("wik_col", (64, NP, NCH)); enm_col = sb("enm_col", (64, NP, NCH))

        GS = P.buf("gs_dram")

        def g_op(eng, fn, extra_r=()):
            P.op(eng, fn, reads=[G] + list(extra_r), writes=[G])

        for p in range(NP):
            g_op("dve", lambda e, p=p: e.tensor_scalar(out=ti[:, p, :], in0=gi_t[:, p, :], scalar1=gb[0:NCH, 2 * p:2 * p + 1],
                                                       scalar2=1.0 / CAP, op0=ALU.add, op1=ALU.mult), [gb_b])
            g_op("dve", lambda e, p=p: e.tensor_scalar(out=tf[:, p, :], in0=gf_t[:, p, :], scalar1=gb[0:NCH, 2 * p + 1:2 * p + 2],
                                                       scalar2=1.0 / CAP, op0=ALU.add, op1=ALU.mult), [gb_b])
        g_op("act", lambda e: e.activation(out=ti[:, :, :], in_=ti[:, :, :], func=AF.Tanh))
        g_op("act", lambda e: e.activation(out=tf[:, :, :], in_=tf[:, :, :], func=AF.Tanh))
        g_op("act", lambda e: e.activation(out=e1[:, :, :], in_=tf[:, :, :], func=AF.Exp, scale=-CAP))
        g_op("act", lambda e: e.activation(out=lfn[:, :, :], in_=e1[:, :, :], func=AF.Ln, bias=one[0:NCH, 0:1], scale=1.0), [one_b])
        for p in range(NP):
            g_op("dve", lambda e, p=p: e.tensor_tensor_scan(out=bb[:, p, :], data0=one[0:NCH, 0:L], data1=lfn[:, p, :], initial=0.0,
                                                            op0=ALU.mult, op1=ALU.subtract), [one_b])
            g_op("dve", lambda e, p=p: e.scalar_tensor_tensor(out=ww[:, p, :], in0=ti[:, p, :], scalar=CAP, in1=bb[:, p, :],
                                                              op0=ALU.mult, op1=ALU.subtract))
            g_op("dve", lambda e, p=p: e.tensor_tensor_scan(out=cm[:, p, :], data0=ww[:, p, :], data1=ww[:, p, :], initial=-1e30,
                                                            op0=ALU.max, op1=ALU.max))
            g_op("dve", lambda e, p=p: e.tensor_copy(out=bl[:, p:p + 1], in_=bb[:, p, L - 1:L]))
            g_op("dve", lambda e, p=p: e.tensor_tensor(out=md[:, p:p + 1], in0=bb[:, p, L - 1:L], in1=cm[:, p, L - 1:L], op=ALU.add))
        g_op("pe", lambda e: e.transpose(ps_num[0][0:NP, 0:NCH], bl[:, :], idt[0:NCH, 0:NCH]), [idt_b])
        g_op("dve", lambda e: e.tensor_copy(out=blT[:, :], in_=ps_num[0][0:NP, 0:NCH]))
        g_op("pe", lambda e: e.transpose(ps_num[0][0:NP, 0:NCH], md[:, :], idt[0:NCH, 0:NCH]), [idt_b])
        g_op("dve", lambda e: e.tensor_copy(out=mdT[:, :], in_=ps_num[0][0:NP, 0:NCH]))
        g_op("dve", lambda e: e.tensor_tensor_scan(out=mnT[:, :], data0=blT[:, :], data1=mdT[:, :], initial=0.0,
                                                   op0=ALU.add, op1=ALU.max))
        g_op("dve", lambda e: e.memset(mpT[:, :], 0.0))
        if NCH > 1:
            g_op("dve", lambda e: e.tensor_copy(out=mpT[:, 1:NCH], in_=mnT[:, 0:NCH - 1]))
        g_op("pe", lambda e: e.transpose(ps_c[0][0:NCH, 0:NP], mnT[:, :], idt[0:NP, 0:NP]), [idt_b])
        g_op("dve", lambda e: e.tensor_copy(out=mn[:, :], in_=ps_c[0][0:NCH, 0:NP]))
        g_op("pe", lambda e: e.transpose(ps_c[0][0:NCH, 0:NP], mpT[:, :], idt[0:NP, 0:NP]), [idt_b])
        g_op("dve", lambda e: e.tensor_copy(out=mp[:, :], in_=ps_c[0][0:NCH, 0:NP]))
        g_op("dve", lambda e: e.tensor_tensor(out=bmn[:, :], in0=bl[:, :], in1=mn[:, :], op=ALU.subtract))
        g_op("dve", lambda e: e.tensor_tensor(out=aot[:, :], in0=mp[:, :], in1=bmn[:, :], op=ALU.add))
        g_op("act", lambda e: e.activation(out=ao[:, :], in_=aot[:, :], func=AF.Exp))
        for p in range(NP):
            g_op("dve", lambda e, p=p: e.tensor_scalar(out=uu[:, p, :], in0=cm[:, p, :], scalar1=mp[:, p:p + 1], scalar2=-1.0,
                                                       op0=ALU.max, op1=ALU.mult))
            g_op("act", lambda e, p=p: e.activation(out=aa[:, p, :], in_=uu[:, p, :], func=AF.Exp, bias=mp[:, p:p + 1], scale=1.0))
            g_op("dve", lambda e, p=p: e.tensor_tensor(out=ub[:, p, :], in0=uu[:, p, :], in1=bb[:, p, :], op=ALU.subtract))
            g_op("act", lambda e, p=p: e.activation(out=enm[:, p, :], in_=ub[:, p, :], func=AF.Exp))
            g_op("act", lambda e, p=p: e.activation(out=wi[:, p, :], in_=ww[:, p, :], func=AF.Exp, bias=bmn[:, p:p + 1], scale=1.0))
            for src, dst, scl in ((ww, w_col, 1.0), (wi, wik_col, KSC), (enm, enm_col, 1.0)):
                g_op("pe", lambda e, p=p, src=src: e.transpose(ps_num[0][0:L, 0:NCH], src[:, p, :], idt[0:NCH, 0:NCH]), [idt_b])
                g_op("act", lambda e, p=p, dst=dst, scl=scl: e.mul(out=dst[:, p, :], in_=ps_num[0][0:L, 0:NCH], mul=scl))
            P.op("sp", lambda e, p=p: e.dma_start(out=gs_u.ap()[p].rearrange("(c j) -> c j", j=L), in_=uu[:, p, :]),
                 reads=[G], writes=[GS], dma=True, sem_buf=GS)
            P.op("sp", lambda e, p=p: e.dma_start(out=gs_a.ap()[p].rearrange("(c j) -> c j", j=L), in_=aa[:, p, :]),
                 reads=[G], writes=[GS], dma=True, sem_buf=GS)
            P.op("sp", lambda e, p=p: e.dma_start(out=gs_ao.ap()[p].rearrange("(c o) -> c o", o=1), in_=ao[:, p:p + 1]),
                 reads=[G], writes=[GS], dma=True, sem_buf=GS)

        ao_bc = sb("ao_bc", (128, NCH)); ao_bc_b = P.buf("aobc")
        Cst = sb("Cst", (128, 2, DV)); Cst_b = P.buf("Cst")
        nst = sb("nst", (128, 2)); nst_b = P.buf("nst")
        Cbf = [sb(f"Cbf{i}", (128, 2, DV), BF16) for i in range(2)]; Cbf_b = P.bufs(2, "Cbf")
        nbf = [sb(f"nbf{i}", (128, 2), BF16) for i in range(2)]; nbf_b = P.bufs(2, "nbf")
        qg = [sb(f"qg{i}", (128, 2, GT)) for i in range(2)]; qg_b = P.bufs(2, "qg")
        kg = [sb(f"kg{i}", (128, 2, GT)) for i in range(2)]; kg_b = P.bufs(2, "kg")
        Ag = [sb(f"Ag{i}", (128, GT)) for i in range(2)]; Ag_b = P.bufs(2, "Ag")
        ug = [sb(f"ug{i}", (64, GT)) for i in range(2)]; ug_b = P.bufs(2, "ug")
        ktg = [sb(f"ktg{i}", (64, GC, DK)) for i in range(2)]; ktg_b = P.bufs(2, "ktg")
        vg = [sb(f"vg{i}", (64, GC, DV)) for i in range(2)]; vg_b = P.bufs(2, "vg")
        og = [sb(f"og{i}", (64, GC, DV)) for i in range(2)]; og_b = P.bufs(2, "og")
        qb = [sb(f"qb{i}", (128, 2, GT), BF16) for i in range(2)]; qb_b = P.bufs(2, "qb")
        kb = [sb(f"kb{i}", (128, 2, GT), BF16) for i in range(2)]; kb_b = P.bufs(2, "kb")
        qtb = [sb(f"qtb{i}", (128, 2, GT), BF16) for i in range(2)]; qtb_b = P.bufs(2, "qtb")
        vb = [sb(f"vb{i}", (64, GC, DV), BF16) for i in range(2)]; vb_b = P.bufs(2, "vb")
        kwb = [sb(f"kwb{i}", (64, GC, DK), BF16) for i in range(2)]; kwb_b = P.bufs(2, "kwb")
        sg = [sb(f"sg{i}", (64, GC, DV)) for i in range(2)]; sg_b = P.bufs(2, "sg")
        hog = [sb(f"hog{i}", (64, GC, DV)) for i in range(2)]; hog_b = P.bufs(2, "hog")
        X = [sb(f"X{i}", (64, 64)) for i in range(2)]; X_b = P.bufs(2, "X")
        E = [sb(f"E{i}", (64, 64)) for i in range(2)]; E_b = P.bufs(2, "E")
        sDT = [sb(f"sDT{i}", (64, 64), BF16) for i in range(2)]; sDT_b = P.bufs(2, "sDT")
        dn = [sb(f"dn{i}", (64, 1)) for i in range(2)]; dn_b = P.bufs(2, "dn")
        hraw = [sb(f"hraw{i}", (64, DV)) for i in range(2)]; hraw_b = P.bufs(2, "hraw")
        junk = [sb(f"junk{i}", (64, DV)) for i in range(2)]; junk_b = P.bufs(2, "junk")
        ss = [sb(f"ss{i}", (64, 1)) for i in range(2)]; ss_b = P.bufs(2, "ss")
        hn_ = [sb(f"hn_{i}", (64, DV)) for i in range(2)]; hn_b = P.bufs(2, "hn")

        def bc_ap(t, off, nparts, n):
            return bass.AP(t, off, [[0, nparts], [1, n]])

        def load_group(p, g):
            gg = p * NGR + g
            i = gg % 2
            t0 = g * GT
            P.op("sp", lambda e: e.dma_start(out=qg[i][:, :, :], in_=qT[p, :, t0:t0 + GT].rearrange("(k d) t -> d k t", d=128)),
                 writes=[qg_b[i]], dma=True)
            P.op("sp", lambda e: e.dma_start(out=kg[i][:, :, :], in_=kT[p, :, t0:t0 + GT].rearrange("(k d) t -> d k t", d=128)),
                 writes=[kg_b[i]], dma=True)
            P.op("sp", lambda e: e.dma_start(out=Ag[i][:, :], in_=bc_ap(gs_a, p * S + t0, 128, GT)), reads=[GS], writes=[Ag_b[i]], dma=True)
            P.op("sp", lambda e: e.dma_start(out=ug[i][:, :], in_=bc_ap(gs_u, p * S + t0, 64, GT)), reads=[GS], writes=[ug_b[i]], dma=True)
            P.op("sp", lambda e: e.dma_start(out=ktg[i][:, :, :], in_=ktm[p, t0:t0 + GT, :].rearrange("(n i) d -> i n d", i=L)),
                 writes=[ktg_b[i]], dma=True)
            P.op("sp", lambda e: e.dma_start(out=vg[i][:, :, :], in_=vtm[p, t0:t0 + GT, :].rearrange("(n i) d -> i n d", i=L)),
                 writes=[vg_b[i]], dma=True)
            P.op("sp", lambda e: e.dma_start(out=og[i][:, :, :], in_=otm[p, t0:t0 + GT, :].rearrange("(n i) d -> i n d", i=L)),
                 writes=[og_b[i]], dma=True)

        def prep_group(p, g):
            gg = p * NGR + g
            i = gg % 2
            P.op("act", lambda e: e.copy(out=qb[i][:, :, :], in_=qg[i][:, :, :]), reads=[qg_b[i]], writes=[qb_b[i]])
            P.op("pool", lambda e: e.tensor_copy(out=kb[i][:, :, :], in_=kg[i][:, :, :]), reads=[kg_b[i]], writes=[kb_b[i]])
            for k in range(2):
                P.op("dve", lambda e, k=k: e.tensor_tensor(out=qtb[i][:, k, :], in0=qg[i][:, k, :], in1=Ag[i][:, :], op=ALU.mult),
                     reads=[qg_b[i], Ag_b[i]], writes=[qtb_b[i]])
            P.op("pool", lambda e: e.tensor_copy(out=vb[i][:, :, :], in_=vg[i][:, :, :]), reads=[vg_b[i]], writes=[vb_b[i]])
            for n in range(GC):
                c = g * GC + n
                P.op("act", lambda e, n=n, c=c: e.activation(out=kwb[i][:, n, :], in_=ktg[i][:, n, :], func=AF.Identity,
                                                             scale=wik_col[:, p, c:c + 1]),
                     reads=[ktg_b[i], G], writes=[kwb_b[i]])
            P.op("act", lambda e: e.activation(out=sg[i][:, :, :], in_=og[i][:, :, :], func=AF.Sigmoid), reads=[og_b[i]], writes=[sg_b[i]])

        def stage_S(p, c, n_glob):
            g, n = divmod(c, GC)
            i = (p * NGR + g) % 2
            j = n_glob % 2
            j0 = n * L
            for k in range(2):
                P.op("pe", lambda e, k=k: e.matmul(ps_s[j][:, :], kb[i][:, k, j0:j0 + L], qb[i][:, k, j0:j0 + L], start=(k == 0), stop=(k == 1)),
                     reads=[kb_b[i], qb_b[i]], writes=[ps_s_b[j]])
            P.op("dve", lambda e: e.scalar_tensor_tensor(out=X[j][:, :], in0=ug[i][:, j0:j0 + L], scalar=w_col[:, p, c:c + 1], in1=nm[:, :],
                                                         op0=ALU.add, op1=ALU.add),
                 reads=[ug_b[i], G, nm_b], writes=[X_b[j]])
            P.op("act", lambda e: e.activation(out=E[j][:, :], in_=X[j][:, :], func=AF.Exp), reads=[X_b[j]], writes=[E_b[j]])
            P.op("dve", lambda e: e.scalar_tensor_tensor(out=sDT[j][:, :], in0=ps_s[j][:, :], scalar=KSC, in1=E[j][:, :],
                                                         op0=ALU.mult, op1=ALU.mult),
                 reads=[ps_s_b[j], E_b[j]], writes=[sDT_b[j]])

        def stage_H(p, c, n_glob):
            g, n = divmod(c, GC)
            i = (p * NGR + g) % 2
            j = n_glob % 2
            cb = c % 2
            j0 = n * L
            for k in range(2):
                P.op("pe", lambda e, k=k: e.matmul(ps_num[j][:, :], qtb[i][:, k, j0:j0 + L], Cbf[cb][:, k, :], start=(k == 0), stop=False),
                     reads=[qtb_b[i], Cbf_b[cb]], writes=[ps_num_b[j]])
            P.op("pe", lambda e: e.matmul(ps_num[j][:, :], sDT[j][:, :], vb[i][:, n, :], start=False, stop=True),
                 reads=[sDT_b[j], vb_b[i]], writes=[ps_num_b[j]])
            for k in range(2):
                P.op("pe", lambda e, k=k: e.matmul(ps_sm[0:64, j:j + 1], qtb[i][:, k, j0:j0 + L], nbf[cb][:, k:k + 1], start=(k == 0), stop=False),
                     reads=[qtb_b[i], nbf_b[cb]], writes=[ps_den_b[j]])
            P.op("pe", lambda e: e.matmul(ps_sm[0:64, j:j + 1], sDT[j][:, :], onec[:, :], start=False, stop=True),
                 reads=[sDT_b[j], onec_b], writes=[ps_den_b[j]])
            for k in range(2):
                P.op("pe", lambda e, k=k: e.matmul(ps_c[k][:, :], kwb[i][:, n, k * 128:(k + 1) * 128], vb[i][:, n, :], start=True, stop=True),
                     reads=[kwb_b[i], vb_b[i]], writes=[ps_c_b[k]])
                P.op("pe", lambda e, k=k: e.matmul(ps_sm[:, 2 + k:3 + k], kwb[i][:, n, k * 128:(k + 1) * 128], onec[:, :], start=True, stop=True),
                     reads=[kwb_b[i], onec_b], writes=[ps_n_b[k]])
            nb = 1 - cb
            for k in range(2):
                P.op("dve", lambda e, k=k: e.scalar_tensor_tensor(out=Cst[:, k, :], in0=Cst[:, k, :], scalar=ao_bc[:, c:c + 1], in1=ps_c[k][:, :],
                                                                  op0=ALU.mult, op1=ALU.add),
                     reads=[Cst_b, ao_bc_b, ps_c_b[k]], writes=[Cst_b])
                P.op("dve", lambda e, k=k: e.scalar_tensor_tensor(out=nst[:, k:k + 1], in0=nst[:, k:k + 1], scalar=ao_bc[:, c:c + 1],
                                                                  in1=ps_sm[:, 2 + k:3 + k], op0=ALU.mult, op1=ALU.add),
                     reads=[nst_b, ao_bc_b, ps_n_b[k]], writes=[nst_b])
            P.op("act", lambda e: e.copy(out=Cbf[nb][:, 0, :], in_=Cst[:, 0, :]), reads=[Cst_b], writes=[Cbf_b[nb]])
            P.op("pool", lambda e: e.tensor_copy(out=Cbf[nb][:, 1, :], in_=Cst[:, 1, :]), reads=[Cst_b], writes=[Cbf_b[nb]])
            P.op("pool", lambda e: e.tensor_copy(out=nbf[nb][:, :], in_=nst[:, :]), reads=[nst_b], writes=[nbf_b[nb]])
            P.op("dve", lambda e: e.tensor_copy(out=dn[j][:, :], in_=ps_sm[0:64, j:j + 1]),
                 reads=[ps_den_b[j]], writes=[dn_b[j]])
            P.op("dve", lambda e: e.scalar_tensor_tensor(out=dn[j][:, :], in0=dn[j][:, :], scalar=-1.0, in1=dn[j][:, :],
                                                         op0=ALU.mult, op1=ALU.max),
                 reads=[dn_b[j]], writes=[dn_b[j]])
            P.op("dve", lambda e: e.tensor_tensor(out=dn[j][:, :], in0=dn[j][:, :], in1=enm_col[:, p, c:c + 1], op=ALU.max),
                 reads=[dn_b[j], G], writes=[dn_b[j]])
            P.op("dve", lambda e: e.reciprocal(out=dn[j][:, :], in_=dn[j][:, :]), reads=[dn_b[j]], writes=[dn_b[j]])
            P.op("dve", lambda e: e.tensor_scalar(out=hraw[j][:, :], in0=ps_num[j][:, :], scalar1=dn[j][:, 0:1], scalar2=None, op0=ALU.mult),
                 reads=[ps_num_b[j], dn_b[j]], writes=[hraw_b[j]])
            P.op("act", lambda e: e.activation(out=junk[j][:, :], in_=hraw[j][:, :], func=AF.Square, accum_out=ss[j][:, :]),
                 reads=[hraw_b[j]], writes=[junk_b[j], ss_b[j]])
            P.op("act", lambda e: e.activation(out=ss[j][:, :], in_=ss[j][:, :], func=AF.Sqrt, bias=epsc[0:64, 0:1], scale=1.0 / DV),
                 reads=[ss_b[j], epsc_b], writes=[ss_b[j]])
            P.op("dve", lambda e: e.reciprocal(out=ss[j][:, :], in_=ss[j][:, :]), reads=[ss_b[j]], writes=[ss_b[j]])
            P.op("dve", lambda e: e.scalar_tensor_tensor(out=hn_[j][:, :], in0=hraw[j][:, :], scalar=ss[j][:, 0:1], in1=gn[:, p * DV:(p + 1) * DV],
                                                         op0=ALU.mult, op1=ALU.mult),
                 reads=[hraw_b[j], ss_b[j], gn_b], writes=[hn_b[j]])
            P.op("pool", lambda e: e.tensor_tensor(out=hog[i][:, n, :], in0=hn_[j][:, :], in1=sg[i][:, n, :], op=ALU.mult),
                 reads=[hn_b[j], sg_b[i]], writes=[hog_b[i]])

        def store_group(p, g):
            i = (p * NGR + g) % 2
            t0 = g * GT
            P.op("pool", lambda e: e.dma_start(out=hout.ap()[p, t0:t0 + GT, :].rearrange("(n i) d -> i n d", i=L), in_=hog[i][:, :, :]),
                 reads=[hog_b[i]], dma=True)

        n_glob = 0
        seq = [(p, c) for p in range(NP) for c in range(NCH)]
        load_group(0, 0)
        for p in range(NP):
            P.op("sp", lambda e, p=p: e.dma_start(out=ao_bc[:, :], in_=bc_ap(gs_ao, p * NCH, 128, NCH)), reads=[GS], writes=[ao_bc_b], dma=True)
            P.op("dve", lambda e: e.memset(Cst[:, :, :], 0.0), writes=[Cst_b])
            P.op("dve", lambda e: e.memset(nst[:, :], 0.0), writes=[nst_b])
            P.op("pool", lambda e: e.memset(Cbf[0][:, :, :], 0.0), writes=[Cbf_b[0]])
            P.op("pool", lambda e: e.memset(nbf[0][:, :], 0.0), writes=[nbf_b[0]])
            for g in range(NGR):
                if g + 1 < NGR:
                    load_group(p, g + 1)
                elif p + 1 < NP:
                    load_group(p + 1, 0)
                prep_group(p, g)
                for n in range(GC):
                    c = g * GC + n
                    stage_S(p, c, n_glob)
                    stage_H(p, c, n_glob)
                    n_glob += 1
                store_group(p, g)
        P.finish("sp")
        P.emit()
    return nc


def build_moba(cfg):
    S, NP = cfg["S"], cfg["NP"]
    DH, BS = 128, 256
    NB = S // BS
    NQT = S // 128
    GW = max(NB, 8)
    SC = DH ** -0.5
    BIG = 30000.0
    PIECE = min(2048, S)
    NPC = S // PIECE
    OG = 4
    nc = bass.Bass("TRN2", target_bir_lowering=False)

    def din(name, shape):
        return nc.dram_tensor(name, list(shape), F32, kind="ExternalInput")

    qT = din("qT", (NP, DH, S))
    kT = din("kT", (NP, DH, S))
    vtm = din("vtm", (NP, S, DH))
    cmask = din("cmask", (128, 2 * BS))
    ident = din("ident", (128, 128))
    att = nc.dram_tensor("att", [NP, S, DH], F32, kind="ExternalOutput")

    with ExitStack() as st:
        P = Prog(nc, st)

        def sb(name, shape, dt=F32):
            return st.enter_context(nc.sbuf_tensor(name, list(shape), dt))

        def ps(name, shape, dt=F32):
            return st.enter_context(nc.psum_tensor(name, list(shape), dt))

        idf = sb("idf", (128, 128)); idf_b = P.buf("idf")
        idb = sb("idb", (128, 128), BF16); idb_b = P.buf("idb")
        cm = sb("cm", (128, 2 * BS)); cm_b = P.buf("cm")
        P.op("sp", lambda e: e.dma_start(out=idf[:, :], in_=ident.ap()), writes=[idf_b], dma=True)
        P.op("sp", lambda e: e.dma_start(out=cm[:, :], in_=cmask.ap()), writes=[cm_b], dma=True)
        P.op("act", lambda e: e.copy(out=idb[:, :], in_=idf[:, :]), reads=[idf_b], writes=[idb_b])

        stage = [sb(f"stage{i}", (128, PIECE)) for i in range(2)]; stage_b = P.bufs(2, "stage")
        qb = sb("qb", (128, S), BF16); qb_b = P.buf("qb")
        kb = sb("kb", (128, S), BF16); kb_b = P.buf("kb")
        vb = sb("vb", (128, NQT, DH), BF16); vb_b = P.buf("vb")
        ksum = sb("ksum", (128, NB)); km = sb("km", (128, NB)); km_b = P.buf("km")
        gate_all = sb("gate_all", (128, NQT, NB)); gate_b = P.buf("gate")
        gsel = [sb(f"gsel{i}", (128, GW)) for i in range(2)]; gsel_b = P.bufs(2, "gsel")
        mx8 = [sb(f"mx8{i}", (128, 8)) for i in range(2)]; mx8_b = P.bufs(2, "mx8")
        mb = [sb(f"mb{i}", (128, GW)) for i in range(2)]; mb_b = P.bufs(2, "mb")
        bm = [sb(f"bm{i}", (128, NB + 1)) for i in range(2)]; bm_b = P.bufs(2, "bm")
        mrow = [sb(f"mrow{i}", (128, 1)) for i in range(2)]; mrow_b = P.bufs(2, "mrow")
        rsum = [sb(f"rsum{i}", (128, 1)) for i in range(2)]; rsum_b = P.bufs(2, "rsum")
        Sb = [sb(f"Sb{i}", (128, S)) for i in range(2)]; Sb_b = P.bufs(2, "Sb")
        Pb = [sb(f"Pb{i}", (128, S), BF16) for i in range(2)]; Pb_b = P.bufs(2, "Pb")
        PT = [sb(f"PT{i}", (128, 8 * 128), BF16) for i in range(2)]; PT_b = P.bufs(2, "PT")
        ost = [sb(f"ost{i}", (128, OG, DH)) for i in range(2)]; ost_b = P.bufs(2, "ost")

        ps_S = [ps(f"ps_S{i}", (128, 512)) for i in range(2)]; ps_S_b = P.bufs(2, "psS")
        ps_T = [ps(f"ps_T{i}", (128, 1024), BF16) for i in range(2)]; ps_T_b = P.bufs(2, "psT")
        ps_o = [ps(f"ps_o{i}", (128, DH)) for i in range(2)]; ps_o_b = P.bufs(2, "pso")
        ps_g = ps("ps_g", (128, GW)); ps_g_b = P.buf("psg")

        cnt = {"st": 0, "S": 0, "T": 0}

        def next_stage():
            i = cnt["st"] % 2
            cnt["st"] += 1
            return i

        def prologue(p):
            for pc in range(NPC):
                i = next_stage()
                t0 = pc * PIECE
                P.op("sp", lambda e, i=i, t0=t0: e.dma_start(out=stage[i][:, :], in_=kT[p, :, t0:t0 + PIECE]), writes=[stage_b[i]], dma=True)
                P.op("act", lambda e, i=i, t0=t0: e.copy(out=kb[:, t0:t0 + PIECE], in_=stage[i][:, :]), reads=[stage_b[i]], writes=[kb_b])
                nb0 = t0 // BS
                nbp = PIECE // BS
                P.op("dve", lambda e, i=i, nb0=nb0, nbp=nbp: e.tensor_reduce(
                    out=ksum[:, nb0:nb0 + nbp], in_=stage[i][:, :].rearrange("d (n t) -> d n t", t=BS), axis=AX.X, op=ALU.add),
                    reads=[stage_b[i]], writes=[km_b])
            P.op("dve", lambda e: e.tensor_scalar(out=km[:, :], in0=ksum[:, :], scalar1=1.0 / BS, scalar2=None, op0=ALU.mult),
                 reads=[km_b], writes=[km_b])
            for pc in range(NPC):
                i = next_stage()
                t0 = pc * PIECE
                P.op("sp", lambda e, i=i, t0=t0: e.dma_start(out=stage[i][:, :].rearrange("i (n e) -> i n e", e=DH),
                                                             in_=vtm[p, t0:t0 + PIECE, :].rearrange("(n i) e -> i n e", i=128)),
                     writes=[stage_b[i]], dma=True)
                c0 = t0 // 128
                P.op("pool", lambda e, i=i, c0=c0: e.tensor_copy(out=vb[:, c0:c0 + PIECE // 128, :],
                                                                 in_=stage[i][:, :].rearrange("i (n e) -> i n e", e=DH)),
                     reads=[stage_b[i]], writes=[vb_b])
            for pc in range(NPC):
                i = next_stage()
                t0 = pc * PIECE
                P.op("sp", lambda e, i=i, t0=t0: e.dma_start(out=stage[i][:, :], in_=qT[p, :, t0:t0 + PIECE]), writes=[stage_b[i]], dma=True)
                P.op("act", lambda e, i=i, t0=t0: e.copy(out=qb[:, t0:t0 + PIECE], in_=stage[i][:, :]), reads=[stage_b[i]], writes=[qb_b])
                for tl in range(PIECE // 128):
                    qt = t0 // 128 + tl
                    P.op("pe", lambda e, i=i, tl=tl: e.matmul(ps_g[:, 0:NB], stage[i][:, tl * 128:(tl + 1) * 128], km[:, :], start=True, stop=True),
                         reads=[stage_b[i], km_b], writes=[ps_g_b])
                    P.op("act", lambda e, qt=qt: e.copy(out=gate_all[:, qt, :], in_=ps_g[:, 0:NB]), reads=[ps_g_b], writes=[gate_b])

        def front(p, qt):
            bi, o = divmod(qt, 2)
            s = qt % 2
            nblk = bi + 1
            P.op("pool", lambda e: e.memset(gsel[s][:, :], -1e30), writes=[gsel_b[s]])
            if bi > 0:
                P.op("pool", lambda e: e.tensor_copy(out=gsel[s][:, 0:bi], in_=gate_all[:, qt, 0:bi]), reads=[gate_b], writes=[gsel_b[s]])
            P.op("dve", lambda e: e.max(out=mx8[s][:, :], in_=gsel[s][:, :]), reads=[gsel_b[s]], writes=[mx8_b[s]])
            P.op("dve", lambda e: e.tensor_scalar(out=mb[s][:, :], in0=gsel[s][:, :], scalar1=mx8[s][:, 2:3], scalar2=-BIG,
                                                  op0=ALU.is_lt, op1=ALU.mult),
                 reads=[gsel_b[s], mx8_b[s]], writes=[mb_b[s]])
            for n0 in range(0, nblk, 2):
                k = cnt["S"] % 2
                cnt["S"] += 1
                nbl = min(2, nblk - n0)
                ncols = nbl * BS
                P.op("pe", lambda e, k=k, n0=n0, ncols=ncols: e.matmul(ps_S[k][:, 0:ncols], qb[:, qt * 128:(qt + 1) * 128],
                                                                       kb[:, n0 * BS:n0 * BS + ncols], start=True, stop=True),
                     reads=[qb_b, kb_b], writes=[ps_S_b[k]])
                for h in range(nbl):
                    n = n0 + h
                    if n < bi:
                        P.op("dve", lambda e, k=k, n=n, h=h: e.tensor_scalar(
                            out=Sb[s][:, n * BS:(n + 1) * BS], in0=ps_S[k][:, h * BS:(h + 1) * BS], scalar1=mb[s][:, n:n + 1], scalar2=None,
                            op0=ALU.add, op1=ALU.max, accum_out=bm[s][:, n:n + 1]),
                            reads=[ps_S_b[k], mb_b[s]], writes=[Sb_b[s], bm_b[s]])
                    else:
                        P.op("dve", lambda e, k=k, n=n, h=h: e.tensor_tensor(
                            out=Sb[s][:, n * BS:(n + 1) * BS], in0=ps_S[k][:, h * BS:(h + 1) * BS], in1=cm[:, o * BS:(o + 1) * BS], op=ALU.add),
                            reads=[ps_S_b[k], cm_b], writes=[Sb_b[s]])
                        P.op("dve", lambda e, n=n: e.reduce_max(out=bm[s][:, n:n + 1], in_=Sb[s][:, n * BS:(n + 1) * BS], axis=AX.X),
                             reads=[Sb_b[s]], writes=[bm_b[s]])
            P.op("dve", lambda e: e.reduce_max(out=mrow[s][:, :], in_=bm[s][:, 0:nblk], axis=AX.X), reads=[bm_b[s]], writes=[mrow_b[s]])
            P.op("dve", lambda e: e.tensor_scalar(out=mrow[s][:, :], in0=mrow[s][:, :], scalar1=-SC, scalar2=None, op0=ALU.mult),
                 reads=[mrow_b[s]], writes=[mrow_b[s]])
            nk = nblk * BS
            P.op("act", lambda e: e.activation(out=Pb[s][:, 0:nk], in_=Sb[s][:, 0:nk], func=AF.Exp, bias=mrow[s][:, 0:1], scale=SC,
                                               accum_out=rsum[s][:, :]),
                 reads=[Sb_b[s], mrow_b[s]], writes=[Pb_b[s], rsum_b[s]])

        def back(p, qt):
            bi, o = divmod(qt, 2)
            s = qt % 2
            nch = (bi + 1) * 2
            groups = [(g0, min(8, nch - g0)) for g0 in range(0, nch, 8)]
            oi = qt % 2
            slot = qt % OG
            osl = (qt // OG) % 2

            def emit_T(g0, ng):
                k = cnt["T"] % 2
                cnt["T"] += 1
                for j in range(ng):
                    kc = g0 + j
                    P.op("pe", lambda e, j=j, kc=kc: e.transpose(ps_T[k][:, j * 128:(j + 1) * 128], Pb[s][:, kc * 128:(kc + 1) * 128], idb[:, :]),
                         reads=[Pb_b[s], idb_b], writes=[ps_T_b[k]])
                P.op("act", lambda e: e.copy(out=PT[k][:, 0:ng * 128], in_=ps_T[k][:, 0:ng * 128]), reads=[ps_T_b[k]], writes=[PT_b[k]])
                return k

            def emit_PV(g0, ng, k):
                for j in range(ng):
                    kc = g0 + j
                    P.op("pe", lambda e, j=j, kc=kc: e.matmul(ps_o[oi][:, :], PT[k][:, j * 128:(j + 1) * 128], vb[:, kc, :],
                                                              start=(kc == 0), stop=(kc == nch - 1)),
                         reads=[PT_b[k], vb_b], writes=[ps_o_b[oi]])

            ks = [None] * len(groups)
            ks[0] = emit_T(*groups[0])
            for gi_, (g0, ng) in enumerate(groups):
                if gi_ + 1 < len(groups):
                    ks[gi_ + 1] = emit_T(*groups[gi_ + 1])
                emit_PV(g0, ng, ks[gi_])
            P.op("dve", lambda e: e.reciprocal(out=rsum[s][:, :], in_=rsum[s][:, :]), reads=[rsum_b[s]], writes=[rsum_b[s]])
            P.op("dve", lambda e: e.tensor_scalar(out=ost[osl][:, slot, :], in0=ps_o[oi][:, :], scalar1=rsum[s][:, 0:1], scalar2=None, op0=ALU.mult),
                 reads=[ps_o_b[oi], rsum_b[s]], writes=[ost_b[osl]])
            if slot == OG - 1:
                q0 = (qt - OG + 1) * 128
                P.op("pool", lambda e: e.dma_start(out=att.ap()[p, q0:q0 + OG * 128, :].rearrange("(n i) e -> i n e", i=128), in_=ost[osl][:, :, :]),
                     reads=[ost_b[osl]], dma=True)

        for p in range(NP):
            prologue(p)
            front(p, 0)
            for qt in range(NQT):
                if qt + 1 < NQT:
                    front(p, qt + 1)
                back(p, qt)
        P.finish("sp")
        P.emit()
    return nc


B_, S_, D_ = 2, 8192, 4096
T_CORE = B_ * S_ // N_CORES
TT_ = 512
HL_ = 2
F_ = 11008
ML_H, ML_DK, ML_DV = 8, 256, 512
MB_H, MB_DH, MB_BS = 32, 128, 256
ML_COLS = ML_H * (2 * ML_DK + ML_DV) + D_ + 2 * ML_H

_progs = {}


def _prog(key, builder, cfg):
    if key not in _progs:
        _progs[key] = builder(cfg)
    return _progs[key]


def _tiles(XT, c, hl):
    b, q = divmod(c, N_CORES // B_)
    NT = T_CORE // TT_
    out = np.zeros((NT, XT.shape[1], TT_ + hl), np.float32)
    for tt in range(NT):
        s0 = q * T_CORE + tt * TT_
        lo = s0 - hl
        if lo < 0:
            out[tt, :, hl - s0:] = XT[b, :, 0:s0 + TT_]
        else:
            out[tt] = XT[b, :, lo:s0 + TT_]
    return out


def _gather_T(res, key, rows):
    per_b = N_CORES // B_
    out = np.empty((B_, rows, S_), np.float32)
    for c in range(N_CORES):
        b, q = divmod(c, per_b)
        out[b, :, q * T_CORE:(q + 1) * T_CORE] = res[c][key][:rows]
    return out


def _run(nc, maps):
    r = run_bass_kernel_spmd(nc, maps, core_ids=list(range(N_CORES)))
    return r.results


def _ffn_maps(ffn_w_up, ffn_conv_w, ffn_conv_b, ffn_w_down, norm_ffn, i):
    return {
        "g_ffn": lay_vec(norm_ffn[i]),
        "w_up": lay_w(ffn_w_up[i]),
        "cw": np.ascontiguousarray(np.stack([lay_vec(ffn_conv_w[i, j]) for j in range(3)], axis=1)),
        "cb": lay_vec(ffn_conv_b[i]),
        "w_dn": lay_w(ffn_w_down[i]),
    }


def kernel(x, norm_mix, norm_ffn, a_w_in, a_gate_bias, a_head_norm, a_w_out,
           b_w_qkv, b_w_out, ffn_w_up, ffn_conv_w, ffn_conv_b, ffn_w_down, final_norm):
    f32 = np.float32
    x = np.asarray(x, f32)
    args = [norm_mix, norm_ffn, a_w_in, a_gate_bias, a_head_norm, a_w_out, b_w_qkv, b_w_out,
            ffn_w_up, ffn_conv_w, ffn_conv_b, ffn_w_down, final_norm]
    (norm_mix, norm_ffn, a_w_in, a_gate_bias, a_head_norm, a_w_out, b_w_qkv, b_w_out,
     ffn_w_up, ffn_conv_w, ffn_conv_b, ffn_w_down, final_norm) = [np.asarray(a, f32) for a in args]
    per_b = N_CORES // B_
    XT = np.ascontiguousarray(x.transpose(0, 2, 1))

    nc1 = _prog("L1", build_dense, dict(T=T_CORE, TT=TT_, HL=0, D=D_, mix=False, ffn=False, nxt=ML_COLS, final=False))
    w1 = lay_w(a_w_in[0])
    g1 = lay_vec(norm_mix[0])
    res = _run(nc1, [{"xT": _tiles(XT, c, 0), "g_nxt": g1, "w_nxt": w1} for c in range(N_CORES)])
    P1 = _gather_T(res, "projT", ML_COLS)
    del res, w1

    NP2 = B_ * ML_H // N_CORES
    nc2 = _prog("L2", build_mlstm, dict(S=S_, NP=NP2))
    s1, s2, s3, s4 = ML_H * ML_DK, 2 * ML_H * ML_DK, 2 * ML_H * ML_DK + ML_H * ML_DV, 2 * ML_H * ML_DK + ML_H * ML_DV + D_
    negmask = np.where(np.arange(64)[None, :] >= np.arange(64)[:, None], 0.0, -30000.0).astype(f32)
    ident = np.eye(128, dtype=f32)
    maps = []
    for c in range(N_CORES):
        pairs = [divmod(c * NP2 + i, ML_H) for i in range(NP2)]
        qT = np.stack([P1[b, h * ML_DK:(h + 1) * ML_DK] for b, h in pairs])
        kT = np.stack([P1[b, s1 + h * ML_DK:s1 + (h + 1) * ML_DK] for b, h in pairs])
        vT = np.stack([P1[b, s2 + h * ML_DV:s2 + (h + 1) * ML_DV] for b, h in pairs])
        oT = np.stack([P1[b, s3 + h * ML_DV:s3 + (h + 1) * ML_DV] for b, h in pairs])
        gi = np.stack([P1[b, s4 + h] for b, h in pairs])
        gf = np.stack([P1[b, s4 + ML_H + h] for b, h in pairs])
        gb = np.array([[a_gate_bias[0, h], a_gate_bias[0, ML_H + h]] for b, h in pairs], f32).reshape(1, -1)
        gn = np.concatenate([a_head_norm[0, h * ML_DV:(h + 1) * ML_DV] for b, h in pairs]).reshape(1, -1)
        maps.append(dict(qT=np.ascontiguousarray(qT), kT=np.ascontiguousarray(kT),
                         ktm=np.ascontiguousarray(kT.transpose(0, 2, 1)),
                         vtm=np.ascontiguousarray(vT.transpose(0, 2, 1)),
                         otm=np.ascontiguousarray(oT.transpose(0, 2, 1)),
                         gi=np.ascontiguousarray(gi), gf=np.ascontiguousarray(gf),
                         gbias=np.ascontiguousarray(np.broadcast_to(gb, (128, gb.shape[1]))),
                         gain=np.ascontiguousarray(np.broadcast_to(gn, (64, gn.shape[1]))),
                         negmask=negmask, ident=ident))
    del P1
    res = _run(nc2, maps)
    del maps
    HT = np.empty((B_, D_, S_), f32)
    for c in range(N_CORES):
        for i in range(NP2):
            b, h = divmod(c * NP2 + i, ML_H)
            HT[b, h * ML_DV:(h + 1) * ML_DV, :] = res[c]["hout"][i].T
    del res

    NQ = 3 * D_
    nc3 = _prog("L3", build_dense, dict(T=T_CORE, TT=TT_, HL=HL_, D=D_, mix=True, ffn=True, F=F_, NG=4, nxt=NQ, final=False))
    shared = dict(w_mix=lay_w(a_w_out[0]), g_nxt=lay_vec(norm_mix[1]), w_nxt=lay_w(b_w_qkv[0]))
    shared.update(_ffn_maps(ffn_w_up, ffn_conv_w, ffn_conv_b, ffn_w_down, norm_ffn, 0))
    res = _run(nc3, [dict(shared, xT=_tiles(XT, c, HL_), hT=_tiles(HT, c, HL_)) for c in range(N_CORES)])
    del shared, HT
    XT = _gather_T(res, "xoT", D_)
    QKV = _gather_T(res, "projT", NQ)
    del res

    NP4 = B_ * MB_H // N_CORES
    nc4 = _prog("L4", build_moba, dict(S=S_, NP=NP4))
    cmask = np.zeros((128, 2 * MB_BS), f32)
    for o in range(2):
        cmask[:, o * MB_BS:(o + 1) * MB_BS] = np.where(np.arange(MB_BS)[None, :] <= o * 128 + np.arange(128)[:, None], 0.0, -30000.0)
    maps = []
    for c in range(N_CORES):
        pairs = [divmod(c * NP4 + i, MB_H) for i in range(NP4)]
        qT = np.stack([QKV[b, h * MB_DH:(h + 1) * MB_DH] for b, h in pairs])
        kT = np.stack([QKV[b, D_ + h * MB_DH:D_ + (h + 1) * MB_DH] for b, h in pairs])
        vT = np.stack([QKV[b, 2 * D_ + h * MB_DH:2 * D_ + (h + 1) * MB_DH] for b, h in pairs])
        maps.append(dict(qT=np.ascontiguousarray(qT), kT=np.ascontiguousarray(kT),
                         vtm=np.ascontiguousarray(vT.transpose(0, 2, 1)), cmask=cmask, ident=ident))
    del QKV
    res = _run(nc4, maps)
    del maps
    AT = np.empty((B_, D_, S_), f32)
    for c in range(N_CORES):
        for i in range(NP4):
            b, h = divmod(c * NP4 + i, MB_H)
            AT[b, h * MB_DH:(h + 1) * MB_DH, :] = res[c]["att"][i].T
    del res

    nc5 = _prog("L5", build_dense, dict(T=T_CORE, TT=TT_, HL=HL_, D=D_, mix=True, ffn=True, F=F_, NG=4, nxt=None, final=True))
    shared = dict(w_mix=lay_w(b_w_out[0]), g_fin=lay_vec(final_norm))
    shared.update(_ffn_maps(ffn_w_up, ffn_conv_w, ffn_conv_b, ffn_w_down, norm_ffn, 1))
    res = _run(nc5, [dict(shared, xT=_tiles(XT, c, HL_), hT=_tiles(AT, c, HL_)) for c in range(N_CORES)])
    del shared, AT, XT
    OT = _gather_T(res, "xoT", D_)
    return np.ascontiguousarray(OT.transpose(0, 2, 1))
```

```python
import numpy as np
from contextlib import ExitStack
import concourse.bass as bass
import concourse.mybir as mybir
from concourse.bass_utils import run_bass_kernel_spmd

F32 = mybir.dt.float32
BF16 = mybir.dt.bfloat16
AF = mybir.ActivationFunctionType
ALU = mybir.AluOpType
AX = mybir.AxisListType

NORM_EPS = 1e-6
N_CORES = 8
GROUPS = [[0, 1, 2, 3], [4, 5, 6, 7]]
SINGLE_BLOCK = True


class Buf:
    __slots__ = ("name", "last_w", "readers", "dsem", "dcount", "slot")

    def __init__(self, name):
        self.name = name
        self.last_w = None
        self.readers = []
        self.dsem = None
        self.dcount = 0
        self.slot = None


class Op:
    __slots__ = ("eng", "fn", "waits", "signal", "idx", "is_dma", "dsem", "dcount", "count", "inc")

    def __init__(self, eng, fn, idx, is_dma):
        self.eng = eng
        self.fn = fn
        self.idx = idx
        self.is_dma = is_dma
        self.waits = []
        self.signal = False
        self.dsem = None
        self.dcount = 0
        self.count = 0
        self.inc = 16


ENGS = ("pe", "act", "dve", "pool", "sp")


class Prog:
    def __init__(self, nc, stack):
        self.nc = nc
        self.stack = stack
        self.ops = {e: [] for e in ENGS}
        self.esem = {e: stack.enter_context(nc.semaphore("es_" + e)) for e in ENGS}
        self.seen_e = {e: {} for e in ENGS}
        self.seen_d = {e: {} for e in ENGS}
        self.nidx = {e: 0 for e in ENGS}
        self.nsig = {e: 0 for e in ENGS}
        self.pstart = {e: 0 for e in ENGS}
        self.free_slots = []
        self.phase_bufs = []
        self.nbuf = 0
        self.ndsem = 0
        self._rank = None

    def buf(self, name="b"):
        self.nbuf += 1
        b = Buf(f"{name}{self.nbuf}")
        self.phase_bufs.append(b)
        return b

    def bufs(self, n, name="b"):
        return [self.buf(name) for _ in range(n)]

    def _get_dsem(self, b):
        if b.dsem is None:
            if self.free_slots:
                slot = self.free_slots.pop()
            else:
                self.ndsem += 1
                slot = [self.stack.enter_context(self.nc.semaphore(f"ds{self.ndsem}")), 0]
            b.slot = slot
            b.dsem = slot[0]
            b.dcount = slot[1]
        return b.dsem

    def rank(self, engine):
        if self._rank is None:
            self._rank = engine.partition_id() % 4
        return self._rank

    def op(self, eng, fn, reads=(), writes=(), dma=False, sem_buf=None, inc=16):
        lst = self.ops[eng]
        o = Op(eng, fn, self.nidx[eng], dma)
        self.nidx[eng] += 1
        o.inc = inc
        deps = []
        for b in reads:
            if b.last_w is not None:
                deps.append(b.last_w)
        for b in writes:
            if b.last_w is not None:
                deps.append(b.last_w)
            deps.extend(b.readers)
        emax = {}
        dmax = {}
        for d in deps:
            if d.is_dma:
                k = id(d.dsem)
                if k not in dmax or dmax[k][1] < d.dcount:
                    dmax[k] = (d.dsem, d.dcount)
            else:
                if d.eng == eng and not dma:
                    if eng == "pe":
                        continue
                if d.eng not in emax or emax[d.eng].idx < d.idx:
                    emax[d.eng] = d
        se = self.seen_e[eng]
        for e2, d in emax.items():
            if se.get(e2, -1) >= d.idx:
                continue
            se[e2] = d.idx
            d.signal = True
            o.waits.append(("e", d))
        sd = self.seen_d[eng]
        for k, (sem, cnt) in dmax.items():
            if sd.get(k, -1) >= cnt:
                continue
            sd[k] = cnt
            o.waits.append(("d", sem, cnt))
        if dma:
            sb = sem_buf
            if sb is None:
                sb = writes[0] if writes else reads[0]
            o.dsem = self._get_dsem(sb)
            sb.dcount += inc
            o.dcount = sb.dcount
        for b in reads:
            b.readers.append(o)
        for b in writes:
            b.last_w = o
            b.readers = []
        lst.append(o)
        return o

    def barrier(self):
        lasts = {}
        for e in ENGS:
            for p in reversed(self.ops[e][self.pstart[e]:]):
                if not p.is_dma and p.fn is not None:
                    lasts[e] = p
                    p.signal = True
                    break
        dm = {}
        for e in ENGS:
            for p in self.ops[e][self.pstart[e]:]:
                if p.is_dma:
                    k = id(p.dsem)
                    if k not in dm or dm[k][1] < p.dcount:
                        dm[k] = (p.dsem, p.dcount)
        for e in ENGS:
            o = Op(e, None, self.nidx[e], False)
            self.nidx[e] += 1
            for e2, last in lasts.items():
                if self.seen_e[e].get(e2, -1) < last.idx:
                    o.waits.append(("e", last))
                    self.seen_e[e][e2] = last.idx
            for k, (sem, cnt) in dm.items():
                if self.seen_d[e].get(k, -1) < cnt:
                    o.waits.append(("d", sem, cnt))
                    self.seen_d[e][k] = cnt
            self.ops[e].append(o)
        self.pstart = {e: len(self.ops[e]) for e in ENGS}

    def end_phase(self):
        self.barrier()
        print("phase ops", {e: len(self.ops[e]) for e in ENGS}, "nsig(before emit)", dict(self.nsig),
              "max dma sem", max([sl[1] for sl in self.free_slots] + [b.dcount for b in self.phase_bufs] + [0]), flush=True)
        if not SINGLE_BLOCK:
            self.emit()
        for b in self.phase_bufs:
            if b.slot is not None:
                b.slot[1] = b.dcount
                self.free_slots.append(b.slot)
                b.slot = None
        self.phase_bufs = []

    def emit(self):
        nc = self.nc
        for e in ENGS:
            c = self.nsig[e]
            for o in self.ops[e]:
                if o.signal:
                    c += 1
                    o.count = c
            self.nsig[e] = c
        handles = {"pe": "tensor", "act": "scalar", "dve": "vector", "pool": "gpsimd", "sp": "sync"}
        with nc.Block() as block:
            for e in ENGS:
                if not self.ops[e]:
                    continue

                def body(engine, e=e):
                    self._rank = None
                    for o in self.ops[e]:
                        for w in o.waits:
                            if w[0] == "e":
                                engine.wait_ge(self.esem[w[1].eng], w[1].count)
                            else:
                                engine.wait_ge(w[1], w[2])
                        if o.fn is None:
                            continue
                        ins = o.fn(engine)
                        if o.is_dma:
                            ins.then_inc(o.dsem, o.inc)
                        elif o.signal:
                            ins.then_inc(self.esem[e], 1)
                    self._rank = None

                getattr(block, handles[e])(body)
        self.ops = {e: [] for e in ENGS}
        self.pstart = {e: 0 for e in ENGS}


def lay_w(W):
    K, N = W.shape
    NC = -(-N // 128)
    if NC * 128 != N:
        Wp = np.zeros((K, NC * 128), W.dtype)
        Wp[:, :N] = W
        W = Wp
    KC = K // 128
    return np.ascontiguousarray(W.reshape(KC, 128, NC, 128).transpose(2, 1, 0, 3))


def lay_w_tm(W, gcols=512):
    K, N = W.shape
    NG = -(-N // gcols)
    if NG * gcols != N:
        Wp = np.zeros((K, NG * gcols), W.dtype)
        Wp[:, :N] = W
        W = Wp
    KC = K // 128
    return np.ascontiguousarray(W.reshape(KC, 128, NG, gcols).transpose(2, 1, 0, 3))


def lay_vec(v):
    n = v.shape[0]
    NC = -(-n // 128)
    if NC * 128 != n:
        vp = np.zeros((NC * 128,), v.dtype)
        vp[:n] = v
        v = vp
    return np.ascontiguousarray(v.reshape(NC, 128).T)


def _split_last(n, cap=2048):
    for b in range(min(n, cap), 0, -1):
        if n % b == 0:
            return b
    return 1


def emit_dense(nc, P, cfg):
    NT, TT, HL, D = cfg["NT"], cfg["TT"], cfg["HL"], cfg["D"]
    KC = D // 128
    TW = TT + HL
    src = cfg["src"]
    mix, ffn, nxt, final = cfg.get("mix"), cfg.get("ffn"), cfg.get("nxt"), cfg.get("final")
    fm = nxt["fm"] if nxt and nxt.get("fm") else []
    tm = nxt["tm"] if nxt and nxt.get("tm") else []
    has_x = src == "x"
    F = ffn["F"] if ffn else 0
    FC = F // 128
    NG = ffn["NG"] if ffn else 1

    with ExitStack() as st:
        pfx = cfg.get("name", "d") + "_"

        def sb(name, shape, dt):
            return st.enter_context(nc.sbuf_tensor(pfx + name, list(shape), dt))

        def ps(name, shape, dt=F32):
            return st.enter_context(nc.psum_tensor(pfx + name, list(shape), dt))

        if has_x:
            x1 = sb("x1", (128, KC, TW), F32)
            x1_b = P.bufs(KC, "x1_")
            ones = sb("ones", (128, 128), F32)
            ones_b = P.buf("ones")
            sq = [sb(f"sq{i}", (128, TW), F32) for i in range(2)]
            sq_b = P.bufs(2, "sq")
            rstd = sb("rstd", (128, TW), F32)
            rstd_b = P.buf("rstd")
            eps_t = sb("eps_t", (128, 1), F32)
            eps_b = P.buf("eps")
        hn = sb("hn", (128, KC, TW), BF16)
        hn_b = P.bufs(KC, "hn_")
        NWB = cfg.get("NWB", 3)
        if ffn:
            gsz = [FC // NG + (1 if i < FC % NG else 0) for i in range(NG)]
            gst = [sum(gsz[:i]) for i in range(NG)]
            FG = max(gsz)
        KW = max(KC, FG if ffn else 0)
        wt = [sb(f"wt{i}", (128, KW, 128), BF16) for i in range(NWB)]
        wt_b = P.bufs(NWB, "wt")
        NEV = 3
        EVW = max(TT, 512) if tm else TT
        if fm or tm or final:
            ev = [sb(f"ev{i}", (128, EVW), F32) for i in range(NEV)]
            ev_b = P.bufs(NEV, "ev")
        if tm:
            wtm = [sb(f"wtm{i}", (128, KC, 512), BF16) for i in range(2)]
            wtm_b = P.bufs(2, "wtm")
        gvec = {}
        if ffn:
            gt = sb("gt", (128, FG, TT), BF16)
            gt_b = P.bufs(FG, "gt_")
            ucat = [sb(f"uc{i}", (128, TW), F32) for i in range(4)]
            ucat_b = P.bufs(4, "uc")
            cacc = [sb(f"ca{i}", (128, TT), F32) for i in range(4)]
            cacc_b = P.bufs(4, "ca")
            cw_s = sb("cw_s", (128, 3, 2 * FC), F32)
            cb_s = sb("cb_s", (128, 2 * FC), F32)
            cwb = P.buf("cw")
            sv = sb("sv", (128, 2 * FC, 2), F32)
            sv_b = P.bufs(2 * FC, "sv")
            gvec["ffn"] = (sb("g_ffn_s", (128, KC), F32), P.buf("gffn"), ffn["g"])
        if mix:
            hb = hn
            hb_b = hn_b
        if nxt and nxt.get("g") is not None:
            gvec["nxt"] = (sb("g_nxt_s", (128, KC), F32), P.buf("gnxt"), nxt["g"])
        if final:
            gvec["fin"] = (sb("g_fin_s", (128, KC), F32), P.buf("gfin"), final["g"])
        NPS = 6
        acc = [ps(f"acc{i}", (128, 512)) for i in range(NPS)]
        acc_b = P.bufs(NPS, "acc")
        if has_x:
            ssp = ps("ssp", (128, 512))
            ssp_b = P.buf("ssp")
        if HL:
            hal = ps("hal", (128, 512))
            hal_b = P.bufs(8, "hal")

        if has_x:
            P.op("pool", lambda e: e.memset(ones[:, :], 1.0), writes=[ones_b])
            P.op("pool", lambda e: e.memset(eps_t[:, :], NORM_EPS), writes=[eps_b])
        for k, (t, b, d) in gvec.items():
            P.op("sp", lambda e, t=t, d=d: e.dma_start(out=t[:, :], in_=d.ap()), writes=[b], dma=True)
        if ffn:
            P.op("sp", lambda e: e.dma_start(out=cw_s[:, :, :], in_=ffn["cw"].ap()), writes=[cwb], dma=True)
            P.op("sp", lambda e: e.dma_start(out=cb_s[:, :], in_=ffn["cb"].ap()), writes=[cwb], dma=True)

        cnt = {"w": 0, "acc": 0, "ev": 0, "sq": 0, "hal": 0, "uc": 0, "ca": 0, "wtm": 0}

        def load_w(dram_ap_2d, nk):
            i = cnt["w"] % NWB
            cnt["w"] += 1
            n = nk * 128
            bsz = _split_last(n)
            s_ = dram_ap_2d.rearrange("p (a b) -> p a b", b=bsz)
            dst = wt[i][:, 0:nk, :].rearrange("p k j -> p (k j)").rearrange("p (a b) -> p a b", b=bsz)
            P.op("pool", lambda e: e.dma_start(out=dst, in_=s_), writes=[wt_b[i]], dma=True)
            return i

        def load_wtm(dram_ap_2d):
            i = cnt["wtm"] % 2
            cnt["wtm"] += 1
            hk = KC // 2
            for h in range(2):
                s_ = dram_ap_2d[:, h * hk * 512:(h + 1) * hk * 512].rearrange("p (a b) -> p a b", b=512)
                dst = wtm[i][:, h * hk:(h + 1) * hk, :]
                P.op("pool", lambda e, dst=dst, s_=s_: e.dma_start(out=dst, in_=s_), writes=[wtm_b[i]], dma=True)
            return i

        def next_acc():
            i = cnt["acc"] % NPS
            cnt["acc"] += 1
            return i

        def compute_rstd(do_halo=True):
            hs = None
            do_halo = do_halo and HL
            if do_halo:
                hs = cnt["hal"] % 8
                cnt["hal"] += 1
            for kc in range(KC):
                i = cnt["sq"] % 2
                cnt["sq"] += 1
                P.op("act", lambda e, kc=kc, i=i: e.activation(out=sq[i][:, :], in_=x1[:, kc, :], func=AF.Square),
                     reads=[x1_b[kc]], writes=[sq_b[i]])
                P.op("pe", lambda e, kc=kc, i=i: e.matmul(ssp[:, 0:TT], ones[:, :], sq[i][:, HL:TW],
                                                          start=(kc == 0), stop=(kc == KC - 1)),
                     reads=[ones_b, sq_b[i]], writes=[ssp_b])
                if do_halo:
                    P.op("pe", lambda e, kc=kc, i=i, hs=hs: e.matmul(hal[:, hs * HL:(hs + 1) * HL], ones[:, :], sq[i][:, 0:HL],
                                                                     start=(kc == 0), stop=(kc == KC - 1)),
                         reads=[ones_b, sq_b[i]], writes=[hal_b[hs]])
            P.op("act", lambda e: e.activation(out=rstd[:, HL:TW], in_=ssp[:, 0:TT], func=AF.Sqrt,
                                               bias=eps_t[:, 0:1], scale=1.0 / D),
                 reads=[ssp_b, eps_b], writes=[rstd_b])
            if do_halo:
                P.op("act", lambda e, hs=hs: e.activation(out=rstd[:, 0:HL], in_=hal[:, hs * HL:(hs + 1) * HL], func=AF.Sqrt,
                                                          bias=eps_t[:, 0:1], scale=1.0 / D),
                     reads=[hal_b[hs], eps_b], writes=[rstd_b])
            P.op("dve", lambda e: e.reciprocal(out=rstd[:, :], in_=rstd[:, :]), reads=[rstd_b], writes=[rstd_b])

        def rmsnorm_to_hn(gkey, do_halo=True):
            gt_, gb_, _ = gvec[gkey]
            compute_rstd(do_halo)
            for kc in range(KC):
                P.op("dve", lambda e, kc=kc: e.scalar_tensor_tensor(
                    out=hn[:, kc, :], in0=x1[:, kc, :], scalar=gt_[:, kc:kc + 1], in1=rstd[:, :],
                    op0=ALU.mult, op1=ALU.mult),
                    reads=[x1_b[kc], gb_, rstd_b], writes=[hn_b[kc]])

        def gemm(wslot, nk, rhs_fn, rhs_bufs, acc_i, halo_slot=None, halo_rhs_fn=None):
            for k in range(nk):
                P.op("pe", lambda e, k=k: e.matmul(acc[acc_i][:, 0:TT], wt[wslot][:, k, :], rhs_fn(k),
                                                   start=(k == 0), stop=(k == nk - 1)),
                     reads=[wt_b[wslot], rhs_bufs[k]], writes=[acc_b[acc_i]])
            if halo_slot is not None:
                for k in range(nk):
                    P.op("pe", lambda e, k=k: e.matmul(hal[:, halo_slot * HL:(halo_slot + 1) * HL],
                                                       wt[wslot][:, k, :], halo_rhs_fn(k),
                                                       start=(k == 0), stop=(k == nk - 1)),
                         reads=[wt_b[wslot], rhs_bufs[k]], writes=[hal_b[halo_slot]])

        for tt in range(NT):
            first = tt == 0
            if has_x:
                for kc in range(KC):
                    P.op("sp", lambda e, kc=kc, tt=tt: e.dma_start(out=x1[:, kc, :], in_=cfg["x_ap"](tt, kc)),
                         writes=[x1_b[kc]], dma=True)
            else:
                for (k0, nk, ap) in cfg["hn_load"](tt):
                    P.op("sp", lambda e, k0=k0, nk=nk, ap=ap: e.dma_start(
                        out=hn[:, k0:k0 + nk, :], in_=ap.rearrange("(k p) t -> p k t", p=128)),
                        writes=hn_b[k0:k0 + nk], dma=True)
            if mix:
                P.op("sp", lambda e, tt=tt: e.dma_start(out=hb[:, :, HL:TW],
                                                        in_=mix["main"](e, tt).rearrange("(k p) t -> p k t", p=128)),
                     writes=hb_b, dma=True)
                P.op("sp", lambda e, tt=tt: e.dma_start(out=hb[:, :, 0:HL],
                                                        in_=mix["halo"](e, tt).rearrange("(k p) t -> p k t", p=128)),
                     writes=hb_b, dma=True)
                for dc in range(KC):
                    ws = load_w(mix["w"][dc].rearrange("p k j -> p (k j)"), KC)
                    ai = next_acc()
                    hs = None
                    if HL and first:
                        hs = cnt["hal"] % 8
                        cnt["hal"] += 1
                    gemm(ws, KC, lambda k: hb[:, k, HL:TW], hb_b, ai, hs, lambda k: hb[:, k, 0:HL])
                    P.op("dve", lambda e, dc=dc, ai=ai: e.tensor_add(out=x1[:, dc, HL:TW], in0=x1[:, dc, HL:TW], in1=acc[ai][:, 0:TT]),
                         reads=[acc_b[ai], x1_b[dc]], writes=[x1_b[dc]])
                    if HL and first:
                        P.op("dve", lambda e, dc=dc, hs=hs: e.tensor_add(out=x1[:, dc, 0:HL], in0=x1[:, dc, 0:HL],
                                                                        in1=hal[:, hs * HL:(hs + 1) * HL]),
                             reads=[hal_b[hs], x1_b[dc]], writes=[x1_b[dc]])
            if ffn:
                w_up, w_dn = ffn["w_up"], ffn["w_dn"]
                rmsnorm_to_hn("ffn", first)
                for g in range(NG):
                    for fl in range(gsz[g]):
                        fc = gst[g] + fl
                        res = []
                        for half in range(2):
                            col = fc + half * FC
                            ws = load_w(w_up[col].rearrange("p k j -> p (k j)"), KC)
                            ai = next_acc()
                            hs = None
                            if first:
                                hs = cnt["hal"] % 8
                                cnt["hal"] += 1
                            gemm(ws, KC, lambda k: hn[:, k, HL:TW], hn_b, ai, hs, lambda k: hn[:, k, 0:HL])
                            ui = cnt["uc"] % 4
                            cnt["uc"] += 1
                            ci = cnt["ca"] % 4
                            cnt["ca"] += 1
                            P.op("act", lambda e, ui=ui, ai=ai: e.copy(out=ucat[ui][:, HL:TW], in_=acc[ai][:, 0:TT]),
                                 reads=[acc_b[ai]], writes=[ucat_b[ui]])
                            if first:
                                P.op("act", lambda e, ui=ui, hs=hs: e.copy(out=ucat[ui][:, 0:HL], in_=hal[:, hs * HL:(hs + 1) * HL]),
                                     reads=[hal_b[hs]], writes=[ucat_b[ui]])
                            else:
                                P.op("act", lambda e, ui=ui, col=col: e.copy(out=ucat[ui][:, 0:HL], in_=sv[:, col, :]),
                                     reads=[sv_b[col]], writes=[ucat_b[ui]])
                            if tt < NT - 1:
                                P.op("pool", lambda e, ui=ui, col=col: e.tensor_copy(out=sv[:, col, :], in_=ucat[ui][:, TW - HL:TW]),
                                     reads=[ucat_b[ui]], writes=[sv_b[col]])
                            P.op("act", lambda e, ai=ai, ci=ci, col=col: e.activation(
                                out=cacc[ci][:, :], in_=acc[ai][:, 0:TT], func=AF.Identity,
                                scale=cw_s[:, 2, col:col + 1], bias=cb_s[:, col:col + 1]),
                                reads=[acc_b[ai], cwb], writes=[cacc_b[ci]])
                            P.op("dve", lambda e, ui=ui, ci=ci, col=col: e.scalar_tensor_tensor(
                                out=cacc[ci][:, :], in0=ucat[ui][:, 1:TW - 1], scalar=cw_s[:, 1, col:col + 1],
                                in1=cacc[ci][:, :], op0=ALU.mult, op1=ALU.add),
                                reads=[ucat_b[ui], cwb, cacc_b[ci]], writes=[cacc_b[ci]])
                            P.op("dve", lambda e, ui=ui, ci=ci, col=col: e.scalar_tensor_tensor(
                                out=cacc[ci][:, :], in0=ucat[ui][:, 0:TW - 2], scalar=cw_s[:, 0, col:col + 1],
                                in1=cacc[ci][:, :], op0=ALU.mult, op1=ALU.add),
                                reads=[ucat_b[ui], cwb, cacc_b[ci]], writes=[cacc_b[ci]])
                            res.append(ci)
                        cg, cu = res
                        P.op("act", lambda e, cg=cg: e.activation(out=cacc[cg][:, :], in_=cacc[cg][:, :], func=AF.Silu),
                             reads=[cacc_b[cg]], writes=[cacc_b[cg]])
                        P.op("pool", lambda e, cg=cg, cu=cu, fl=fl: e.tensor_tensor(
                            out=gt[:, fl, :], in0=cacc[cg][:, :], in1=cacc[cu][:, :], op=ALU.mult),
                            reads=[cacc_b[cg], cacc_b[cu]], writes=[gt_b[fl]])
                    for dc in range(KC):
                        ws = load_w(w_dn[dc, :, gst[g]:gst[g] + gsz[g], :].rearrange("p k j -> p (k j)"), gsz[g])
                        ai = next_acc()
                        gemm(ws, gsz[g], lambda k: gt[:, k, :], gt_b, ai)
                        P.op("dve", lambda e, dc=dc, ai=ai: e.tensor_add(out=x1[:, dc, HL:TW], in0=x1[:, dc, HL:TW], in1=acc[ai][:, 0:TT]),
                             reads=[acc_b[ai], x1_b[dc]], writes=[x1_b[dc]])
            if cfg.get("after_ffn"):
                cfg["after_ffn"](tt, x1, x1_b, HL, TW)
            if nxt:
                if nxt.get("g") is not None:
                    rmsnorm_to_hn("nxt", False)
                if nxt.get("hn_store"):
                    nxt["hn_store"](tt, hn, hn_b, HL, TW)
                for oc, (w_ap, out_fn) in enumerate(fm):
                    ws = load_w(w_ap, KC)
                    ai = next_acc()
                    gemm(ws, KC, lambda k: hn[:, k, HL:TW], hn_b, ai)
                    i = cnt["ev"] % NEV
                    cnt["ev"] += 1
                    if oc % 2 == 0:
                        P.op("act", lambda e, i=i, ai=ai: e.copy(out=ev[i][:, 0:TT], in_=acc[ai][:, 0:TT]),
                             reads=[acc_b[ai]], writes=[ev_b[i]])
                    else:
                        P.op("dve", lambda e, i=i, ai=ai: e.tensor_copy(out=ev[i][:, 0:TT], in_=acc[ai][:, 0:TT]),
                             reads=[acc_b[ai]], writes=[ev_b[i]])
                    for (dst, s_) in out_fn(tt, ev[i]):
                        P.op("sp", lambda e, dst=dst, s_=s_: e.dma_start(out=dst, in_=s_), reads=[ev_b[i]], dma=True)
                for gi_, (w_ap, ncols, out_fn) in enumerate(tm):
                    wsl = load_wtm(w_ap)
                    for tb in range(TT // 128):
                        ai = next_acc()
                        for k in range(KC):
                            P.op("pe", lambda e, k=k, ai=ai, tb=tb, wsl=wsl, ncols=ncols: e.matmul(
                                acc[ai][:, 0:ncols], hn[:, k, HL + tb * 128:HL + (tb + 1) * 128], wtm[wsl][:, k, 0:ncols],
                                start=(k == 0), stop=(k == KC - 1)),
                                reads=[wtm_b[wsl], hn_b[k]], writes=[acc_b[ai]])
                        i = cnt["ev"] % NEV
                        cnt["ev"] += 1
                        if (gi_ + tb) % 2 == 0:
                            P.op("act", lambda e, i=i, ai=ai, ncols=ncols: e.copy(out=ev[i][:, 0:ncols], in_=acc[ai][:, 0:ncols]),
                                 reads=[acc_b[ai]], writes=[ev_b[i]])
                        else:
                            P.op("dve", lambda e, i=i, ai=ai, ncols=ncols: e.tensor_copy(out=ev[i][:, 0:ncols], in_=acc[ai][:, 0:ncols]),
                                 reads=[acc_b[ai]], writes=[ev_b[i]])
                        dst = out_fn(tt, tb)
                        P.op("sp", lambda e, dst=dst, i=i, ncols=ncols: e.dma_start(out=dst, in_=ev[i][:, 0:ncols]),
                             reads=[ev_b[i]], dma=True)
            if final:
                gt_, gb_, _ = gvec["fin"]
                compute_rstd(False)
                for kc in range(KC):
                    i = cnt["ev"] % NEV
                    cnt["ev"] += 1
                    P.op("dve", lambda e, kc=kc, i=i: e.scalar_tensor_tensor(
                        out=ev[i][:, 0:TT], in0=x1[:, kc, HL:TW], scalar=gt_[:, kc:kc + 1], in1=rstd[:, HL:TW],
                        op0=ALU.mult, op1=ALU.mult),
                        reads=[x1_b[kc], gb_, rstd_b], writes=[ev_b[i]])
                    P.op("sp", lambda e, kc=kc, i=i, tt=tt: e.dma_start(out=final["out"](tt, kc), in_=ev[i][:, 0:TT]),
                         reads=[ev_b[i]], dma=True)
        P.end_phase()
        print("dense phase", cfg.get("name"), "sbuf bytes remaining", nc.sbuf_bytes_remaining, flush=True)

def emit_mlstm(nc, P, cfg, io):
    S, NP, TT, KCD = cfg["S"], cfg["NP"], cfg["TT"], cfg["KC"]
    L, DK, DV, GC = 64, 256, 512, 4
    NCH = S // L
    NGR = NCH // GC
    GT = GC * L
    KSC = DK ** -0.5
    CAP = 15.0
    assert NCH <= 128 and TT % GT == 0
    qT, kT, ktm, vtm, otm, gi, gf = (io[k] for k in ("qT", "kT", "ktm", "vtm", "otm", "gi", "gf"))
    gbias, gain, negmask, ident = (io[k] for k in ("gbias", "gain", "negmask", "ident"))
    gs_u, gs_a, gs_ao = io["gs_u"], io["gs_a"], io["gs_ao"]
    srcH, dstH = io["srcH"], io["dstH"]
    NPC = S // TT

    with ExitStack() as st:

        def sb(name, shape, dt=F32):
            return st.enter_context(nc.sbuf_tensor("ml_" + name, list(shape), dt))

        def ps(name, shape, dt=F32):
            return st.enter_context(nc.psum_tensor("ml_" + name, list(shape), dt))

        idt = sb("idt", (128, 128)); idt_b = P.buf("idt")
        nm = sb("nm", (64, 64)); nm_b = P.buf("nm")
        gb = sb("gb", (128, NP * 2)); gb_b = P.buf("gb")
        gn = sb("gn", (64, NP * DV)); gn_b = P.buf("gn")
        one = sb("one", (128, 64)); one_b = P.buf("one")
        onec = sb("onec", (64, 1), BF16); onec_b = P.buf("onec")
        epsc = sb("epsc", (128, 1)); epsc_b = P.buf("epsc")
        P.op("sp", lambda e: e.dma_start(out=idt[:, :], in_=ident.ap()), writes=[idt_b], dma=True)
        P.op("sp", lambda e: e.dma_start(out=nm[:, :], in_=negmask.ap()), writes=[nm_b], dma=True)
        P.op("sp", lambda e: e.dma_start(out=gb[:, :], in_=gbias.ap()), writes=[gb_b], dma=True)
        P.op("sp", lambda e: e.dma_start(out=gn[:, :], in_=gain.ap()), writes=[gn_b], dma=True)
        P.op("pool", lambda e: e.memset(one[:, :], 1.0), writes=[one_b])
        P.op("pool", lambda e: e.memset(onec[:, :], 1.0), writes=[onec_b])
        P.op("pool", lambda e: e.memset(epsc[:, :], NORM_EPS), writes=[epsc_b])
        idb = sb("idb", (64, 64), BF16); idb_b = P.buf("idb")
        P.op("act", lambda e: e.copy(out=idb[:, :], in_=idt[0:64, 0:64]), reads=[idt_b], writes=[idb_b])
        zt = sb("zt", (128, KCD * 2), BF16); zt_b = P.buf("zt")
        P.op("pool", lambda e: e.memset(zt[:, :], 0.0), writes=[zt_b])
        dstH_b = P.buf("dstH")
        srcH_b = P.bufs(NPC, "srcH")
        P.op("sp", lambda e: e.dma_start(out=dstH.ap()[0:KCD * 128, TT - 2:TT].rearrange("(k p) c -> p k c", p=128),
                                         in_=zt[:, :].rearrange("p (k c) -> p k c", c=2)),
             reads=[zt_b], writes=[dstH_b], dma=True)
        if io.get("dstB") is not None:
            dstB_b0 = P.buf("dstB0")
            P.op("pool", lambda e: e.dma_start(out=io["dstB"].ap()[0:KCD * 128, :].rearrange("(k p) c -> p k c", p=128),
                                               in_=zt[:, :].rearrange("p (k c) -> p k c", c=2)),
                 reads=[zt_b], writes=[dstB_b0], dma=True)

        ps_s = [ps(f"ps_s{i}", (64, 64)) for i in range(2)]; ps_s_b = P.bufs(2, "pss")
        ps_num = [ps(f"ps_num{i}", (64, 512)) for i in range(2)]; ps_num_b = P.bufs(2, "psn")
        ps_c = [ps(f"ps_c{i}", (128, 512)) for i in range(2)]; ps_c_b = P.bufs(2, "psc")
        ps_sm = ps("ps_sm", (128, 512))
        psT = ps("psT", (128, 4, GT), BF16); psT_b = P.buf("psT")
        ps_den_b = P.bufs(2, "psd")
        ps_n_b = P.bufs(2, "psnn")

        gi_t = sb("gi_t", (NCH, NP, L)); gf_t = sb("gf_t", (NCH, NP, L))
        G = P.buf("gates")
        for p in range(NP):
            P.op("sp", lambda e, p=p: e.dma_start(out=gi_t[:, p, :], in_=gi[p].rearrange("(c j) -> c j", j=L)), writes=[G], dma=True)
            P.op("sp", lambda e, p=p: e.dma_start(out=gf_t[:, p, :], in_=gf[p].rearrange("(c j) -> c j", j=L)), writes=[G], dma=True)
        ti = sb("ti", (NCH, NP, L)); tf = sb("tf", (NCH, NP, L))
        e1 = sb("e1", (NCH, NP, L)); lfn = sb("lfn", (NCH, NP, L))
        bb = sb("bb", (NCH, NP, L)); ww = sb("ww", (NCH, NP, L)); cm = sb("cm", (NCH, NP, L))
        uu = sb("uu", (NCH, NP, L)); aa = sb("aa", (NCH, NP, L)); ub = sb("ub", (NCH, NP, L))
        enm = sb("enm", (NCH, NP, L)); wi = sb("wi", (NCH, NP, L))
        bl = sb("bl", (NCH, NP)); md = sb("md", (NCH, NP)); mn = sb("mn", (NCH, NP)); mp = sb("mp", (NCH, NP))
        bmn = sb("bmn", (NCH, NP)); aot = sb("aot", (NCH, NP)); ao = sb("ao", (NCH, NP))
        blT = sb("blT", (NP, NCH)); mdT = sb("mdT", (NP, NCH)); mnT = sb("mnT", (NP, NCH)); mpT = sb("mpT", (NP, NCH))
        w_col = sb("w_col", (64, NP, NCH)); wik_col = sb("wik_col", (64, NP, NCH)); enm_col = sb("enm_col", (64, NP, NCH))

        GS = P.buf("gs_dram")

        def g_op(eng, fn, extra_r=()):
            P.op(eng, fn, reads=[G] + list(extra_r), writes=[G])

        for p in range(NP):
            g_op("dve", lambda e, p=p: e.tensor_scalar(out=ti[:, p, :], in0=gi_t[:, p, :], scalar1=gb[0:NCH, 2 * p:2 * p + 1],
                                                       scalar2=1.0 / CAP, op0=ALU.add, op1=ALU.mult), [gb_b])
            g_op("dve", lambda e, p=p: e.tensor_scalar(out=tf[:, p, :], in0=gf_t[:, p, :], scalar1=gb[0:NCH, 2 * p + 1:2 * p + 2],
                                                       scalar2=1.0 / CAP, op0=ALU.add, op1=ALU.mult), [gb_b])
        g_op("act", lambda e: e.activation(out=ti[:, :, :], in_=ti[:, :, :], func=AF.Tanh))
        g_op("act", lambda e: e.activation(out=tf[:, :, :], in_=tf[:, :, :], func=AF.Tanh))
        g_op("act", lambda e: e.activation(out=e1[:, :, :], in_=tf[:, :, :], func=AF.Exp, scale=-CAP))
        g_op("act", lambda e: e.activation(out=lfn[:, :, :], in_=e1[:, :, :], func=AF.Ln, bias=one[0:NCH, 0:1], scale=1.0), [one_b])
        for p in range(NP):
            g_op("dve", lambda e, p=p: e.tensor_tensor_scan(out=bb[:, p, :], data0=one[0:NCH, 0:L], data1=lfn[:, p, :], initial=0.0,
                                                            op0=ALU.mult, op1=ALU.subtract), [one_b])
            g_op("dve", lambda e, p=p: e.scalar_tensor_tensor(out=ww[:, p, :], in0=ti[:, p, :], scalar=CAP, in1=bb[:, p, :],
                                                              op0=ALU.mult, op1=ALU.subtract))
            g_op("dve", lambda e, p=p: e.tensor_tensor_scan(out=cm[:, p, :], data0=ww[:, p, :], data1=ww[:, p, :], initial=-1e30,
                                                            op0=ALU.max, op1=ALU.max))
            g_op("dve", lambda e, p=p: e.tensor_copy(out=bl[:, p:p + 1], in_=bb[:, p, L - 1:L]))
            g_op("dve", lambda e, p=p: e.tensor_tensor(out=md[:, p:p + 1], in0=bb[:, p, L - 1:L], in1=cm[:, p, L - 1:L], op=ALU.add))
        g_op("pe", lambda e: e.transpose(ps_num[0][0:NP, 0:NCH], bl[:, :], idt[0:NCH, 0:NCH]), [idt_b])
        g_op("dve", lambda e: e.tensor_copy(out=blT[:, :], in_=ps_num[0][0:NP, 0:NCH]))
        g_op("pe", lambda e: e.transpose(ps_num[0][0:NP, 0:NCH], md[:, :], idt[0:NCH, 0:NCH]), [idt_b])
        g_op("dve", lambda e: e.tensor_copy(out=mdT[:, :], in_=ps_num[0][0:NP, 0:NCH]))
        g_op("dve", lambda e: e.tensor_tensor_scan(out=mnT[:, :], data0=blT[:, :], data1=mdT[:, :], initial=0.0,
                                                   op0=ALU.add, op1=ALU.max))
        g_op("dve", lambda e: e.memset(mpT[:, :], 0.0))
        if NCH > 1:
            g_op("dve", lambda e: e.tensor_copy(out=mpT[:, 1:NCH], in_=mnT[:, 0:NCH - 1]))
        g_op("pe", lambda e: e.transpose(ps_c[0][0:NCH, 0:NP], mnT[:, :], idt[0:NP, 0:NP]), [idt_b])
        g_op("dve", lambda e: e.tensor_copy(out=mn[:, :], in_=ps_c[0][0:NCH, 0:NP]))
        g_op("pe", lambda e: e.transpose(ps_c[0][0:NCH, 0:NP], mpT[:, :], idt[0:NP, 0:NP]), [idt_b])
        g_op("dve", lambda e: e.tensor_copy(out=mp[:, :], in_=ps_c[0][0:NCH, 0:NP]))
        g_op("dve", lambda e: e.tensor_tensor(out=bmn[:, :], in0=bl[:, :], in1=mn[:, :], op=ALU.subtract))
        g_op("dve", lambda e: e.tensor_tensor(out=aot[:, :], in0=mp[:, :], in1=bmn[:, :], op=ALU.add))
        g_op("act", lambda e: e.activation(out=ao[:, :], in_=aot[:, :], func=AF.Exp))
        for p in range(NP):
            g_op("dve", lambda e, p=p: e.tensor_scalar(out=uu[:, p, :], in0=cm[:, p, :], scalar1=mp[:, p:p + 1], scalar2=-1.0,
                                                       op0=ALU.max, op1=ALU.mult))
            g_op("act", lambda e, p=p: e.activation(out=aa[:, p, :], in_=uu[:, p, :], func=AF.Exp, bias=mp[:, p:p + 1], scale=1.0))
            g_op("dve", lambda e, p=p: e.tensor_tensor(out=ub[:, p, :], in0=uu[:, p, :], in1=bb[:, p, :], op=ALU.subtract))
            g_op("act", lambda e, p=p: e.activation(out=enm[:, p, :], in_=ub[:, p, :], func=AF.Exp))
            g_op("act", lambda e, p=p: e.activation(out=wi[:, p, :], in_=ww[:, p, :], func=AF.Exp, bias=bmn[:, p:p + 1], scale=1.0))
            for src, dst, scl in ((ww, w_col, 1.0), (wi, wik_col, KSC), (enm, enm_col, 1.0)):
                g_op("pe", lambda e, p=p, src=src: e.transpose(ps_num[0][0:L, 0:NCH], src[:, p, :], idt[0:NCH, 0:NCH]), [idt_b])
                g_op("act", lambda e, p=p, dst=dst, scl=scl: e.mul(out=dst[:, p, :], in_=ps_num[0][0:L, 0:NCH], mul=scl))
            P.op("sp", lambda e, p=p: e.dma_start(out=gs_u.ap()[p].rearrange("(c j) -> c j", j=L), in_=uu[:, p, :]),
                 reads=[G], writes=[GS], dma=True, sem_buf=GS)
            P.op("sp", lambda e, p=p: e.dma_start(out=gs_a.ap()[p].rearrange("(c j) -> c j", j=L), in_=aa[:, p, :]),
                 reads=[G], writes=[GS], dma=True, sem_buf=GS)
            P.op("sp", lambda e, p=p: e.dma_start(out=gs_ao.ap()[p].rearrange("(c o) -> c o", o=1), in_=ao[:, p:p + 1]),
                 reads=[G], writes=[GS], dma=True, sem_buf=GS)

        ao_bc = sb("ao_bc", (128, NCH)); ao_bc_b = P.buf("aobc")
        Cst = sb("Cst", (128, 2, DV)); Cst_b = P.buf("Cst")
        nst = sb("nst", (128, 2)); nst_b = P.buf("nst")
        Cbf = [sb(f"Cbf{i}", (128, 2, DV), BF16) for i in range(2)]; Cbf_b = P.bufs(2, "Cbf")
        nbf = [sb(f"nbf{i}", (128, 2), BF16) for i in range(2)]; nbf_b = P.bufs(2, "nbf")
        qg = [sb(f"qg{i}", (128, 2, GT)) for i in range(2)]; qg_b = P.bufs(2, "qg")
        kg = [sb(f"kg{i}", (128, 2, GT)) for i in range(2)]; kg_b = P.bufs(2, "kg")
        Ag = [sb(f"Ag{i}", (128, GT)) for i in range(2)]; Ag_b = P.bufs(2, "Ag")
        ug = [sb(f"ug{i}", (64, GT)) for i in range(2)]; ug_b = P.bufs(2, "ug")
        ktg = [sb(f"ktg{i}", (64, GC, DK)) for i in range(2)]; ktg_b = P.bufs(2, "ktg")
        vg = [sb(f"vg{i}", (64, GC, DV)) for i in range(2)]; vg_b = P.bufs(2, "vg")
        og = [sb(f"og{i}", (64, GC, DV)) for i in range(2)]; og_b = P.bufs(2, "og")
        qb = [sb(f"qb{i}", (128, 2, GT), BF16) for i in range(2)]; qb_b = P.bufs(2, "qb")
        kb = [sb(f"kb{i}", (128, 2, GT), BF16) for i in range(2)]; kb_b = P.bufs(2, "kb")
        qtb = [sb(f"qtb{i}", (128, 2, GT), BF16) for i in range(2)]; qtb_b = P.bufs(2, "qtb")
        vb = [sb(f"vb{i}", (64, GC, DV), BF16) for i in range(2)]; vb_b = P.bufs(2, "vb")
        kwb = [sb(f"kwb{i}", (64, GC, DK), BF16) for i in range(2)]; kwb_b = P.bufs(2, "kwb")
        sg = [sb(f"sg{i}", (64, GC, DV)) for i in range(2)]; sg_b = P.bufs(2, "sg")
        hog = [sb(f"hog{i}", (64, GC, DV), BF16) for i in range(2)]; hog_b = P.bufs(2, "hog")
        hTs = [sb(f"hTs{i}", (128, 4, GT), BF16) for i in range(2)]; hTs_b = P.bufs(2, "hTs")
        X = [sb(f"X{i}", (64, 64)) for i in range(2)]; X_b = P.bufs(2, "X")
        E = [sb(f"E{i}", (64, 64)) for i in range(2)]; E_b = P.bufs(2, "E")
        sDT = [sb(f"sDT{i}", (64, 64), BF16) for i in range(2)]; sDT_b = P.bufs(2, "sDT")
        dn = [sb(f"dn{i}", (64, 1)) for i in range(2)]; dn_b = P.bufs(2, "dn")
        hraw = [sb(f"hraw{i}", (64, DV)) for i in range(2)]; hraw_b = P.bufs(2, "hraw")
        junk = [sb(f"junk{i}", (64, DV)) for i in range(2)]; junk_b = P.bufs(2, "junk")
        ss = [sb(f"ss{i}", (64, 1)) for i in range(2)]; ss_b = P.bufs(2, "ss")
        hn_ = [sb(f"hn_{i}", (64, DV)) for i in range(2)]; hn_b = P.bufs(2, "hn")

        def bc_ap(t, off, nparts, n):
            return bass.AP(t, off, [[0, nparts], [1, n]])

        def load_group(p, g):
            gg = p * NGR + g
            i = gg % 2
            t0 = g * GT
            P.op("sp", lambda e: e.dma_start(out=qg[i][:, :, :], in_=qT[p, :, t0:t0 + GT].rearrange("(k d) t -> d k t", d=128)),
                 writes=[qg_b[i]], dma=True)
            P.op("sp", lambda e: e.dma_start(out=kg[i][:, :, :], in_=kT[p, :, t0:t0 + GT].rearrange("(k d) t -> d k t", d=128)),
                 writes=[kg_b[i]], dma=True)
            P.op("sp", lambda e: e.dma_start(out=Ag[i][:, :], in_=bc_ap(gs_a, p * S + t0, 128, GT)), reads=[GS], writes=[Ag_b[i]], dma=True)
            P.op("sp", lambda e: e.dma_start(out=ug[i][:, :], in_=bc_ap(gs_u, p * S + t0, 64, GT)), reads=[GS], writes=[ug_b[i]], dma=True)
            P.op("sp", lambda e: e.dma_start(out=ktg[i][:, :, :], in_=ktm[p, t0:t0 + GT, :].rearrange("(n i) d -> i n d", i=L)),
                 writes=[ktg_b[i]], dma=True)
            P.op("sp", lambda e: e.dma_start(out=vg[i][:, :, :], in_=vtm[p, t0:t0 + GT, :].rearrange("(n i) d -> i n d", i=L)),
                 writes=[vg_b[i]], dma=True)
            P.op("sp", lambda e: e.dma_start(out=og[i][:, :, :], in_=otm[p, t0:t0 + GT, :].rearrange("(n i) d -> i n d", i=L)),
                 writes=[og_b[i]], dma=True)

        def prep_group(p, g):
            gg = p * NGR + g
            i = gg % 2
            P.op("act", lambda e: e.copy(out=qb[i][:, :, :], in_=qg[i][:, :, :]), reads=[qg_b[i]], writes=[qb_b[i]])
            P.op("pool", lambda e: e.tensor_copy(out=kb[i][:, :, :], in_=kg[i][:, :, :]), reads=[kg_b[i]], writes=[kb_b[i]])
            for k in range(2):
                P.op("dve", lambda e, k=k: e.tensor_tensor(out=qtb[i][:, k, :], in0=qg[i][:, k, :], in1=Ag[i][:, :], op=ALU.mult),
                     reads=[qg_b[i], Ag_b[i]], writes=[qtb_b[i]])
            P.op("pool", lambda e: e.tensor_copy(out=vb[i][:, :, :], in_=vg[i][:, :, :]), reads=[vg_b[i]], writes=[vb_b[i]])
            for n in range(GC):
                c = g * GC + n
                P.op("act", lambda e, n=n, c=c: e.activation(out=kwb[i][:, n, :], in_=ktg[i][:, n, :], func=AF.Identity,
                                                             scale=wik_col[:, p, c:c + 1]),
                     reads=[ktg_b[i], G], writes=[kwb_b[i]])
            P.op("act", lambda e: e.activation(out=sg[i][:, :, :], in_=og[i][:, :, :], func=AF.Sigmoid), reads=[og_b[i]], writes=[sg_b[i]])

        def stage_S(p, c, n_glob):
            g, n = divmod(c, GC)
            i = (p * NGR + g) % 2
            j = n_glob % 2
            j0 = n * L
            for k in range(2):
                P.op("pe", lambda e, k=k: e.matmul(ps_s[j][:, :], kb[i][:, k, j0:j0 + L], qb[i][:, k, j0:j0 + L], start=(k == 0), stop=(k == 1)),
                     reads=[kb_b[i], qb_b[i]], writes=[ps_s_b[j]])
            P.op("dve", lambda e: e.scalar_tensor_tensor(out=X[j][:, :], in0=ug[i][:, j0:j0 + L], scalar=w_col[:, p, c:c + 1], in1=nm[:, :],
                                                         op0=ALU.add, op1=ALU.add),
                 reads=[ug_b[i], G, nm_b], writes=[X_b[j]])
            P.op("act", lambda e: e.activation(out=E[j][:, :], in_=X[j][:, :], func=AF.Exp), reads=[X_b[j]], writes=[E_b[j]])
            P.op("dve", lambda e: e.scalar_tensor_tensor(out=sDT[j][:, :], in0=ps_s[j][:, :], scalar=KSC, in1=E[j][:, :],
                                                         op0=ALU.mult, op1=ALU.mult),
                 reads=[ps_s_b[j], E_b[j]], writes=[sDT_b[j]])

        def stage_H(p, c, n_glob):
            g, n = divmod(c, GC)
            i = (p * NGR + g) % 2
            j = n_glob % 2
            cb = c % 2
            j0 = n * L
            for k in range(2):
                P.op("pe", lambda e, k=k: e.matmul(ps_num[j][:, :], qtb[i][:, k, j0:j0 + L], Cbf[cb][:, k, :], start=(k == 0), stop=False),
                     reads=[qtb_b[i], Cbf_b[cb]], writes=[ps_num_b[j]])
            P.op("pe", lambda e: e.matmul(ps_num[j][:, :], sDT[j][:, :], vb[i][:, n, :], start=False, stop=True),
                 reads=[sDT_b[j], vb_b[i]], writes=[ps_num_b[j]])
            for k in range(2):
                P.op("pe", lambda e, k=k: e.matmul(ps_sm[0:64, j:j + 1], qtb[i][:, k, j0:j0 + L], nbf[cb][:, k:k + 1], start=(k == 0), stop=False),
                     reads=[qtb_b[i], nbf_b[cb]], writes=[ps_den_b[j]])
            P.op("pe", lambda e: e.matmul(ps_sm[0:64, j:j + 1], sDT[j][:, :], onec[:, :], start=False, stop=True),
                 reads=[sDT_b[j], onec_b], writes=[ps_den_b[j]])
            for k in range(2):
                P.op("pe", lambda e, k=k: e.matmul(ps_c[k][:, :], kwb[i][:, n, k * 128:(k + 1) * 128], vb[i][:, n, :], start=True, stop=True),
                     reads=[kwb_b[i], vb_b[i]], writes=[ps_c_b[k]])
                P.op("pe", lambda e, k=k: e.matmul(ps_sm[:, 2 + k:3 + k], kwb[i][:, n, k * 128:(k + 1) * 128], onec[:, :], start=True, stop=True),
                     reads=[kwb_b[i], onec_b], writes=[ps_n_b[k]])
            nb = 1 - cb
            for k in range(2):
                P.op("dve", lambda e, k=k: e.scalar_tensor_tensor(out=Cst[:, k, :], in0=Cst[:, k, :], scalar=ao_bc[:, c:c + 1], in1=ps_c[k][:, :],
                                                                  op0=ALU.mult, op1=ALU.add),
                     reads=[Cst_b, ao_bc_b, ps_c_b[k]], writes=[Cst_b])
                P.op("dve", lambda e, k=k: e.scalar_tensor_tensor(out=nst[:, k:k + 1], in0=nst[:, k:k + 1], scalar=ao_bc[:, c:c + 1],
                                                                  in1=ps_sm[:, 2 + k:3 + k], op0=ALU.mult, op1=ALU.add),
                     reads=[nst_b, ao_bc_b, ps_n_b[k]], writes=[nst_b])
            P.op("act", lambda e: e.copy(out=Cbf[nb][:, 0, :], in_=Cst[:, 0, :]), reads=[Cst_b], writes=[Cbf_b[nb]])
            P.op("pool", lambda e: e.tensor_copy(out=Cbf[nb][:, 1, :], in_=Cst[:, 1, :]), reads=[Cst_b], writes=[Cbf_b[nb]])
            P.op("pool", lambda e: e.tensor_copy(out=nbf[nb][:, :], in_=nst[:, :]), reads=[nst_b], writes=[nbf_b[nb]])
            P.op("dve", lambda e: e.tensor_copy(out=dn[j][:, :], in_=ps_sm[0:64, j:j + 1]),
                 reads=[ps_den_b[j]], writes=[dn_b[j]])
            P.op("dve", lambda e: e.scalar_tensor_tensor(out=dn[j][:, :], in0=dn[j][:, :], scalar=-1.0, in1=dn[j][:, :],
                                                         op0=ALU.mult, op1=ALU.max),
                 reads=[dn_b[j]], writes=[dn_b[j]])
            P.op("dve", lambda e: e.tensor_tensor(out=dn[j][:, :], in0=dn[j][:, :], in1=enm_col[:, p, c:c + 1], op=ALU.max),
                 reads=[dn_b[j], G], writes=[dn_b[j]])
            P.op("dve", lambda e: e.reciprocal(out=dn[j][:, :], in_=dn[j][:, :]), reads=[dn_b[j]], writes=[dn_b[j]])
            P.op("dve", lambda e: e.tensor_scalar(out=hraw[j][:, :], in0=ps_num[j][:, :], scalar1=dn[j][:, 0:1], scalar2=None, op0=ALU.mult),
                 reads=[ps_num_b[j], dn_b[j]], writes=[hraw_b[j]])
            P.op("act", lambda e: e.activation(out=junk[j][:, :], in_=hraw[j][:, :], func=AF.Square, accum_out=ss[j][:, :]),
                 reads=[hraw_b[j]], writes=[junk_b[j], ss_b[j]])
            P.op("act", lambda e: e.activation(out=ss[j][:, :], in_=ss[j][:, :], func=AF.Sqrt, bias=epsc[0:64, 0:1], scale=1.0 / DV),
                 reads=[ss_b[j], epsc_b], writes=[ss_b[j]])
            P.op("dve", lambda e: e.reciprocal(out=ss[j][:, :], in_=ss[j][:, :]), reads=[ss_b[j]], writes=[ss_b[j]])
            P.op("dve", lambda e: e.scalar_tensor_tensor(out=hn_[j][:, :], in0=hraw[j][:, :], scalar=ss[j][:, 0:1], in1=gn[:, p * DV:(p + 1) * DV],
                                                         op0=ALU.mult, op1=ALU.mult),
                 reads=[hraw_b[j], ss_b[j], gn_b], writes=[hn_b[j]])
            P.op("pool", lambda e: e.tensor_tensor(out=hog[i][:, n, :], in0=hn_[j][:, :], in1=sg[i][:, n, :], op=ALU.mult),
                 reads=[hn_b[j], sg_b[i]], writes=[hog_b[i]])

        def store_group(p, g):
            i = (p * NGR + g) % 2
            t0 = g * GT
            for n in range(GC):
                for fc in range(4):
                    P.op("pe", lambda e, n=n, fc=fc: e.transpose(psT[:, fc, n * L:(n + 1) * L], hog[i][:, n, fc * 128:(fc + 1) * 128], idb[:, :]),
                         reads=[hog_b[i], idb_b], writes=[psT_b])
            P.op("act", lambda e: e.copy(out=hTs[i][:, :, :], in_=psT[:, :, :]), reads=[psT_b], writes=[hTs_b[i]])
            piece, c0 = divmod(t0, TT)
            P.op("pool", lambda e: e.dma_start(out=srcH.ap()[piece, p * DV:(p + 1) * DV, c0:c0 + GT].rearrange("(f q) t -> q f t", q=128),
                                               in_=hTs[i][:, :, :]),
                 reads=[hTs_b[i]], writes=[srcH_b[piece]], dma=True)
            if p == NP - 1 and (t0 + GT) % TT == 0:
                P.op("pool", lambda e: e.collective_compute("AllGather", ALU.bypass, replica_groups=GROUPS,
                                                            ins=[srcH.ap()[piece]], outs=[dstH.ap()[(piece + 1) * KCD * 128:(piece + 2) * KCD * 128, :]]),
                     reads=[srcH_b[piece]], writes=[dstH_b], dma=True, inc=1)

        n_glob = 0
        seq = [(p, c) for p in range(NP) for c in range(NCH)]
        load_group(0, 0)
        for p in range(NP):
            P.op("sp", lambda e, p=p: e.dma_start(out=ao_bc[:, :], in_=bc_ap(gs_ao, p * NCH, 128, NCH)), reads=[GS], writes=[ao_bc_b], dma=True)
            P.op("dve", lambda e: e.memset(Cst[:, :, :], 0.0), writes=[Cst_b])
            P.op("dve", lambda e: e.memset(nst[:, :], 0.0), writes=[nst_b])
            P.op("pool", lambda e: e.memset(Cbf[0][:, :, :], 0.0), writes=[Cbf_b[0]])
            P.op("pool", lambda e: e.memset(nbf[0][:, :], 0.0), writes=[nbf_b[0]])
            for g in range(NGR):
                if g + 1 < NGR:
                    load_group(p, g + 1)
                elif p + 1 < NP:
                    load_group(p + 1, 0)
                prep_group(p, g)
                for n in range(GC):
                    c = g * GC + n
                    stage_S(p, c, n_glob)
                    stage_H(p, c, n_glob)
                    n_glob += 1
                store_group(p, g)
        P.end_phase()


def emit_moba(nc, P, cfg, io):
    S, NP, TT, KCD = cfg["S"], cfg["NP"], cfg["TT"], cfg["KC"]
    DH, BS = 128, 256
    NB = S // BS
    NQT = S // 128
    GW = max(NB, 8)
    SC = DH ** -0.5
    BIG = 30000.0
    PIECE = min(2048, S)
    NPC = S // PIECE
    OG = min(4, TT // 128)
    qT, kT, vtm, cmask, ident = (io[k] for k in ("qT", "kT", "vtm", "cmask", "ident"))
    srcA, dstA = io["srcA"], io["dstA"]

    with ExitStack() as st:

        def sb(name, shape, dt=F32):
            return st.enter_context(nc.sbuf_tensor("mo_" + name, list(shape), dt))

        def ps(name, shape, dt=F32):
            return st.enter_context(nc.psum_tensor("mo_" + name, list(shape), dt))

        idf = sb("idf", (128, 128)); idf_b = P.buf("idf")
        idb = sb("idb", (128, 128), BF16); idb_b = P.buf("idb")
        cm = sb("cm", (128, 2 * BS)); cm_b = P.buf("cm")
        P.op("sp", lambda e: e.dma_start(out=idf[:, :], in_=ident.ap()), writes=[idf_b], dma=True)
        P.op("sp", lambda e: e.dma_start(out=cm[:, :], in_=cmask.ap()), writes=[cm_b], dma=True)
        P.op("act", lambda e: e.copy(out=idb[:, :], in_=idf[:, :]), reads=[idf_b], writes=[idb_b])
        zt = sb("zt", (128, KCD * 2), BF16); zt_b = P.buf("zt")
        P.op("pool", lambda e: e.memset(zt[:, :], 0.0), writes=[zt_b])
        dstA_b = P.buf("dstA")
        srcA_b = P.bufs(S // TT, "srcA")
        P.op("sp", lambda e: e.dma_start(out=dstA.ap()[0:KCD * 128, TT - 2:TT].rearrange("(k p) c -> p k c", p=128),
                                         in_=zt[:, :].rearrange("p (k c) -> p k c", c=2)),
             reads=[zt_b], writes=[dstA_b], dma=True)

        stage = [sb(f"stage{i}", (128, PIECE)) for i in range(2)]; stage_b = P.bufs(2, "stage")
        qb = sb("qb", (128, S), BF16); qb_b = P.buf("qb")
        kb = sb("kb", (128, S), BF16); kb_b = P.buf("kb")
        vb = sb("vb", (128, NQT, DH), BF16); vb_b = P.buf("vb")
        ksum = sb("ksum", (128, NB)); km = sb("km", (128, NB)); km_b = P.buf("km")
        gate_all = sb("gate_all", (128, NQT, NB)); gate_b = P.buf("gate")
        gsel = [sb(f"gsel{i}", (128, GW)) for i in range(2)]; gsel_b = P.bufs(2, "gsel")
        mx8 = [sb(f"mx8{i}", (128, 8)) for i in range(2)]; mx8_b = P.bufs(2, "mx8")
        mb = [sb(f"mb{i}", (128, GW)) for i in range(2)]; mb_b = P.bufs(2, "mb")
        bm = [sb(f"bm{i}", (128, NB + 1)) for i in range(2)]; bm_b = P.bufs(2, "bm")
        mrow = [sb(f"mrow{i}", (128, 1)) for i in range(2)]; mrow_b = P.bufs(2, "mrow")
        rsum = [sb(f"rsum{i}", (128, 1)) for i in range(2)]; rsum_b = P.bufs(2, "rsum")
        Sb = [sb(f"Sb{i}", (128, S)) for i in range(2)]; Sb_b = P.bufs(2, "Sb")
        Pb = [sb(f"Pb{i}", (128, S), BF16) for i in range(2)]; Pb_b = P.bufs(2, "Pb")
        PT = [sb(f"PT{i}", (128, 8 * 128), BF16) for i in range(2)]; PT_b = P.bufs(2, "PT")
        ost = [sb(f"ost{i}", (128, OG, DH), BF16) for i in range(2)]; ost_b = P.bufs(2, "ost")
        oTs = [sb(f"oTs{i}", (128, OG * 128), BF16) for i in range(2)]; oTs_b = P.bufs(2, "oTs")

        ps_S = [ps(f"ps_S{i}", (128, 512)) for i in range(2)]; ps_S_b = P.bufs(2, "psS")
        ps_T = [ps(f"ps_T{i}", (128, 1024), BF16) for i in range(2)]; ps_T_b = P.bufs(2, "psT")
        ps_o = [ps(f"ps_o{i}", (128, DH)) for i in range(2)]; ps_o_b = P.bufs(2, "pso")
        ps_g = ps("ps_g", (128, GW)); ps_g_b = P.buf("psg")
        psO = ps("psO", (128, OG * 128), BF16); psO_b = P.buf("psO")

        cnt = {"st": 0, "S": 0, "T": 0}

        def next_stage():
            i = cnt["st"] % 2
            cnt["st"] += 1
            return i

        def prologue(p):
            for pc in range(NPC):
                i = next_stage()
                t0 = pc * PIECE
                P.op("sp", lambda e, i=i, t0=t0: e.dma_start(out=stage[i][:, :], in_=kT[p, :, t0:t0 + PIECE]), writes=[stage_b[i]], dma=True)
                P.op("act", lambda e, i=i, t0=t0: e.copy(out=kb[:, t0:t0 + PIECE], in_=stage[i][:, :]), reads=[stage_b[i]], writes=[kb_b])
                nb0 = t0 // BS
                nbp = PIECE // BS
                P.op("dve", lambda e, i=i, nb0=nb0, nbp=nbp: e.tensor_reduce(
                    out=ksum[:, nb0:nb0 + nbp], in_=stage[i][:, :].rearrange("d (n t) -> d n t", t=BS), axis=AX.X, op=ALU.add),
                    reads=[stage_b[i]], writes=[km_b])
            P.op("dve", lambda e: e.tensor_scalar(out=km[:, :], in0=ksum[:, :], scalar1=1.0 / BS, scalar2=None, op0=ALU.mult),
                 reads=[km_b], writes=[km_b])
            for pc in range(NPC):
                i = next_stage()
                t0 = pc * PIECE
                P.op("sp", lambda e, i=i, t0=t0: e.dma_start(out=stage[i][:, :].rearrange("i (n e) -> i n e", e=DH),
                                                             in_=vtm.ap()[t0:t0 + PIECE, p * DH:(p + 1) * DH].rearrange("(n i) e -> i n e", i=128)),
                     writes=[stage_b[i]], dma=True)
                c0 = t0 // 128
                P.op("pool", lambda e, i=i, c0=c0: e.tensor_copy(out=vb[:, c0:c0 + PIECE // 128, :],
                                                                 in_=stage[i][:, :].rearrange("i (n e) -> i n e", e=DH)),
                     reads=[stage_b[i]], writes=[vb_b])
            for pc in range(NPC):
                i = next_stage()
                t0 = pc * PIECE
                P.op("sp", lambda e, i=i, t0=t0: e.dma_start(out=stage[i][:, :], in_=qT[p, :, t0:t0 + PIECE]), writes=[stage_b[i]], dma=True)
                P.op("act", lambda e, i=i, t0=t0: e.copy(out=qb[:, t0:t0 + PIECE], in_=stage[i][:, :]), reads=[stage_b[i]], writes=[qb_b])
                for tl in range(PIECE // 128):
                    qt = t0 // 128 + tl
                    P.op("pe", lambda e, i=i, tl=tl: e.matmul(ps_g[:, 0:NB], stage[i][:, tl * 128:(tl + 1) * 128], km[:, :], start=True, stop=True),
                         reads=[stage_b[i], km_b], writes=[ps_g_b])
                    P.op("act", lambda e, qt=qt: e.copy(out=gate_all[:, qt, :], in_=ps_g[:, 0:NB]), reads=[ps_g_b], writes=[gate_b])

        def front(p, qt):
            bi, o = divmod(qt, 2)
            s = qt % 2
            nblk = bi + 1
            P.op("pool", lambda e: e.memset(gsel[s][:, :], -1e30), writes=[gsel_b[s]])
            if bi > 0:
                P.op("pool", lambda e: e.tensor_copy(out=gsel[s][:, 0:bi], in_=gate_all[:, qt, 0:bi]), reads=[gate_b], writes=[gsel_b[s]])
            P.op("dve", lambda e: e.max(out=mx8[s][:, :], in_=gsel[s][:, :]), reads=[gsel_b[s]], writes=[mx8_b[s]])
            P.op("dve", lambda e: e.tensor_scalar(out=mb[s][:, :], in0=gsel[s][:, :], scalar1=mx8[s][:, 2:3], scalar2=-BIG,
                                                  op0=ALU.is_lt, op1=ALU.mult),
                 reads=[gsel_b[s], mx8_b[s]], writes=[mb_b[s]])
            for n0 in range(0, nblk, 2):
                k = cnt["S"] % 2
                cnt["S"] += 1
                nbl = min(2, nblk - n0)
                ncols = nbl * BS
                P.op("pe", lambda e, k=k, n0=n0, ncols=ncols: e.matmul(ps_S[k][:, 0:ncols], qb[:, qt * 128:(qt + 1) * 128],
                                                                       kb[:, n0 * BS:n0 * BS + ncols], start=True, stop=True),
                     reads=[qb_b, kb_b], writes=[ps_S_b[k]])
                for h in range(nbl):
                    n = n0 + h
                    if n < bi:
                        P.op("dve", lambda e, k=k, n=n, h=h: e.tensor_scalar(
                            out=Sb[s][:, n * BS:(n + 1) * BS], in0=ps_S[k][:, h * BS:(h + 1) * BS], scalar1=mb[s][:, n:n + 1], scalar2=None,
                            op0=ALU.add, op1=ALU.max, accum_out=bm[s][:, n:n + 1]),
                            reads=[ps_S_b[k], mb_b[s]], writes=[Sb_b[s], bm_b[s]])
                    else:
                        P.op("dve", lambda e, k=k, n=n, h=h: e.tensor_tensor(
                            out=Sb[s][:, n * BS:(n + 1) * BS], in0=ps_S[k][:, h * BS:(h + 1) * BS], in1=cm[:, o * BS:(o + 1) * BS], op=ALU.add),
                            reads=[ps_S_b[k], cm_b], writes=[Sb_b[s]])
                        P.op("dve", lambda e, n=n: e.reduce_max(out=bm[s][:, n:n + 1], in_=Sb[s][:, n * BS:(n + 1) * BS], axis=AX.X),
                             reads=[Sb_b[s]], writes=[bm_b[s]])
            P.op("dve", lambda e: e.reduce_max(out=mrow[s][:, :], in_=bm[s][:, 0:nblk], axis=AX.X), reads=[bm_b[s]], writes=[mrow_b[s]])
            P.op("dve", lambda e: e.tensor_scalar(out=mrow[s][:, :], in0=mrow[s][:, :], scalar1=-SC, scalar2=None, op0=ALU.mult),
                 reads=[mrow_b[s]], writes=[mrow_b[s]])
            nk = nblk * BS
            P.op("act", lambda e: e.activation(out=Pb[s][:, 0:nk], in_=Sb[s][:, 0:nk], func=AF.Exp, bias=mrow[s][:, 0:1], scale=SC,
                                               accum_out=rsum[s][:, :]),
                 reads=[Sb_b[s], mrow_b[s]], writes=[Pb_b[s], rsum_b[s]])

        def back(p, qt):
            bi, o = divmod(qt, 2)
            s = qt % 2
            nch = (bi + 1) * 2
            groups = [(g0, min(8, nch - g0)) for g0 in range(0, nch, 8)]
            oi = qt % 2
            slot = qt % OG
            osl = (qt // OG) % 2

            def emit_T(g0, ng):
                k = cnt["T"] % 2
                cnt["T"] += 1
                for j in range(ng):
                    kc = g0 + j
                    P.op("pe", lambda e, j=j, kc=kc: e.transpose(ps_T[k][:, j * 128:(j + 1) * 128], Pb[s][:, kc * 128:(kc + 1) * 128], idb[:, :]),
                         reads=[Pb_b[s], idb_b], writes=[ps_T_b[k]])
                P.op("act", lambda e: e.copy(out=PT[k][:, 0:ng * 128], in_=ps_T[k][:, 0:ng * 128]), reads=[ps_T_b[k]], writes=[PT_b[k]])
                return k

            def emit_PV(g0, ng, k):
                for j in range(ng):
                    kc = g0 + j
                    P.op("pe", lambda e, j=j, kc=kc: e.matmul(ps_o[oi][:, :], PT[k][:, j * 128:(j + 1) * 128], vb[:, kc, :],
                                                              start=(kc == 0), stop=(kc == nch - 1)),
                         reads=[PT_b[k], vb_b], writes=[ps_o_b[oi]])

            ks = [None] * len(groups)
            ks[0] = emit_T(*groups[0])
            for gi_, (g0, ng) in enumerate(groups):
                if gi_ + 1 < len(groups):
                    ks[gi_ + 1] = emit_T(*groups[gi_ + 1])
                emit_PV(g0, ng, ks[gi_])
            P.op("dve", lambda e: e.reciprocal(out=rsum[s][:, :], in_=rsum[s][:, :]), reads=[rsum_b[s]], writes=[rsum_b[s]])
            P.op("dve", lambda e: e.tensor_scalar(out=ost[osl][:, slot, :], in0=ps_o[oi][:, :], scalar1=rsum[s][:, 0:1], scalar2=None, op0=ALU.mult),
                 reads=[ps_o_b[oi], rsum_b[s]], writes=[ost_b[osl]])
            if slot == OG - 1:
                q0 = (qt - OG + 1) * 128
                for j in range(OG):
                    P.op("pe", lambda e, j=j: e.transpose(psO[:, j * 128:(j + 1) * 128], ost[osl][:, j, :], idb[:, :]),
                         reads=[ost_b[osl], idb_b], writes=[psO_b])
                P.op("act", lambda e: e.copy(out=oTs[osl][:, :], in_=psO[:, :]), reads=[psO_b], writes=[oTs_b[osl]])
                piece, c0 = divmod(q0, TT)
                P.op("pool", lambda e: e.dma_start(out=srcA.ap()[piece, p * DH:(p + 1) * DH, c0:c0 + OG * 128], in_=oTs[osl][:, :]),
                     reads=[oTs_b[osl]], writes=[srcA_b[piece]], dma=True)
                if p == NP - 1 and (q0 + OG * 128) % TT == 0:
                    P.op("pool", lambda e: e.collective_compute("AllGather", ALU.bypass, replica_groups=GROUPS,
                                                                ins=[srcA.ap()[piece]], outs=[dstA.ap()[(piece + 1) * KCD * 128:(piece + 2) * KCD * 128, :]]),
                         reads=[srcA_b[piece]], writes=[dstA_b], dma=True, inc=1)

        for p in range(NP):
            prologue(p)
            front(p, 0)
            for qt in range(NQT):
                if qt + 1 < NQT:
                    front(p, qt + 1)
                back(p, qt)
        P.end_phase()


def build_fused(cfg):
    S, D, F, TT, NG = cfg["S"], cfg["D"], cfg["F"], cfg["TT"], cfg["NG"]
    KC = D // 128
    FC = F // 128
    Q = S // 4
    NTQ = Q // TT
    NPCS = S // TT
    DQ = D // 4
    KQ = KC // 4
    DK, DV, DH = 256, 512, 128
    NP2 = DQ // DV
    NP4 = DQ // DH
    HL = 2
    PH = cfg.get("phases", "ABCDEF")
    nc = bass.Bass("TRN2", target_bir_lowering=False)

    def din(name, shape, dt=F32):
        return nc.dram_tensor(name, list(shape), dt, kind="ExternalInput")

    def dint(name, shape, dt=F32):
        return nc.dram_tensor(name, list(shape), dt, kind="Internal")

    xT_all = din("xT_all", (NPCS, D, TT))
    xq = din("xq", (NTQ, D, TT + HL))
    gA = din("gA", (128, KC))
    NFA = NP2 * 4 + 1
    NGA = NP2 * 3
    wA_fm = din("wA_fm", (NFA, 128, KC, 128))
    wA_tm = din("wA_tm", (NGA, 128, KC, 512))
    gbias = din("gbias", (128, NP2 * 2))
    gain = din("gain", (64, NP2 * DV))
    negmask = din("negmask", (64, 64))
    ident = din("ident", (128, 128))
    cmask = din("cmask", (128, 512))
    w_mix = [din(f"w_mix{i}", (KC, 128, KC, 128)) for i in range(2)]
    ffn = [dict(g=din(f"g_ffn{i}", (128, KC)), w_up=din(f"w_up{i}", (2 * FC, 128, KC, 128)),
                cw=din(f"cw{i}", (128, 3, 2 * FC)), cb=din(f"cb{i}", (128, 2 * FC)),
                w_dn=din(f"w_dn{i}", (KC, 128, FC, 128)), F=F, NG=NG) for i in range(2)]
    gD = din("gD", (128, KC))
    NGD = NP4 * DH // 512
    wD_fm = din("wD_fm", (2 * NP4, 128, KC, 128))
    wD_tm = din("wD_tm", (NGD, 128, KC, 512))
    g_fin = din("g_fin", (128, KC))
    xoT = nc.dram_tensor("xoT", [D, Q], F32, kind="ExternalOutput")

    qT2 = dint("qT2", (NP2, DK, S)); kT2 = dint("kT2", (NP2, DK, S))
    ktm2 = dint("ktm2", (NP2, S, DK)); vtm2 = dint("vtm2", (NP2, S, DV)); otm2 = dint("otm2", (NP2, S, DV))
    gi = dint("gi", (NP2, S)); gf = dint("gf", (NP2, S))
    gs_u = dint("gs_u", (NP2, S)); gs_a = dint("gs_a", (NP2, S)); gs_ao = dint("gs_ao", (NP2, S // 64))
    srcH = dint("srcH", (NPCS, DQ, TT), BF16); dstH = dint("dstH", ((NPCS + 1) * D, TT), BF16)
    x1s = dint("x1s", (D, HL + Q)); srcB = dint("srcB", (D, HL)); dstB = dint("dstB", (5 * D, HL))
    srcX = dint("srcX", (NTQ * 4, DQ, TT), BF16); dstX = dint("dstX", (NTQ * 4, 4 * DQ, TT), BF16)
    qT4 = dint("qT4", (NP4, DH, S)); kT4 = dint("kT4", (NP4, DH, S)); vtm4 = dint("vtm4", (S, NP4 * DH))
    srcA = dint("srcA", (NPCS, DQ, TT), BF16); dstA = dint("dstA", ((NPCS + 1) * D, TT), BF16)

    def w2d(t, i):
        return t[i].rearrange("p k j -> p (k j)")

    def gathered_mix(dst, w):
        return dict(
            w=w,
            main=lambda e, tt: dst.ap()[bass.ds((P.rank(e) * NTQ + 1 + tt) * D, D), :],
            halo=lambda e, tt: dst.ap()[bass.ds((P.rank(e) * NTQ + tt) * D, D), TT - HL:TT])

    with ExitStack() as gst:
        P = Prog(nc, gst)

        fmA = []
        for p in range(NP2):
            for j in range(2):
                fmA.append((w2d(wA_fm, p * 4 + j),
                            lambda tt, ev, p=p, j=j: [(qT2.ap()[p, j * 128:(j + 1) * 128, tt * TT:(tt + 1) * TT], ev[:, 0:TT])]))
            for j in range(2):
                fmA.append((w2d(wA_fm, p * 4 + 2 + j),
                            lambda tt, ev, p=p, j=j: [(kT2.ap()[p, j * 128:(j + 1) * 128, tt * TT:(tt + 1) * TT], ev[:, 0:TT])]))
        fmA.append((w2d(wA_fm, NP2 * 4),
                    lambda tt, ev: [(gi.ap()[0:NP2, tt * TT:(tt + 1) * TT], ev[0:NP2, 0:TT]),
                                    (gf.ap()[0:NP2, tt * TT:(tt + 1) * TT], ev[NP2:2 * NP2, 0:TT])]))
        tmA = []
        for p in range(NP2):
            for j, (dt_, ncols) in enumerate(((ktm2, DK), (vtm2, DV), (otm2, DV))):
                tmA.append((w2d(wA_tm, p * 3 + j), ncols,
                            lambda tt, tb, p=p, dt_=dt_: dt_.ap()[p, tt * TT + tb * 128:tt * TT + (tb + 1) * 128, :]))
        if "A" in PH: emit_dense(nc, P, dict(name="A", NT=NPCS, TT=TT, HL=0, D=D, src="x",
                               x_ap=lambda tt, kc: xT_all[tt, kc * 128:(kc + 1) * 128, :],
                               nxt=dict(g=gA, fm=fmA, tm=tmA)))

        if "B" in PH: emit_mlstm(nc, P, dict(S=S, NP=NP2, TT=TT, KC=KC),
                   dict(qT=qT2, kT=kT2, ktm=ktm2, vtm=vtm2, otm=otm2, gi=gi, gf=gf, gbias=gbias, gain=gain,
                        negmask=negmask, ident=ident, gs_u=gs_u, gs_a=gs_a, gs_ao=gs_ao, srcH=srcH, dstH=dstH, dstB=dstB))

        x1s_b = P.buf("x1s")
        srcB_b = P.buf("srcB")
        dstB_b = P.buf("dstB")
        srcX_b = P.bufs(NTQ * 4, "srcX")
        dstX_b = P.buf("dstX")

        def after_ffn_C(tt, x1, x1_b, HL_, TW):
            for kc in range(KC):
                P.op("sp", lambda e, kc=kc: e.dma_start(out=x1s.ap()[kc * 128:(kc + 1) * 128, HL + tt * TT:HL + (tt + 1) * TT],
                                                        in_=x1[:, kc, HL_:TW]),
                     reads=[x1_b[kc]], writes=[x1s_b], dma=True)
            if tt == NTQ - 1:
                P.op("sp", lambda e: e.dma_start(out=srcB.ap().rearrange("(k p) c -> p k c", p=128), in_=x1[:, :, TW - HL:TW]),
                     reads=x1_b, writes=[srcB_b], dma=True)
                P.op("pool", lambda e: e.collective_compute("AllGather", ALU.bypass, replica_groups=GROUPS,
                                                            ins=[srcB.ap()], outs=[dstB.ap()[D:5 * D, :]]),
                     reads=[srcB_b], writes=[dstB_b], dma=True, inc=1)
                P.op("sp", lambda e: e.dma_start(out=x1s.ap()[:, 0:HL], in_=dstB.ap()[bass.ds(P.rank(e) * D, D), :]),
                     reads=[dstB_b], writes=[x1s_b], dma=True)

        def hn_store_C(tt, hn, hn_b, HL_, TW):
            for fb in range(4):
                c = tt * 4 + fb
                P.op("sp", lambda e, c=c, fb=fb: e.dma_start(out=srcX.ap()[c].rearrange("(k p) t -> p k t", p=128),
                                                             in_=hn[:, fb * KQ:(fb + 1) * KQ, HL_:TW]),
                     reads=hn_b[fb * KQ:(fb + 1) * KQ], writes=[srcX_b[c]], dma=True)
                P.op("pool", lambda e, c=c: e.collective_compute("AllGather", ALU.bypass, replica_groups=GROUPS,
                                                                 ins=[srcX.ap()[c]], outs=[dstX.ap()[c]]),
                     reads=[srcX_b[c]], writes=[dstX_b], dma=True, inc=1)

        if "C" in PH: emit_dense(nc, P, dict(name="C", NT=NTQ, TT=TT, HL=HL, D=D, src="x",
                               x_ap=lambda tt, kc: xq[tt, kc * 128:(kc + 1) * 128, :],
                               mix=gathered_mix(dstH, w_mix[0]), ffn=ffn[0], after_ffn=after_ffn_C, NWB=6,
                               nxt=dict(g=gD, hn_store=hn_store_C)))

        def hn_load_D(tg):
            r, tt = divmod(tg, NTQ)
            return [(fb * KQ, KQ, dstX.ap()[tt * 4 + fb, r * DQ:(r + 1) * DQ, :]) for fb in range(4)]

        fmD = []
        for p in range(NP4):
            fmD.append((w2d(wD_fm, p), lambda tg, ev, p=p: [(qT4.ap()[p, :, tg * TT:(tg + 1) * TT], ev[:, 0:TT])]))
        for p in range(NP4):
            fmD.append((w2d(wD_fm, NP4 + p), lambda tg, ev, p=p: [(kT4.ap()[p, :, tg * TT:(tg + 1) * TT], ev[:, 0:TT])]))
        tmD = [(w2d(wD_tm, g), 512, lambda tg, tb, g=g: vtm4.ap()[tg * TT + tb * 128:tg * TT + (tb + 1) * 128, g * 512:(g + 1) * 512])
               for g in range(NGD)]
        if "D" in PH: emit_dense(nc, P, dict(name="D", NT=NPCS, TT=TT, HL=0, D=D, src="hn", hn_load=hn_load_D, NWB=6,
                               nxt=dict(g=None, fm=fmD, tm=tmD)))

        if "E" in PH: emit_moba(nc, P, dict(S=S, NP=NP4, TT=TT, KC=KC),
                  dict(qT=qT4, kT=kT4, vtm=vtm4, cmask=cmask, ident=ident, srcA=srcA, dstA=dstA))

        if "F" in PH: emit_dense(nc, P, dict(name="F", NT=NTQ, TT=TT, HL=HL, D=D, src="x",
                               x_ap=lambda tt, kc: x1s.ap()[kc * 128:(kc + 1) * 128, tt * TT:(tt + 1) * TT + HL],
                               mix=gathered_mix(dstA, w_mix[1]), ffn=ffn[1], NWB=6,
                               final=dict(g=g_fin, out=lambda tt, kc: xoT.ap()[kc * 128:(kc + 1) * 128, tt * TT:(tt + 1) * TT])))
        if SINGLE_BLOCK:
            P.emit()
        print("fused program built; dma semaphores:", P.ndsem, flush=True)
    return nc


_progs = {}


def _prog(key, cfg):
    if key not in _progs:
        _progs[key] = build_fused(cfg)
    return _progs[key]


def run_fused(cfg, x, norm_mix, norm_ffn, a_w_in, a_gate_bias, a_head_norm, a_w_out,
              b_w_qkv, b_w_out, ffn_w_up, ffn_conv_w, ffn_conv_b, ffn_w_down, final_norm, trace=False):
    f32 = np.float32
    S, D, F, TT = cfg["S"], cfg["D"], cfg["F"], cfg["TT"]
    B = x.shape[0]
    assert B * 4 == N_CORES
    Q = S // 4
    NTQ = Q // TT
    NPCS = S // TT
    DQ = D // 4
    DK, DV, DH = 256, 512, 128
    MLH, MBH = D // DV, D // DH
    NP2, NP4 = DQ // DV, DQ // DH
    HL = 2
    nc = _prog((S, D, F, TT, cfg.get("phases", "ABCDEF")), cfg)

    negmask = np.where(np.arange(64)[None, :] >= np.arange(64)[:, None], 0.0, -30000.0).astype(f32)
    ident = np.eye(128, dtype=f32)
    cmask = np.zeros((128, 512), f32)
    for o in range(2):
        cmask[:, o * 256:(o + 1) * 256] = np.where(np.arange(256)[None, :] <= o * 128 + np.arange(128)[:, None], 0.0, -30000.0)
    shared = dict(gA=lay_vec(norm_mix[0]), gD=lay_vec(norm_mix[1]), g_fin=lay_vec(final_norm),
                  negmask=negmask, ident=ident, cmask=cmask,
                  w_mix0=lay_w(a_w_out[0]), w_mix1=lay_w(b_w_out[0]))
    for i in range(2):
        shared[f"g_ffn{i}"] = lay_vec(norm_ffn[i])
        shared[f"w_up{i}"] = lay_w(ffn_w_up[i])
        shared[f"cw{i}"] = np.ascontiguousarray(np.stack([lay_vec(ffn_conv_w[i, j]) for j in range(3)], axis=1))
        shared[f"cb{i}"] = lay_vec(ffn_conv_b[i])
        shared[f"w_dn{i}"] = lay_w(ffn_w_down[i])

    s1, s2 = MLH * DK, 2 * MLH * DK
    s3 = s2 + MLH * DV
    s4 = s3 + D
    Wa = a_w_in[0]
    Wq = b_w_qkv[0]
    per_rank = []
    for r in range(4):
        heads = [r * NP2 + p for p in range(NP2)]
        cols = []
        for h in heads:
            cols.append(Wa[:, h * DK:(h + 1) * DK])
            cols.append(Wa[:, s1 + h * DK:s1 + (h + 1) * DK])
        gate_cols = np.zeros((D, 128), f32)
        for p, h in enumerate(heads):
            gate_cols[:, p] = Wa[:, s4 + h]
            gate_cols[:, NP2 + p] = Wa[:, s4 + MLH + h]
        cols.append(gate_cols)
        wA_fm = lay_w(np.concatenate(cols, axis=1))
        tcols = []
        for h in heads:
            kpad = np.zeros((D, 512), f32)
            kpad[:, :DK] = Wa[:, s1 + h * DK:s1 + (h + 1) * DK]
            tcols += [kpad, Wa[:, s2 + h * DV:s2 + (h + 1) * DV], Wa[:, s3 + h * DV:s3 + (h + 1) * DV]]
        wA_tm = lay_w_tm(np.concatenate(tcols, axis=1))
        gb = np.array([[a_gate_bias[0, h], a_gate_bias[0, MLH + h]] for h in heads], f32).reshape(1, -1)
        gn = np.concatenate([a_head_norm[0, h * DV:(h + 1) * DV] for h in heads]).reshape(1, -1)
        mh = [r * NP4 + p for p in range(NP4)]
        wD_fm = lay_w(np.concatenate([Wq[:, h * DH:(h + 1) * DH] for h in mh] +
                                     [Wq[:, D + h * DH:D + (h + 1) * DH] for h in mh], axis=1))
        wD_tm = lay_w_tm(np.concatenate([Wq[:, 2 * D + h * DH:2 * D + (h + 1) * DH] for h in mh], axis=1))
        per_rank.append(dict(wA_fm=wA_fm, wA_tm=wA_tm,
                             gbias=np.ascontiguousarray(np.broadcast_to(gb, (128, gb.shape[1]))),
                             gain=np.ascontiguousarray(np.broadcast_to(gn, (64, gn.shape[1]))),
                             wD_fm=wD_fm, wD_tm=wD_tm))
    maps = []
    for b in range(B):
        xTb = np.ascontiguousarray(x[b].T)
        xT_all = np.ascontiguousarray(xTb.reshape(D, NPCS, TT).transpose(1, 0, 2))
        for r in range(4):
            xq = np.zeros((NTQ, D, TT + HL), f32)
            for tt in range(NTQ):
                s0 = r * Q + tt * TT
                if s0 == 0:
                    xq[tt, :, HL:] = xTb[:, 0:TT]
                else:
                    xq[tt] = xTb[:, s0 - HL:s0 + TT]
            maps.append(dict(shared, **per_rank[r], xT_all=xT_all, xq=xq))
        del xTb
    res = run_bass_kernel_spmd(nc, maps, core_ids=list(range(N_CORES)), trace=trace)
    out = np.empty((B, S, D), f32)
    for c in range(N_CORES):
        b, r = divmod(c, 4)
        out[b, r * Q:(r + 1) * Q, :] = res.results[c]["xoT"].T
    return out, res


def kernel(x, norm_mix, norm_ffn, a_w_in, a_gate_bias, a_head_norm, a_w_out,
           b_w_qkv, b_w_out, ffn_w_up, ffn_conv_w, ffn_conv_b, ffn_w_down, final_norm):
    f32 = np.float32
    args = [x, norm_mix, norm_ffn, a_w_in, a_gate_bias, a_head_norm, a_w_out, b_w_qkv, b_w_out,
            ffn_w_up, ffn_conv_w, ffn_conv_b, ffn_w_down, final_norm]
    args = [np.asarray(a, f32) for a in args]
    cfg = dict(S=8192, D=4096, F=11008, TT=512, NG=4)
    out, _ = run_fused(cfg, *args)
    return out
```

```python
import numpy as np
from contextlib import ExitStack
import concourse.bass as bass
import concourse.mybir as mybir
from concourse.bass_utils import run_bass_kernel_spmd

F32 = mybir.dt.float32
BF16 = mybir.dt.bfloat16
AF = mybir.ActivationFunctionType
ALU = mybir.AluOpType
AX = mybir.AxisListType

NORM_EPS = 1e-6
N_CORES = 8
GROUPS = [[0, 1, 2, 3], [4, 5, 6, 7]]
SINGLE_BLOCK = True


class Buf:
    __slots__ = ("name", "last_w", "readers", "dsem", "dcount", "slot")

    def __init__(self, name):
        self.name = name
        self.last_w = None
        self.readers = []
        self.dsem = None
        self.dcount = 0
        self.slot = None


class Op:
    __slots__ = ("eng", "fn", "waits", "signal", "idx", "is_dma", "dsem", "dcount", "count", "inc")

    def __init__(self, eng, fn, idx, is_dma):
        self.eng = eng
        self.fn = fn
        self.idx = idx
        self.is_dma = is_dma
        self.waits = []
        self.signal = False
        self.dsem = None
        self.dcount = 0
        self.count = 0
        self.inc = 16


ENGS = ("pe", "act", "dve", "pool", "sp")


class Prog:
    def __init__(self, nc, stack):
        self.nc = nc
        self.stack = stack
        self.ops = {e: [] for e in ENGS}
        self.esem = {e: stack.enter_context(nc.semaphore("es_" + e)) for e in ENGS}
        self.seen_e = {e: {} for e in ENGS}
        self.seen_d = {e: {} for e in ENGS}
        self.nidx = {e: 0 for e in ENGS}
        self.nsig = {e: 0 for e in ENGS}
        self.pstart = {e: 0 for e in ENGS}
        self.free_slots = []
        self.phase_bufs = []
        self.nbuf = 0
        self.ndsem = 0
        self._rank = None

    def buf(self, name="b"):
        self.nbuf += 1
        b = Buf(f"{name}{self.nbuf}")
        self.phase_bufs.append(b)
        return b

    def bufs(self, n, name="b"):
        return [self.buf(name) for _ in range(n)]

    def _get_dsem(self, b):
        if b.dsem is None:
            if self.free_slots:
                slot = self.free_slots.pop()
            else:
                self.ndsem += 1
                slot = [self.stack.enter_context(self.nc.semaphore(f"ds{self.ndsem}")), 0]
            b.slot = slot
            b.dsem = slot[0]
            b.dcount = slot[1]
        return b.dsem

    def rank(self, engine):
        if self._rank is None:
            self._rank = engine.partition_id() % 4
        return self._rank

    def op(self, eng, fn, reads=(), writes=(), dma=False, sem_buf=None, inc=16):
        lst = self.ops[eng]
        o = Op(eng, fn, self.nidx[eng], dma)
        self.nidx[eng] += 1
        o.inc = inc
        deps = []
        for b in reads:
            if b.last_w is not None:
                deps.append(b.last_w)
        for b in writes:
            if b.last_w is not None:
                deps.append(b.last_w)
            deps.extend(b.readers)
        emax = {}
        dmax = {}
        for d in deps:
            if d.is_dma:
                k = id(d.dsem)
                if k not in dmax or dmax[k][1] < d.dcount:
                    dmax[k] = (d.dsem, d.dcount)
            else:
                if d.eng == eng and not dma:
                    if eng == "pe":
                        continue
                if d.eng not in emax or emax[d.eng].idx < d.idx:
                    emax[d.eng] = d
        se = self.seen_e[eng]
        for e2, d in emax.items():
            if se.get(e2, -1) >= d.idx:
                continue
            se[e2] = d.idx
            d.signal = True
            o.waits.append(("e", d))
        sd = self.seen_d[eng]
        for k, (sem, cnt) in dmax.items():
            if sd.get(k, -1) >= cnt:
                continue
            sd[k] = cnt
            o.waits.append(("d", sem, cnt))
        if dma:
            sb = sem_buf
            if sb is None:
                sb = writes[0] if writes else reads[0]
            o.dsem = self._get_dsem(sb)
            sb.dcount += inc
            o.dcount = sb.dcount
        for b in reads:
            b.readers.append(o)
        for b in writes:
            b.last_w = o
            b.readers = []
        lst.append(o)
        return o

    def barrier(self):
        lasts = {}
        for e in ENGS:
            for p in reversed(self.ops[e][self.pstart[e]:]):
                if not p.is_dma and p.fn is not None:
                    lasts[e] = p
                    p.signal = True
                    break
        dm = {}
        for e in ENGS:
            for p in self.ops[e][self.pstart[e]:]:
                if p.is_dma:
                    k = id(p.dsem)
                    if k not in dm or dm[k][1] < p.dcount:
                        dm[k] = (p.dsem, p.dcount)
        for e in ENGS:
            o = Op(e, None, self.nidx[e], False)
            self.nidx[e] += 1
            for e2, last in lasts.items():
                if self.seen_e[e].get(e2, -1) < last.idx:
                    o.waits.append(("e", last))
                    self.seen_e[e][e2] = last.idx
            for k, (sem, cnt) in dm.items():
                if self.seen_d[e].get(k, -1) < cnt:
                    o.waits.append(("d", sem, cnt))
                    self.seen_d[e][k] = cnt
            self.ops[e].append(o)
        self.pstart = {e: len(self.ops[e]) for e in ENGS}

    def end_phase(self):
        self.barrier()
        print("phase ops", {e: len(self.ops[e]) for e in ENGS}, "nsig(before emit)", dict(self.nsig),
              "max dma sem", max([sl[1] for sl in self.free_slots] + [b.dcount for b in self.phase_bufs] + [0]), flush=True)
        if not SINGLE_BLOCK:
            self.emit()
        for b in self.phase_bufs:
            if b.slot is not None:
                b.slot[1] = b.dcount
                self.free_slots.append(b.slot)
                b.slot = None
        self.phase_bufs = []

    def emit(self):
        nc = self.nc
        for e in ENGS:
            c = self.nsig[e]
            for o in self.ops[e]:
                if o.signal:
                    c += 1
                    o.count = c
            self.nsig[e] = c
        handles = {"pe": "tensor", "act": "scalar", "dve": "vector", "pool": "gpsimd", "sp": "sync"}
        with nc.Block() as block:
            for e in ENGS:
                if not self.ops[e]:
                    continue

                def body(engine, e=e):
                    self._rank = None
                    for o in self.ops[e]:
                        for w in o.waits:
                            if w[0] == "e":
                                engine.wait_ge(self.esem[w[1].eng], w[1].count)
                            else:
                                engine.wait_ge(w[1], w[2])
                        if o.fn is None:
                            continue
                        ins = o.fn(engine)
                        if o.is_dma:
                            ins.then_inc(o.dsem, o.inc)
                        elif o.signal:
                            ins.then_inc(self.esem[e], 1)
                    self._rank = None

                getattr(block, handles[e])(body)
        self.ops = {e: [] for e in ENGS}
        self.pstart = {e: 0 for e in ENGS}


def lay_w(W):
    K, N = W.shape
    NC = -(-N // 128)
    if NC * 128 != N:
        Wp = np.zeros((K, NC * 128), W.dtype)
        Wp[:, :N] = W
        W = Wp
    KC = K // 128
    return np.ascontiguousarray(W.reshape(KC, 128, NC, 128).transpose(2, 1, 0, 3))


def lay_w_tm(W, gcols=512):
    K, N = W.shape
    NG = -(-N // gcols)
    if NG * gcols != N:
        Wp = np.zeros((K, NG * gcols), W.dtype)
        Wp[:, :N] = W
        W = Wp
    KC = K // 128
    return np.ascontiguousarray(W.reshape(KC, 128, NG, gcols).transpose(2, 1, 0, 3))


def lay_vec(v):
    n = v.shape[0]
    NC = -(-n // 128)
    if NC * 128 != n:
        vp = np.zeros((NC * 128,), v.dtype)
        vp[:n] = v
        v = vp
    return np.ascontiguousarray(v.reshape(NC, 128).T)


def _split_last(n, cap=2048):
    for b in range(min(n, cap), 0, -1):
        if n % b == 0:
            return b
    return 1


def emit_dense(nc, P, cfg):
    NT, TT, HL, D = cfg["NT"], cfg["TT"], cfg["HL"], cfg["D"]
    KC = D // 128
    TW = TT + HL
    src = cfg["src"]
    mix, ffn, nxt, final = cfg.get("mix"), cfg.get("ffn"), cfg.get("nxt"), cfg.get("final")
    fm = nxt["fm"] if nxt and nxt.get("fm") else []
    tm = nxt["tm"] if nxt and nxt.get("tm") else []
    has_x = src == "x"
    F = ffn["F"] if ffn else 0
    FC = F // 128
    NG = ffn["NG"] if ffn else 1

    with ExitStack() as st:
        pfx = cfg.get("name", "d") + "_"

        def sb(name, shape, dt):
            return st.enter_context(nc.sbuf_tensor(pfx + name, list(shape), dt))

        def ps(name, shape, dt=F32):
            return st.enter_context(nc.psum_tensor(pfx + name, list(shape), dt))

        if has_x:
            x1 = sb("x1", (128, KC, TW), F32)
            x1_b = P.bufs(KC, "x1_")
            ones = sb("ones", (128, 128), F32)
            ones_b = P.buf("ones")
            sq = [sb(f"sq{i}", (128, TW), F32) for i in range(2)]
            sq_b = P.bufs(2, "sq")
            rstd = sb("rstd", (128, TW), F32)
            rstd_b = P.buf("rstd")
            eps_t = sb("eps_t", (128, 1), F32)
            eps_b = P.buf("eps")
        hn = sb("hn", (128, KC, TW), BF16)
        hn_b = P.bufs(KC, "hn_")
        NWB = cfg.get("NWB", 3)
        if ffn:
            gsz = [FC // NG + (1 if i < FC % NG else 0) for i in range(NG)]
            gst = [sum(gsz[:i]) for i in range(NG)]
            FG = max(gsz)
        KW = max(KC, FG if ffn else 0)
        wt = [sb(f"wt{i}", (128, KW, 128), BF16) for i in range(NWB)]
        wt_b = P.bufs(NWB, "wt")
        NEV = 3
        EVW = max(TT, 512) if tm else TT
        if fm or tm or final:
            ev = [sb(f"ev{i}", (128, EVW), F32) for i in range(NEV)]
            ev_b = P.bufs(NEV, "ev")
        if tm:
            wtm = [sb(f"wtm{i}", (128, KC, 512), BF16) for i in range(2)]
            wtm_b = P.bufs(2, "wtm")
        gvec = {}
        if ffn:
            gt = sb("gt", (128, FG, TT), BF16)
            gt_b = P.bufs(FG, "gt_")
            ucat = [sb(f"uc{i}", (128, TW), F32) for i in range(4)]
            ucat_b = P.bufs(4, "uc")
            cacc = [sb(f"ca{i}", (128, TT), F32) for i in range(4)]
            cacc_b = P.bufs(4, "ca")
            cw_s = sb("cw_s", (128, 3, 2 * FC), F32)
            cb_s = sb("cb_s", (128, 2 * FC), F32)
            cwb = P.buf("cw")
            sv = sb("sv", (128, 2 * FC, 2), F32)
            sv_b = P.bufs(2 * FC, "sv")
            gvec["ffn"] = (sb("g_ffn_s", (128, KC), F32), P.buf("gffn"), ffn["g"])
        if mix:
            hb = hn
            hb_b = hn_b
        if nxt and nxt.get("g") is not None:
            gvec["nxt"] = (sb("g_nxt_s", (128, KC), F32), P.buf("gnxt"), nxt["g"])
        if final:
            gvec["fin"] = (sb("g_fin_s", (128, KC), F32), P.buf("gfin"), final["g"])
        NPS = 6
        acc = [ps(f"acc{i}", (128, 512)) for i in range(NPS)]
        acc_b = P.bufs(NPS, "acc")
        if has_x:
            ssp = ps("ssp", (128, 512))
            ssp_b = P.buf("ssp")
        if HL:
            hal = ps("hal", (128, 512))
            hal_b = P.bufs(8, "hal")

        if has_x:
            P.op("pool", lambda e: e.memset(ones[:, :], 1.0), writes=[ones_b])
            P.op("pool", lambda e: e.memset(eps_t[:, :], NORM_EPS), writes=[eps_b])
        for k, (t, b, d) in gvec.items():
            P.op("sp", lambda e, t=t, d=d: e.dma_start(out=t[:, :], in_=d.ap()), writes=[b], dma=True)
        if ffn:
            P.op("sp", lambda e: e.dma_start(out=cw_s[:, :, :], in_=ffn["cw"].ap()), writes=[cwb], dma=True)
            P.op("sp", lambda e: e.dma_start(out=cb_s[:, :], in_=ffn["cb"].ap()), writes=[cwb], dma=True)

        cnt = {"w": 0, "acc": 0, "ev": 0, "sq": 0, "hal": 0, "uc": 0, "ca": 0, "wtm": 0}

        def load_w(dram_ap_2d, nk):
            i = cnt["w"] % NWB
            cnt["w"] += 1
            n = nk * 128
            bsz = _split_last(n)
            s_ = dram_ap_2d.rearrange("p (a b) -> p a b", b=bsz)
            dst = wt[i][:, 0:nk, :].rearrange("p k j -> p (k j)").rearrange("p (a b) -> p a b", b=bsz)
            P.op("pool", lambda e: e.dma_start(out=dst, in_=s_), writes=[wt_b[i]], dma=True)
            return i

        def load_wtm(dram_ap_2d):
            i = cnt["wtm"] % 2
            cnt["wtm"] += 1
            hk = KC // 2
            for h in range(2):
                s_ = dram_ap_2d[:, h * hk * 512:(h + 1) * hk * 512].rearrange("p (a b) -> p a b", b=512)
                dst = wtm[i][:, h * hk:(h + 1) * hk, :]
                P.op("pool", lambda e, dst=dst, s_=s_: e.dma_start(out=dst, in_=s_), writes=[wtm_b[i]], dma=True)
            return i

        def next_acc():
            i = cnt["acc"] % NPS
            cnt["acc"] += 1
            return i

        def compute_rstd(do_halo=True):
            hs = None
            do_halo = do_halo and HL
            if do_halo:
                hs = cnt["hal"] % 8
                cnt["hal"] += 1
            for kc in range(KC):
                i = cnt["sq"] % 2
                cnt["sq"] += 1
                P.op("act", lambda e, kc=kc, i=i: e.activation(out=sq[i][:, :], in_=x1[:, kc, :], func=AF.Square),
                     reads=[x1_b[kc]], writes=[sq_b[i]])
                P.op("pe", lambda e, kc=kc, i=i: e.matmul(ssp[:, 0:TT], ones[:, :], sq[i][:, HL:TW],
                                                          start=(kc == 0), stop=(kc == KC - 1)),
                     reads=[ones_b, sq_b[i]], writes=[ssp_b])
                if do_halo:
                    P.op("pe", lambda e, kc=kc, i=i, hs=hs: e.matmul(hal[:, hs * HL:(hs + 1) * HL], ones[:, :], sq[i][:, 0:HL],
                                                                     start=(kc == 0), stop=(kc == KC - 1)),
                         reads=[ones_b, sq_b[i]], writes=[hal_b[hs]])
            P.op("act", lambda e: e.activation(out=rstd[:, HL:TW], in_=ssp[:, 0:TT], func=AF.Sqrt,
                                               bias=eps_t[:, 0:1], scale=1.0 / D),
                 reads=[ssp_b, eps_b], writes=[rstd_b])
            if do_halo:
                P.op("act", lambda e, hs=hs: e.activation(out=rstd[:, 0:HL], in_=hal[:, hs * HL:(hs + 1) * HL], func=AF.Sqrt,
                                                          bias=eps_t[:, 0:1], scale=1.0 / D),
                     reads=[hal_b[hs], eps_b], writes=[rstd_b])
            P.op("dve", lambda e: e.reciprocal(out=rstd[:, :], in_=rstd[:, :]), reads=[rstd_b], writes=[rstd_b])

        def rmsnorm_to_hn(gkey, do_halo=True):
            gt_, gb_, _ = gvec[gkey]
            compute_rstd(do_halo)
            for kc in range(KC):
                P.op("dve", lambda e, kc=kc: e.scalar_tensor_tensor(
                    out=hn[:, kc, :], in0=x1[:, kc, :], scalar=gt_[:, kc:kc + 1], in1=rstd[:, :],
                    op0=ALU.mult, op1=ALU.mult),
                    reads=[x1_b[kc], gb_, rstd_b], writes=[hn_b[kc]])

        def gemm(wslot, nk, rhs_fn, rhs_bufs, acc_i, halo_slot=None, halo_rhs_fn=None):
            for k in range(nk):
                P.op("pe", lambda e, k=k: e.matmul(acc[acc_i][:, 0:TT], wt[wslot][:, k, :], rhs_fn(k),
                                                   start=(k == 0), stop=(k == nk - 1)),
                     reads=[wt_b[wslot], rhs_bufs[k]], writes=[acc_b[acc_i]])
            if halo_slot is not None:
                for k in range(nk):
                    P.op("pe", lambda e, k=k: e.matmul(hal[:, halo_slot * HL:(halo_slot + 1) * HL],
                                                       wt[wslot][:, k, :], halo_rhs_fn(k),
                                                       start=(k == 0), stop=(k == nk - 1)),
                         reads=[wt_b[wslot], rhs_bufs[k]], writes=[hal_b[halo_slot]])

        for tt in range(NT):
            first = tt == 0
            if has_x:
                for kc in range(KC):
                    P.op("sp", lambda e, kc=kc, tt=tt: e.dma_start(out=x1[:, kc, :], in_=cfg["x_ap"](tt, kc)),
                         writes=[x1_b[kc]], dma=True)
            else:
                for (k0, nk, ap) in cfg["hn_load"](tt):
                    P.op("sp", lambda e, k0=k0, nk=nk, ap=ap: e.dma_start(
                        out=hn[:, k0:k0 + nk, :], in_=ap.rearrange("(k p) t -> p k t", p=128)),
                        writes=hn_b[k0:k0 + nk], dma=True)
            if mix:
                P.op("sp", lambda e, tt=tt: e.dma_start(out=hb[:, :, HL:TW],
                                                        in_=mix["main"](e, tt).rearrange("(k p) t -> p k t", p=128)),
                     writes=hb_b, dma=True)
                P.op("sp", lambda e, tt=tt: e.dma_start(out=hb[:, :, 0:HL],
                                                        in_=mix["halo"](e, tt).rearrange("(k p) t -> p k t", p=128)),
                     writes=hb_b, dma=True)
                for dc in range(KC):
                    ws = load_w(mix["w"][dc].rearrange("p k j -> p (k j)"), KC)
                    ai = next_acc()
                    hs = None
                    if HL and first:
                        hs = cnt["hal"] % 8
                        cnt["hal"] += 1
                    gemm(ws, KC, lambda k: hb[:, k, HL:TW], hb_b, ai, hs, lambda k: hb[:, k, 0:HL])
                    P.op("dve", lambda e, dc=dc, ai=ai: e.tensor_add(out=x1[:, dc, HL:TW], in0=x1[:, dc, HL:TW], in1=acc[ai][:, 0:TT]),
                         reads=[acc_b[ai], x1_b[dc]], writes=[x1_b[dc]])
                    if HL and first:
                        P.op("dve", lambda e, dc=dc, hs=hs: e.tensor_add(out=x1[:, dc, 0:HL], in0=x1[:, dc, 0:HL],
                                                                        in1=hal[:, hs * HL:(hs + 1) * HL]),
                             reads=[hal_b[hs], x1_b[dc]], writes=[x1_b[dc]])
            if ffn:
                w_up, w_dn = ffn["w_up"], ffn["w_dn"]
                rmsnorm_to_hn("ffn", first)
                for g in range(NG):
                    for fl in range(gsz[g]):
                        fc = gst[g] + fl
                        res = []
                        for half in range(2):
                            col = fc + half * FC
                            ws = load_w(w_up[col].rearrange("p k j -> p (k j)"), KC)
                            ai = next_acc()
                            hs = None
                            if first:
                                hs = cnt["hal"] % 8
                                cnt["hal"] += 1
                            gemm(ws, KC, lambda k: hn[:, k, HL:TW], hn_b, ai, hs, lambda k: hn[:, k, 0:HL])
                            ui = cnt["uc"] % 4
                            cnt["uc"] += 1
                            ci = cnt["ca"] % 4
                            cnt["ca"] += 1
                            P.op("act", lambda e, ui=ui, ai=ai: e.copy(out=ucat[ui][:, HL:TW], in_=acc[ai][:, 0:TT]),
                                 reads=[acc_b[ai]], writes=[ucat_b[ui]])
                            if first:
                                P.op("act", lambda e, ui=ui, hs=hs: e.copy(out=ucat[ui][:, 0:HL], in_=hal[:, hs * HL:(hs + 1) * HL]),
                                     reads=[hal_b[hs]], writes=[ucat_b[ui]])
                            else:
                                P.op("act", lambda e, ui=ui, col=col: e.copy(out=ucat[ui][:, 0:HL], in_=sv[:, col, :]),
                                     reads=[sv_b[col]], writes=[ucat_b[ui]])
                            if tt < NT - 1:
                                P.op("dve", lambda e, ui=ui, col=col: e.tensor_copy(out=sv[:, col, :], in_=ucat[ui][:, TW - HL:TW]),
                                     reads=[ucat_b[ui]], writes=[sv_b[col]])
                            P.op("act", lambda e, ai=ai, ci=ci, col=col: e.activation(
                                out=cacc[ci][:, :], in_=acc[ai][:, 0:TT], func=AF.Identity,
                                scale=cw_s[:, 2, col:col + 1], bias=cb_s[:, col:col + 1]),
                                reads=[acc_b[ai], cwb], writes=[cacc_b[ci]])
                            P.op("dve", lambda e, ui=ui, ci=ci, col=col: e.scalar_tensor_tensor(
                                out=cacc[ci][:, :], in0=ucat[ui][:, 1:TW - 1], scalar=cw_s[:, 1, col:col + 1],
                                in1=cacc[ci][:, :], op0=ALU.mult, op1=ALU.add),
                                reads=[ucat_b[ui], cwb, cacc_b[ci]], writes=[cacc_b[ci]])
                            P.op("dve", lambda e, ui=ui, ci=ci, col=col: e.scalar_tensor_tensor(
                                out=cacc[ci][:, :], in0=ucat[ui][:, 0:TW - 2], scalar=cw_s[:, 0, col:col + 1],
                                in1=cacc[ci][:, :], op0=ALU.mult, op1=ALU.add),
                                reads=[ucat_b[ui], cwb, cacc_b[ci]], writes=[cacc_b[ci]])
                            res.append(ci)
                        cg, cu = res
                        P.op("act", lambda e, cg=cg: e.activation(out=cacc[cg][:, :], in_=cacc[cg][:, :], func=AF.Silu),
                             reads=[cacc_b[cg]], writes=[cacc_b[cg]])
                        P.op("dve", lambda e, cg=cg, cu=cu, fl=fl: e.tensor_tensor(
                            out=gt[:, fl, :], in0=cacc[cg][:, :], in1=cacc[cu][:, :], op=ALU.mult),
                            reads=[cacc_b[cg], cacc_b[cu]], writes=[gt_b[fl]])
                    for dc in range(KC):
                        ws = load_w(w_dn[dc, :, gst[g]:gst[g] + gsz[g], :].rearrange("p k j -> p (k j)"), gsz[g])
                        ai = next_acc()
                        gemm(ws, gsz[g], lambda k: gt[:, k, :], gt_b, ai)
                        P.op("dve", lambda e, dc=dc, ai=ai: e.tensor_add(out=x1[:, dc, HL:TW], in0=x1[:, dc, HL:TW], in1=acc[ai][:, 0:TT]),
                             reads=[acc_b[ai], x1_b[dc]], writes=[x1_b[dc]])
            if cfg.get("after_ffn"):
                cfg["after_ffn"](tt, x1, x1_b, HL, TW)
            if nxt:
                if nxt.get("g") is not None:
                    rmsnorm_to_hn("nxt", False)
                if nxt.get("hn_store"):
                    nxt["hn_store"](tt, hn, hn_b, HL, TW)
                for oc, (w_ap, out_fn) in enumerate(fm):
                    ws = load_w(w_ap, KC)
                    ai = next_acc()
                    gemm(ws, KC, lambda k: hn[:, k, HL:TW], hn_b, ai)
                    i = cnt["ev"] % NEV
                    cnt["ev"] += 1
                    if oc % 2 == 0:
                        P.op("act", lambda e, i=i, ai=ai: e.copy(out=ev[i][:, 0:TT], in_=acc[ai][:, 0:TT]),
                             reads=[acc_b[ai]], writes=[ev_b[i]])
                    else:
                        P.op("dve", lambda e, i=i, ai=ai: e.tensor_copy(out=ev[i][:, 0:TT], in_=acc[ai][:, 0:TT]),
                             reads=[acc_b[ai]], writes=[ev_b[i]])
                    for (dst, s_) in out_fn(tt, ev[i]):
                        P.op("sp", lambda e, dst=dst, s_=s_: e.dma_start(out=dst, in_=s_), reads=[ev_b[i]], dma=True)
                for gi_, (w_ap, ncols, out_fn) in enumerate(tm):
                    wsl = load_wtm(w_ap)
                    for tb in range(TT // 128):
                        ai = next_acc()
                        for k in range(KC):
                            P.op("pe", lambda e, k=k, ai=ai, tb=tb, wsl=wsl, ncols=ncols: e.matmul(
                                acc[ai][:, 0:ncols], hn[:, k, HL + tb * 128:HL + (tb + 1) * 128], wtm[wsl][:, k, 0:ncols],
                                start=(k == 0), stop=(k == KC - 1)),
                                reads=[wtm_b[wsl], hn_b[k]], writes=[acc_b[ai]])
                        i = cnt["ev"] % NEV
                        cnt["ev"] += 1
                        if (gi_ + tb) % 2 == 0:
                            P.op("act", lambda e, i=i, ai=ai, ncols=ncols: e.copy(out=ev[i][:, 0:ncols], in_=acc[ai][:, 0:ncols]),
                                 reads=[acc_b[ai]], writes=[ev_b[i]])
                        else:
                            P.op("dve", lambda e, i=i, ai=ai, ncols=ncols: e.tensor_copy(out=ev[i][:, 0:ncols], in_=acc[ai][:, 0:ncols]),
                                 reads=[acc_b[ai]], writes=[ev_b[i]])
                        dst = out_fn(tt, tb)
                        P.op("sp", lambda e, dst=dst, i=i, ncols=ncols: e.dma_start(out=dst, in_=ev[i][:, 0:ncols]),
                             reads=[ev_b[i]], dma=True)
            if final:
                gt_, gb_, _ = gvec["fin"]
                compute_rstd(False)
                for kc in range(KC):
                    i = cnt["ev"] % NEV
                    cnt["ev"] += 1
                    P.op("dve", lambda e, kc=kc, i=i: e.scalar_tensor_tensor(
                        out=ev[i][:, 0:TT], in0=x1[:, kc, HL:TW], scalar=gt_[:, kc:kc + 1], in1=rstd[:, HL:TW],
                        op0=ALU.mult, op1=ALU.mult),
                        reads=[x1_b[kc], gb_, rstd_b], writes=[ev_b[i]])
                    P.op("sp", lambda e, kc=kc, i=i, tt=tt: e.dma_start(out=final["out"](tt, kc), in_=ev[i][:, 0:TT]),
                         reads=[ev_b[i]], dma=True)
        P.end_phase()
        print("dense phase", cfg.get("name"), "sbuf bytes remaining", nc.sbuf_bytes_remaining, flush=True)

def emit_mlstm(nc, P, cfg, io):
    S, NP, TT, KCD = cfg["S"], cfg["NP"], cfg["TT"], cfg["KC"]
    L, DK, DV, GC = 64, 256, 512, 4
    NCH = S // L
    NGR = NCH // GC
    GT = GC * L
    KSC = DK ** -0.5
    CAP = 15.0
    assert NCH <= 128 and TT % GT == 0
    qT, kT, ktm, vtm, otm, gi, gf = (io[k] for k in ("qT", "kT", "ktm", "vtm", "otm", "gi", "gf"))
    gbias, gain, negmask, ident = (io[k] for k in ("gbias", "gain", "negmask", "ident"))
    gs_u, gs_a, gs_ao = io["gs_u"], io["gs_a"], io["gs_ao"]
    srcH, dstH = io["srcH"], io["dstH"]
    NPC = S // TT

    with ExitStack() as st:

        def sb(name, shape, dt=F32):
            return st.enter_context(nc.sbuf_tensor("ml_" + name, list(shape), dt))

        def ps(name, shape, dt=F32):
            return st.enter_context(nc.psum_tensor("ml_" + name, list(shape), dt))

        idt = sb("idt", (128, 128)); idt_b = P.buf("idt")
        nm = sb("nm", (64, 64)); nm_b = P.buf("nm")
        gb = sb("gb", (128, NP * 2)); gb_b = P.buf("gb")
        gn = sb("gn", (64, NP * DV)); gn_b = P.buf("gn")
        one = sb("one", (128, 64)); one_b = P.buf("one")
        onec = sb("onec", (64, 1), BF16); onec_b = P.buf("onec")
        epsc = sb("epsc", (128, 1)); epsc_b = P.buf("epsc")
        P.op("sp", lambda e: e.dma_start(out=idt[:, :], in_=ident.ap()), writes=[idt_b], dma=True)
        P.op("sp", lambda e: e.dma_start(out=nm[:, :], in_=negmask.ap()), writes=[nm_b], dma=True)
        P.op("sp", lambda e: e.dma_start(out=gb[:, :], in_=gbias.ap()), writes=[gb_b], dma=True)
        P.op("sp", lambda e: e.dma_start(out=gn[:, :], in_=gain.ap()), writes=[gn_b], dma=True)
        P.op("pool", lambda e: e.memset(one[:, :], 1.0), writes=[one_b])
        P.op("pool", lambda e: e.memset(onec[:, :], 1.0), writes=[onec_b])
        P.op("pool", lambda e: e.memset(epsc[:, :], NORM_EPS), writes=[epsc_b])
        idb = sb("idb", (64, 64), BF16); idb_b = P.buf("idb")
        P.op("act", lambda e: e.copy(out=idb[:, :], in_=idt[0:64, 0:64]), reads=[idt_b], writes=[idb_b])
        zt = sb("zt", (128, KCD * 2), BF16); zt_b = P.buf("zt")
        P.op("pool", lambda e: e.memset(zt[:, :], 0.0), writes=[zt_b])
        dstH_b = P.buf("dstH")
        srcH_b = P.bufs(NPC, "srcH")
        P.op("sp", lambda e: e.dma_start(out=dstH.ap()[0:KCD * 128, TT - 2:TT].rearrange("(k p) c -> p k c", p=128),
                                         in_=zt[:, :].rearrange("p (k c) -> p k c", c=2)),
             reads=[zt_b], writes=[dstH_b], dma=True)
        if io.get("dstB") is not None:
            dstB_b0 = P.buf("dstB0")
            P.op("pool", lambda e: e.dma_start(out=io["dstB"].ap()[0:KCD * 128, :].rearrange("(k p) c -> p k c", p=128),
                                               in_=zt[:, :].rearrange("p (k c) -> p k c", c=2)),
                 reads=[zt_b], writes=[dstB_b0], dma=True)

        ps_s = [ps(f"ps_s{i}", (64, 64)) for i in range(2)]; ps_s_b = P.bufs(2, "pss")
        ps_num = [ps(f"ps_num{i}", (64, 512)) for i in range(2)]; ps_num_b = P.bufs(2, "psn")
        ps_c = [ps(f"ps_c{i}", (128, 512)) for i in range(2)]; ps_c_b = P.bufs(2, "psc")
        ps_sm = ps("ps_sm", (128, 512))
        psT = ps("psT", (128, 4, GT), BF16); psT_b = P.buf("psT")
        ps_den_b = P.bufs(2, "psd")
        ps_n_b = P.bufs(2, "psnn")

        gi_t = sb("gi_t", (NCH, NP, L)); gf_t = sb("gf_t", (NCH, NP, L))
        G = P.buf("gates")
        for p in range(NP):
            P.op("sp", lambda e, p=p: e.dma_start(out=gi_t[:, p, :], in_=gi[p].rearrange("(c j) -> c j", j=L)), writes=[G], dma=True)
            P.op("sp", lambda e, p=p: e.dma_start(out=gf_t[:, p, :], in_=gf[p].rearrange("(c j) -> c j", j=L)), writes=[G], dma=True)
        ti = sb("ti", (NCH, NP, L)); tf = sb("tf", (NCH, NP, L))
        e1 = sb("e1", (NCH, NP, L)); lfn = sb("lfn", (NCH, NP, L))
        bb = sb("bb", (NCH, NP, L)); ww = sb("ww", (NCH, NP, L)); cm = sb("cm", (NCH, NP, L))
        uu = sb("uu", (NCH, NP, L)); aa = sb("aa", (NCH, NP, L)); ub = sb("ub", (NCH, NP, L))
        enm = sb("enm", (NCH, NP, L)); wi = sb("wi", (NCH, NP, L))
        bl = sb("bl", (NCH, NP)); md = sb("md", (NCH, NP)); mn = sb("mn", (NCH, NP)); mp = sb("mp", (NCH, NP))
        bmn = sb("bmn", (NCH, NP)); aot = sb("aot", (NCH, NP)); ao = sb("ao", (NCH, NP))
        blT = sb("blT", (NP, NCH)); mdT = sb("mdT", (NP, NCH)); mnT = sb("mnT", (NP, NCH)); mpT = sb("mpT", (NP, NCH))
        w_col = sb("w_col", (64, NP, NCH)); wik_col = sb("wik_col", (64, NP, NCH)); enm_col = sb("enm_col", (64, NP, NCH))

        GS = P.buf("gs_dram")

        def g_op(eng, fn, extra_r=()):
            P.op(eng, fn, reads=[G] + list(extra_r), writes=[G])

        for p in range(NP):
            g_op("dve", lambda e, p=p: e.tensor_scalar(out=ti[:, p, :], in0=gi_t[:, p, :], scalar1=gb[0:NCH, 2 * p:2 * p + 1],
                                                       scalar2=1.0 / CAP, op0=ALU.add, op1=ALU.mult), [gb_b])
            g_op("dve", lambda e, p=p: e.tensor_scalar(out=tf[:, p, :], in0=gf_t[:, p, :], scalar1=gb[0:NCH, 2 * p + 1:2 * p + 2],
                                                       scalar2=1.0 / CAP, op0=ALU.add, op1=ALU.mult), [gb_b])
        g_op("act", lambda e: e.activation(out=ti[:, :, :], in_=ti[:, :, :], func=AF.Tanh))
        g_op("act", lambda e: e.activation(out=tf[:, :, :], in_=tf[:, :, :], func=AF.Tanh))
        g_op("act", lambda e: e.activation(out=e1[:, :, :], in_=tf[:, :, :], func=AF.Exp, scale=-CAP))
        g_op("act", lambda e: e.activation(out=lfn[:, :, :], in_=e1[:, :, :], func=AF.Ln, bias=one[0:NCH, 0:1], scale=1.0), [one_b])
        for p in range(NP):
            g_op("dve", lambda e, p=p: e.tensor_tensor_scan(out=bb[:, p, :], data0=one[0:NCH, 0:L], data1=lfn[:, p, :], initial=0.0,
                                                            op0=ALU.mult, op1=ALU.subtract), [one_b])
            g_op("dve", lambda e, p=p: e.scalar_tensor_tensor(out=ww[:, p, :], in0=ti[:, p, :], scalar=CAP, in1=bb[:, p, :],
                                                              op0=ALU.mult, op1=ALU.subtract))
            g_op("dve", lambda e, p=p: e.tensor_tensor_scan(out=cm[:, p, :], data0=ww[:, p, :], data1=ww[:, p, :], initial=-1e30,
                                                            op0=ALU.max, op1=ALU.max))
            g_op("dve", lambda e, p=p: e.tensor_copy(out=bl[:, p:p + 1], in_=bb[:, p, L - 1:L]))
            g_op("dve", lambda e, p=p: e.tensor_tensor(out=md[:, p:p + 1], in0=bb[:, p, L - 1:L], in1=cm[:, p, L - 1:L], op=ALU.add))
        g_op("pe", lambda e: e.transpose(ps_num[0][0:NP, 0:NCH], bl[:, :], idt[0:NCH, 0:NCH]), [idt_b])
        g_op("dve", lambda e: e.tensor_copy(out=blT[:, :], in_=ps_num[0][0:NP, 0:NCH]))
        g_op("pe", lambda e: e.transpose(ps_num[0][0:NP, 0:NCH], md[:, :], idt[0:NCH, 0:NCH]), [idt_b])
        g_op("dve", lambda e: e.tensor_copy(out=mdT[:, :], in_=ps_num[0][0:NP, 0:NCH]))
        g_op("dve", lambda e: e.tensor_tensor_scan(out=mnT[:, :], data0=blT[:, :], data1=mdT[:, :], initial=0.0,
                                                   op0=ALU.add, op1=ALU.max))
        g_op("dve", lambda e: e.memset(mpT[:, :], 0.0))
        if NCH > 1:
            g_op("dve", lambda e: e.tensor_copy(out=mpT[:, 1:NCH], in_=mnT[:, 0:NCH - 1]))
        g_op("pe", lambda e: e.transpose(ps_c[0][0:NCH, 0:NP], mnT[:, :], idt[0:NP, 0:NP]), [idt_b])
        g_op("dve", lambda e: e.tensor_copy(out=mn[:, :], in_=ps_c[0][0:NCH, 0:NP]))
        g_op("pe", lambda e: e.transpose(ps_c[0][0:NCH, 0:NP], mpT[:, :], idt[0:NP, 0:NP]), [idt_b])
        g_op("dve", lambda e: e.tensor_copy(out=mp[:, :], in_=ps_c[0][0:NCH, 0:NP]))
        g_op("dve", lambda e: e.tensor_tensor(out=bmn[:, :], in0=bl[:, :], in1=mn[:, :], op=ALU.subtract))
        g_op("dve", lambda e: e.tensor_tensor(out=aot[:, :], in0=mp[:, :], in1=bmn[:, :], op=ALU.add))
        g_op("act", lambda e: e.activation(out=ao[:, :], in_=aot[:, :], func=AF.Exp))
        for p in range(NP):
            g_op("dve", lambda e, p=p: e.tensor_scalar(out=uu[:, p, :], in0=cm[:, p, :], scalar1=mp[:, p:p + 1], scalar2=-1.0,
                                                       op0=ALU.max, op1=ALU.mult))
            g_op("act", lambda e, p=p: e.activation(out=aa[:, p, :], in_=uu[:, p, :], func=AF.Exp, bias=mp[:, p:p + 1], scale=1.0))
            g_op("dve", lambda e, p=p: e.tensor_tensor(out=ub[:, p, :], in0=uu[:, p, :], in1=bb[:, p, :], op=ALU.subtract))
            g_op("act", lambda e, p=p: e.activation(out=enm[:, p, :], in_=ub[:, p, :], func=AF.Exp))
            g_op("act", lambda e, p=p: e.activation(out=wi[:, p, :], in_=ww[:, p, :], func=AF.Exp, bias=bmn[:, p:p + 1], scale=1.0))
            for src, dst, scl in ((ww, w_col, 1.0), (wi, wik_col, KSC), (enm, enm_col, 1.0)):
                g_op("pe", lambda e, p=p, src=src: e.transpose(ps_num[0][0:L, 0:NCH], src[:, p, :], idt[0:NCH, 0:NCH]), [idt_b])
                g_op("act", lambda e, p=p, dst=dst, scl=scl: e.mul(out=dst[:, p, :], in_=ps_num[0][0:L, 0:NCH], mul=scl))
            P.op("sp", lambda e, p=p: e.dma_start(out=gs_u.ap()[p].rearrange("(c j) -> c j", j=L), in_=uu[:, p, :]),
                 reads=[G], writes=[GS], dma=True, sem_buf=GS)
            P.op("sp", lambda e, p=p: e.dma_start(out=gs_a.ap()[p].rearrange("(c j) -> c j", j=L), in_=aa[:, p, :]),
                 reads=[G], writes=[GS], dma=True, sem_buf=GS)
            P.op("sp", lambda e, p=p: e.dma_start(out=gs_ao.ap()[p].rearrange("(c o) -> c o", o=1), in_=ao[:, p:p + 1]),
                 reads=[G], writes=[GS], dma=True, sem_buf=GS)

        ao_bc = sb("ao_bc", (128, NCH)); ao_bc_b = P.buf("aobc")
        Cst = sb("Cst", (128, 2, DV)); Cst_b = P.buf("Cst")
        nst = sb("nst", (128, 2)); nst_b = P.buf("nst")
        Cbf = [sb(f"Cbf{i}", (128, 2, DV), BF16) for i in range(2)]; Cbf_b = P.bufs(2, "Cbf")
        nbf = [sb(f"nbf{i}", (128, 2), BF16) for i in range(2)]; nbf_b = P.bufs(2, "nbf")
        qg = [sb(f"qg{i}", (128, 2, GT)) for i in range(2)]; qg_b = P.bufs(2, "qg")
        kg = [sb(f"kg{i}", (128, 2, GT)) for i in range(2)]; kg_b = P.bufs(2, "kg")
        Ag = [sb(f"Ag{i}", (128, GT)) for i in range(2)]; Ag_b = P.bufs(2, "Ag")
        ug = [sb(f"ug{i}", (64, GT)) for i in range(2)]; ug_b = P.bufs(2, "ug")
        ktg = [sb(f"ktg{i}", (64, GC, DK)) for i in range(2)]; ktg_b = P.bufs(2, "ktg")
        vg = [sb(f"vg{i}", (64, GC, DV)) for i in range(2)]; vg_b = P.bufs(2, "vg")
        og = [sb(f"og{i}", (64, GC, DV)) for i in range(2)]; og_b = P.bufs(2, "og")
        qb = [sb(f"qb{i}", (128, 2, GT), BF16) for i in range(2)]; qb_b = P.bufs(2, "qb")
        kb = [sb(f"kb{i}", (128, 2, GT), BF16) for i in range(2)]; kb_b = P.bufs(2, "kb")
        qtb = [sb(f"qtb{i}", (128, 2, GT), BF16) for i in range(2)]; qtb_b = P.bufs(2, "qtb")
        vb = [sb(f"vb{i}", (64, GC, DV), BF16) for i in range(2)]; vb_b = P.bufs(2, "vb")
        kwb = [sb(f"kwb{i}", (64, GC, DK), BF16) for i in range(2)]; kwb_b = P.bufs(2, "kwb")
        sg = [sb(f"sg{i}", (64, GC, DV)) for i in range(2)]; sg_b = P.bufs(2, "sg")
        hog = [sb(f"hog{i}", (64, GC, DV), BF16) for i in range(2)]; hog_b = P.bufs(2, "hog")
        hTs = [sb(f"hTs{i}", (128, 4, GT), BF16) for i in range(2)]; hTs_b = P.bufs(2, "hTs")
        X = [sb(f"X{i}", (64, 64)) for i in range(2)]; X_b = P.bufs(2, "X")
        E = [sb(f"E{i}", (64, 64)) for i in range(2)]; E_b = P.bufs(2, "E")
        sDT = [sb(f"sDT{i}", (64, 64), BF16) for i in range(2)]; sDT_b = P.bufs(2, "sDT")
        dn = [sb(f"dn{i}", (64, 1)) for i in range(2)]; dn_b = P.bufs(2, "dn")
        hraw = [sb(f"hraw{i}", (64, DV)) for i in range(2)]; hraw_b = P.bufs(2, "hraw")
        junk = [sb(f"junk{i}", (64, DV)) for i in range(2)]; junk_b = P.bufs(2, "junk")
        ss = [sb(f"ss{i}", (64, 1)) for i in range(2)]; ss_b = P.bufs(2, "ss")
        hn_ = [sb(f"hn_{i}", (64, DV)) for i in range(2)]; hn_b = P.bufs(2, "hn")

        def bc_ap(t, off, nparts, n):
            return bass.AP(t, off, [[0, nparts], [1, n]])

        def load_group(p, g):
            gg = p * NGR + g
            i = gg % 2
            t0 = g * GT
            P.op("sp", lambda e: e.dma_start(out=qg[i][:, :, :], in_=qT[p, :, t0:t0 + GT].rearrange("(k d) t -> d k t", d=128)),
                 writes=[qg_b[i]], dma=True)
            P.op("sp", lambda e: e.dma_start(out=kg[i][:, :, :], in_=kT[p, :, t0:t0 + GT].rearrange("(k d) t -> d k t", d=128)),
                 writes=[kg_b[i]], dma=True)
            P.op("sp", lambda e: e.dma_start(out=Ag[i][:, :], in_=bc_ap(gs_a, p * S + t0, 128, GT)), reads=[GS], writes=[Ag_b[i]], dma=True)
            P.op("sp", lambda e: e.dma_start(out=ug[i][:, :], in_=bc_ap(gs_u, p * S + t0, 64, GT)), reads=[GS], writes=[ug_b[i]], dma=True)
            P.op("sp", lambda e: e.dma_start(out=ktg[i][:, :, :], in_=ktm[p, t0:t0 + GT, :].rearrange("(n i) d -> i n d", i=L)),
                 writes=[ktg_b[i]], dma=True)
            P.op("sp", lambda e: e.dma_start(out=vg[i][:, :, :], in_=vtm[p, t0:t0 + GT, :].rearrange("(n i) d -> i n d", i=L)),
                 writes=[vg_b[i]], dma=True)
            P.op("sp", lambda e: e.dma_start(out=og[i][:, :, :], in_=otm[p, t0:t0 + GT, :].rearrange("(n i) d -> i n d", i=L)),
                 writes=[og_b[i]], dma=True)

        def prep_group(p, g):
            gg = p * NGR + g
            i = gg % 2
            P.op("act", lambda e: e.copy(out=qb[i][:, :, :], in_=qg[i][:, :, :]), reads=[qg_b[i]], writes=[qb_b[i]])
            P.op("pool", lambda e: e.tensor_copy(out=kb[i][:, :, :], in_=kg[i][:, :, :]), reads=[kg_b[i]], writes=[kb_b[i]])
            for k in range(2):
                P.op("dve", lambda e, k=k: e.tensor_tensor(out=qtb[i][:, k, :], in0=qg[i][:, k, :], in1=Ag[i][:, :], op=ALU.mult),
                     reads=[qg_b[i], Ag_b[i]], writes=[qtb_b[i]])
            P.op("pool", lambda e: e.tensor_copy(out=vb[i][:, :, :], in_=vg[i][:, :, :]), reads=[vg_b[i]], writes=[vb_b[i]])
            for n in range(GC):
                c = g * GC + n
                P.op("act", lambda e, n=n, c=c: e.activation(out=kwb[i][:, n, :], in_=ktg[i][:, n, :], func=AF.Identity,
                                                             scale=wik_col[:, p, c:c + 1]),
                     reads=[ktg_b[i], G], writes=[kwb_b[i]])
            P.op("act", lambda e: e.activation(out=sg[i][:, :, :], in_=og[i][:, :, :], func=AF.Sigmoid), reads=[og_b[i]], writes=[sg_b[i]])

        def stage_S(p, c, n_glob):
            g, n = divmod(c, GC)
            i = (p * NGR + g) % 2
            j = n_glob % 2
            j0 = n * L
            for k in range(2):
                P.op("pe", lambda e, k=k: e.matmul(ps_s[j][:, :], kb[i][:, k, j0:j0 + L], qb[i][:, k, j0:j0 + L], start=(k == 0), stop=(k == 1)),
                     reads=[kb_b[i], qb_b[i]], writes=[ps_s_b[j]])
            P.op("dve", lambda e: e.scalar_tensor_tensor(out=X[j][:, :], in0=ug[i][:, j0:j0 + L], scalar=w_col[:, p, c:c + 1], in1=nm[:, :],
                                                         op0=ALU.add, op1=ALU.add),
                 reads=[ug_b[i], G, nm_b], writes=[X_b[j]])
            P.op("act", lambda e: e.activation(out=E[j][:, :], in_=X[j][:, :], func=AF.Exp), reads=[X_b[j]], writes=[E_b[j]])
            P.op("dve", lambda e: e.scalar_tensor_tensor(out=sDT[j][:, :], in0=ps_s[j][:, :], scalar=KSC, in1=E[j][:, :],
                                                         op0=ALU.mult, op1=ALU.mult),
                 reads=[ps_s_b[j], E_b[j]], writes=[sDT_b[j]])

        def stage_H(p, c, n_glob):
            g, n = divmod(c, GC)
            i = (p * NGR + g) % 2
            j = n_glob % 2
            cb = c % 2
            j0 = n * L
            for k in range(2):
                P.op("pe", lambda e, k=k: e.matmul(ps_num[j][:, :], qtb[i][:, k, j0:j0 + L], Cbf[cb][:, k, :], start=(k == 0), stop=False),
                     reads=[qtb_b[i], Cbf_b[cb]], writes=[ps_num_b[j]])
            P.op("pe", lambda e: e.matmul(ps_num[j][:, :], sDT[j][:, :], vb[i][:, n, :], start=False, stop=True),
                 reads=[sDT_b[j], vb_b[i]], writes=[ps_num_b[j]])
            for k in range(2):
                P.op("pe", lambda e, k=k: e.matmul(ps_sm[0:64, j:j + 1], qtb[i][:, k, j0:j0 + L], nbf[cb][:, k:k + 1], start=(k == 0), stop=False),
                     reads=[qtb_b[i], nbf_b[cb]], writes=[ps_den_b[j]])
            P.op("pe", lambda e: e.matmul(ps_sm[0:64, j:j + 1], sDT[j][:, :], onec[:, :], start=False, stop=True),
                 reads=[sDT_b[j], onec_b], writes=[ps_den_b[j]])
            for k in range(2):
                P.op("pe", lambda e, k=k: e.matmul(ps_c[k][:, :], kwb[i][:, n, k * 128:(k + 1) * 128], vb[i][:, n, :], start=True, stop=True),
                     reads=[kwb_b[i], vb_b[i]], writes=[ps_c_b[k]])
                P.op("pe", lambda e, k=k: e.matmul(ps_sm[:, 2 + k:3 + k], kwb[i][:, n, k * 128:(k + 1) * 128], onec[:, :], start=True, stop=True),
                     reads=[kwb_b[i], onec_b], writes=[ps_n_b[k]])
            nb = 1 - cb
            for k in range(2):
                P.op("dve", lambda e, k=k: e.scalar_tensor_tensor(out=Cst[:, k, :], in0=Cst[:, k, :], scalar=ao_bc[:, c:c + 1], in1=ps_c[k][:, :],
                                                                  op0=ALU.mult, op1=ALU.add),
                     reads=[Cst_b, ao_bc_b, ps_c_b[k]], writes=[Cst_b])
                P.op("dve", lambda e, k=k: e.scalar_tensor_tensor(out=nst[:, k:k + 1], in0=nst[:, k:k + 1], scalar=ao_bc[:, c:c + 1],
                                                                  in1=ps_sm[:, 2 + k:3 + k], op0=ALU.mult, op1=ALU.add),
                     reads=[nst_b, ao_bc_b, ps_n_b[k]], writes=[nst_b])
            P.op("act", lambda e: e.copy(out=Cbf[nb][:, 0, :], in_=Cst[:, 0, :]), reads=[Cst_b], writes=[Cbf_b[nb]])
            P.op("pool", lambda e: e.tensor_copy(out=Cbf[nb][:, 1, :], in_=Cst[:, 1, :]), reads=[Cst_b], writes=[Cbf_b[nb]])
            P.op("pool", lambda e: e.tensor_copy(out=nbf[nb][:, :], in_=nst[:, :]), reads=[nst_b], writes=[nbf_b[nb]])
            P.op("dve", lambda e: e.tensor_copy(out=dn[j][:, :], in_=ps_sm[0:64, j:j + 1]),
                 reads=[ps_den_b[j]], writes=[dn_b[j]])
            P.op("dve", lambda e: e.scalar_tensor_tensor(out=dn[j][:, :], in0=dn[j][:, :], scalar=-1.0, in1=dn[j][:, :],
                                                         op0=ALU.mult, op1=ALU.max),
                 reads=[dn_b[j]], writes=[dn_b[j]])
            P.op("dve", lambda e: e.tensor_tensor(out=dn[j][:, :], in0=dn[j][:, :], in1=enm_col[:, p, c:c + 1], op=ALU.max),
                 reads=[dn_b[j], G], writes=[dn_b[j]])
            P.op("dve", lambda e: e.reciprocal(out=dn[j][:, :], in_=dn[j][:, :]), reads=[dn_b[j]], writes=[dn_b[j]])
            P.op("dve", lambda e: e.tensor_scalar(out=hraw[j][:, :], in0=ps_num[j][:, :], scalar1=dn[j][:, 0:1], scalar2=None, op0=ALU.mult),
                 reads=[ps_num_b[j], dn_b[j]], writes=[hraw_b[j]])
            P.op("act", lambda e: e.activation(out=junk[j][:, :], in_=hraw[j][:, :], func=AF.Square, accum_out=ss[j][:, :]),
                 reads=[hraw_b[j]], writes=[junk_b[j], ss_b[j]])
            P.op("act", lambda e: e.activation(out=ss[j][:, :], in_=ss[j][:, :], func=AF.Sqrt, bias=epsc[0:64, 0:1], scale=1.0 / DV),
                 reads=[ss_b[j], epsc_b], writes=[ss_b[j]])
            P.op("dve", lambda e: e.reciprocal(out=ss[j][:, :], in_=ss[j][:, :]), reads=[ss_b[j]], writes=[ss_b[j]])
            P.op("dve", lambda e: e.scalar_tensor_tensor(out=hn_[j][:, :], in0=hraw[j][:, :], scalar=ss[j][:, 0:1], in1=gn[:, p * DV:(p + 1) * DV],
                                                         op0=ALU.mult, op1=ALU.mult),
                 reads=[hraw_b[j], ss_b[j], gn_b], writes=[hn_b[j]])
            P.op("pool", lambda e: e.tensor_tensor(out=hog[i][:, n, :], in0=hn_[j][:, :], in1=sg[i][:, n, :], op=ALU.mult),
                 reads=[hn_b[j], sg_b[i]], writes=[hog_b[i]])

        def store_group(p, g):
            i = (p * NGR + g) % 2
            t0 = g * GT
            for n in range(GC):
                for fc in range(4):
                    P.op("pe", lambda e, n=n, fc=fc: e.transpose(psT[:, fc, n * L:(n + 1) * L], hog[i][:, n, fc * 128:(fc + 1) * 128], idb[:, :]),
                         reads=[hog_b[i], idb_b], writes=[psT_b])
            P.op("act", lambda e: e.copy(out=hTs[i][:, :, :], in_=psT[:, :, :]), reads=[psT_b], writes=[hTs_b[i]])
            piece, c0 = divmod(t0, TT)
            P.op("pool", lambda e: e.dma_start(out=srcH.ap()[piece, p * DV:(p + 1) * DV, c0:c0 + GT].rearrange("(f q) t -> q f t", q=128),
                                               in_=hTs[i][:, :, :]),
                 reads=[hTs_b[i]], writes=[srcH_b[piece]], dma=True)
            if p == NP - 1 and (t0 + GT) % TT == 0:
                P.op("pool", lambda e: e.collective_compute("AllGather", ALU.bypass, replica_groups=GROUPS,
                                                            ins=[srcH.ap()[piece]], outs=[dstH.ap()[(piece + 1) * KCD * 128:(piece + 2) * KCD * 128, :]]),
                     reads=[srcH_b[piece]], writes=[dstH_b], dma=True, inc=1)

        n_glob = 0
        seq = [(p, c) for p in range(NP) for c in range(NCH)]
        load_group(0, 0)
        for p in range(NP):
            P.op("sp", lambda e, p=p: e.dma_start(out=ao_bc[:, :], in_=bc_ap(gs_ao, p * NCH, 128, NCH)), reads=[GS], writes=[ao_bc_b], dma=True)
            P.op("dve", lambda e: e.memset(Cst[:, :, :], 0.0), writes=[Cst_b])
            P.op("dve", lambda e: e.memset(nst[:, :], 0.0), writes=[nst_b])
            P.op("pool", lambda e: e.memset(Cbf[0][:, :, :], 0.0), writes=[Cbf_b[0]])
            P.op("pool", lambda e: e.memset(nbf[0][:, :], 0.0), writes=[nbf_b[0]])
            for g in range(NGR):
                if g + 1 < NGR:
                    load_group(p, g + 1)
                elif p + 1 < NP:
                    load_group(p + 1, 0)
                prep_group(p, g)
                for n in range(GC):
                    c = g * GC + n
                    stage_S(p, c, n_glob)
                    stage_H(p, c, n_glob)
                    n_glob += 1
                store_group(p, g)
        P.end_phase()


def emit_moba(nc, P, cfg, io):
    S, NP, TT, KCD = cfg["S"], cfg["NP"], cfg["TT"], cfg["KC"]
    DH, BS = 128, 256
    NB = S // BS
    NQT = S // 128
    GW = max(NB, 8)
    SC = DH ** -0.5
    BIG = 30000.0
    PIECE = min(2048, S)
    NPC = S // PIECE
    OG = min(4, TT // 128)
    qT, kT, vtm, cmask, ident = (io[k] for k in ("qT", "kT", "vtm", "cmask", "ident"))
    srcA, dstA = io["srcA"], io["dstA"]

    with ExitStack() as st:

        def sb(name, shape, dt=F32):
            return st.enter_context(nc.sbuf_tensor("mo_" + name, list(shape), dt))

        def ps(name, shape, dt=F32):
            return st.enter_context(nc.psum_tensor("mo_" + name, list(shape), dt))

        idf = sb("idf", (128, 128)); idf_b = P.buf("idf")
        idb = sb("idb", (128, 128), BF16); idb_b = P.buf("idb")
        cm = sb("cm", (128, 2 * BS)); cm_b = P.buf("cm")
        P.op("sp", lambda e: e.dma_start(out=idf[:, :], in_=ident.ap()), writes=[idf_b], dma=True)
        P.op("sp", lambda e: e.dma_start(out=cm[:, :], in_=cmask.ap()), writes=[cm_b], dma=True)
        P.op("act", lambda e: e.copy(out=idb[:, :], in_=idf[:, :]), reads=[idf_b], writes=[idb_b])
        zt = sb("zt", (128, KCD * 2), BF16); zt_b = P.buf("zt")
        P.op("pool", lambda e: e.memset(zt[:, :], 0.0), writes=[zt_b])
        dstA_b = P.buf("dstA")
        srcA_b = P.bufs(S // TT, "srcA")
        P.op("sp", lambda e: e.dma_start(out=dstA.ap()[0:KCD * 128, TT - 2:TT].rearrange("(k p) c -> p k c", p=128),
                                         in_=zt[:, :].rearrange("p (k c) -> p k c", c=2)),
             reads=[zt_b], writes=[dstA_b], dma=True)

        stage = [sb(f"stage{i}", (128, PIECE)) for i in range(2)]; stage_b = P.bufs(2, "stage")
        qb = sb("qb", (128, S), BF16); qb_b = P.buf("qb")
        kb = sb("kb", (128, S), BF16); kb_b = P.buf("kb")
        vb = sb("vb", (128, NQT, DH), BF16); vb_b = P.buf("vb")
        ksum = sb("ksum", (128, NB)); km = sb("km", (128, NB)); km_b = P.buf("km")
        gate_all = sb("gate_all", (128, NQT, NB)); gate_b = P.buf("gate")
        gsel = [sb(f"gsel{i}", (128, GW)) for i in range(2)]; gsel_b = P.bufs(2, "gsel")
        mx8 = [sb(f"mx8{i}", (128, 8)) for i in range(2)]; mx8_b = P.bufs(2, "mx8")
        mb = [sb(f"mb{i}", (128, GW)) for i in range(2)]; mb_b = P.bufs(2, "mb")
        bm = [sb(f"bm{i}", (128, NB + 1)) for i in range(2)]; bm_b = P.bufs(2, "bm")
        mrow = [sb(f"mrow{i}", (128, 1)) for i in range(2)]; mrow_b = P.bufs(2, "mrow")
        rsum = [sb(f"rsum{i}", (128, 1)) for i in range(2)]; rsum_b = P.bufs(2, "rsum")
        Sb = [sb(f"Sb{i}", (128, S)) for i in range(2)]; Sb_b = P.bufs(2, "Sb")
        Pb = [sb(f"Pb{i}", (128, S), BF16) for i in range(2)]; Pb_b = P.bufs(2, "Pb")
        PT = [sb(f"PT{i}", (128, 8 * 128), BF16) for i in range(2)]; PT_b = P.bufs(2, "PT")
        ost = [sb(f"ost{i}", (128, OG, DH), BF16) for i in range(2)]; ost_b = P.bufs(2, "ost")
        oTs = [sb(f"oTs{i}", (128, OG * 128), BF16) for i in range(2)]; oTs_b = P.bufs(2, "oTs")

        ps_S = [ps(f"ps_S{i}", (128, 512)) for i in range(2)]; ps_S_b = P.bufs(2, "psS")
        ps_T = [ps(f"ps_T{i}", (128, 1024), BF16) for i in range(2)]; ps_T_b = P.bufs(2, "psT")
        ps_o = [ps(f"ps_o{i}", (128, DH)) for i in range(2)]; ps_o_b = P.bufs(2, "pso")
        ps_g = ps("ps_g", (128, GW)); ps_g_b = P.buf("psg")
        psO = ps("psO", (128, OG * 128), BF16); psO_b = P.buf("psO")

        cnt = {"st": 0, "S": 0, "T": 0}

        def next_stage():
            i = cnt["st"] % 2
            cnt["st"] += 1
            return i

        def prologue(p):
            for pc in range(NPC):
                i = next_stage()
                t0 = pc * PIECE
                P.op("sp", lambda e, i=i, t0=t0: e.dma_start(out=stage[i][:, :], in_=kT[p, :, t0:t0 + PIECE]), writes=[stage_b[i]], dma=True)
                P.op("act", lambda e, i=i, t0=t0: e.copy(out=kb[:, t0:t0 + PIECE], in_=stage[i][:, :]), reads=[stage_b[i]], writes=[kb_b])
                nb0 = t0 // BS
                nbp = PIECE // BS
                P.op("dve", lambda e, i=i, nb0=nb0, nbp=nbp: e.tensor_reduce(
                    out=ksum[:, nb0:nb0 + nbp], in_=stage[i][:, :].rearrange("d (n t) -> d n t", t=BS), axis=AX.X, op=ALU.add),
                    reads=[stage_b[i]], writes=[km_b])
            P.op("dve", lambda e: e.tensor_scalar(out=km[:, :], in0=ksum[:, :], scalar1=1.0 / BS, scalar2=None, op0=ALU.mult),
                 reads=[km_b], writes=[km_b])
            for pc in range(NPC):
                i = next_stage()
                t0 = pc * PIECE
                P.op("sp", lambda e, i=i, t0=t0: e.dma_start(out=stage[i][:, :].rearrange("i (n e) -> i n e", e=DH),
                                                             in_=vtm.ap()[t0:t0 + PIECE, p * DH:(p + 1) * DH].rearrange("(n i) e -> i n e", i=128)),
                     writes=[stage_b[i]], dma=True)
                c0 = t0 // 128
                P.op("pool", lambda e, i=i, c0=c0: e.tensor_copy(out=vb[:, c0:c0 + PIECE // 128, :],
                                                                 in_=stage[i][:, :].rearrange("i (n e) -> i n e", e=DH)),
                     reads=[stage_b[i]], writes=[vb_b])
            for pc in range(NPC):
                i = next_stage()
                t0 = pc * PIECE
                P.op("sp", lambda e, i=i, t0=t0: e.dma_start(out=stage[i][:, :], in_=qT[p, :, t0:t0 + PIECE]), writes=[stage_b[i]], dma=True)
                P.op("act", lambda e, i=i, t0=t0: e.copy(out=qb[:, t0:t0 + PIECE], in_=stage[i][:, :]), reads=[stage_b[i]], writes=[qb_b])
                for tl in range(PIECE // 128):
                    qt = t0 // 128 + tl
                    P.op("pe", lambda e, i=i, tl=tl: e.matmul(ps_g[:, 0:NB], stage[i][:, tl * 128:(tl + 1) * 128], km[:, :], start=True, stop=True),
                         reads=[stage_b[i], km_b], writes=[ps_g_b])
                    P.op("act", lambda e, qt=qt: e.copy(out=gate_all[:, qt, :], in_=ps_g[:, 0:NB]), reads=[ps_g_b], writes=[gate_b])

        def front(p, qt):
            bi, o = divmod(qt, 2)
            s = qt % 2
            nblk = bi + 1
            P.op("pool", lambda e: e.memset(gsel[s][:, :], -1e30), writes=[gsel_b[s]])
            if bi > 0:
                P.op("pool", lambda e: e.tensor_copy(out=gsel[s][:, 0:bi], in_=gate_all[:, qt, 0:bi]), reads=[gate_b], writes=[gsel_b[s]])
            P.op("dve", lambda e: e.max(out=mx8[s][:, :], in_=gsel[s][:, :]), reads=[gsel_b[s]], writes=[mx8_b[s]])
            P.op("dve", lambda e: e.tensor_scalar(out=mb[s][:, :], in0=gsel[s][:, :], scalar1=mx8[s][:, 2:3], scalar2=-BIG,
                                                  op0=ALU.is_lt, op1=ALU.mult),
                 reads=[gsel_b[s], mx8_b[s]], writes=[mb_b[s]])
            for n0 in range(0, nblk, 2):
                k = cnt["S"] % 2
                cnt["S"] += 1
                nbl = min(2, nblk - n0)
                ncols = nbl * BS
                P.op("pe", lambda e, k=k, n0=n0, ncols=ncols: e.matmul(ps_S[k][:, 0:ncols], qb[:, qt * 128:(qt + 1) * 128],
                                                                       kb[:, n0 * BS:n0 * BS + ncols], start=True, stop=True),
                     reads=[qb_b, kb_b], writes=[ps_S_b[k]])
                for h in range(nbl):
                    n = n0 + h
                    if n < bi:
                        P.op("dve", lambda e, k=k, n=n, h=h: e.tensor_scalar(
                            out=Sb[s][:, n * BS:(n + 1) * BS], in0=ps_S[k][:, h * BS:(h + 1) * BS], scalar1=mb[s][:, n:n + 1], scalar2=None,
                            op0=ALU.add, op1=ALU.max, accum_out=bm[s][:, n:n + 1]),
                            reads=[ps_S_b[k], mb_b[s]], writes=[Sb_b[s], bm_b[s]])
                    else:
                        P.op("dve", lambda e, k=k, n=n, h=h: e.tensor_tensor(
                            out=Sb[s][:, n * BS:(n + 1) * BS], in0=ps_S[k][:, h * BS:(h + 1) * BS], in1=cm[:, o * BS:(o + 1) * BS], op=ALU.add),
                            reads=[ps_S_b[k], cm_b], writes=[Sb_b[s]])
                        P.op("dve", lambda e, n=n: e.reduce_max(out=bm[s][:, n:n + 1], in_=Sb[s][:, n * BS:(n + 1) * BS], axis=AX.X),
                             reads=[Sb_b[s]], writes=[bm_b[s]])
            P.op("dve", lambda e: e.reduce_max(out=mrow[s][:, :], in_=bm[s][:, 0:nblk], axis=AX.X), reads=[bm_b[s]], writes=[mrow_b[s]])
            P.op("dve", lambda e: e.tensor_scalar(out=mrow[s][:, :], in0=mrow[s][:, :], scalar1=-SC, scalar2=None, op0=ALU.mult),
                 reads=[mrow_b[s]], writes=[mrow_b[s]])
            nk = nblk * BS
            P.op("act", lambda e: e.activation(out=Pb[s][:, 0:nk], in_=Sb[s][:, 0:nk], func=AF.Exp, bias=mrow[s][:, 0:1], scale=SC,
                                               accum_out=rsum[s][:, :]),
                 reads=[Sb_b[s], mrow_b[s]], writes=[Pb_b[s], rsum_b[s]])

        def back(p, qt):
            bi, o = divmod(qt, 2)
            s = qt % 2
            nch = (bi + 1) * 2
            groups = [(g0, min(8, nch - g0)) for g0 in range(0, nch, 8)]
            oi = qt % 2
            slot = qt % OG
            osl = (qt // OG) % 2

            def emit_T(g0, ng):
                k = cnt["T"] % 2
                cnt["T"] += 1
                for j in range(ng):
                    kc = g0 + j
                    P.op("pe", lambda e, j=j, kc=kc: e.transpose(ps_T[k][:, j * 128:(j + 1) * 128], Pb[s][:, kc * 128:(kc + 1) * 128], idb[:, :]),
                         reads=[Pb_b[s], idb_b], writes=[ps_T_b[k]])
                P.op("act", lambda e: e.copy(out=PT[k][:, 0:ng * 128], in_=ps_T[k][:, 0:ng * 128]), reads=[ps_T_b[k]], writes=[PT_b[k]])
                return k

            def emit_PV(g0, ng, k):
                for j in range(ng):
                    kc = g0 + j
                    P.op("pe", lambda e, j=j, kc=kc: e.matmul(ps_o[oi][:, :], PT[k][:, j * 128:(j + 1) * 128], vb[:, kc, :],
                                                              start=(kc == 0), stop=(kc == nch - 1)),
                         reads=[PT_b[k], vb_b], writes=[ps_o_b[oi]])

            ks = [None] * len(groups)
            ks[0] = emit_T(*groups[0])
            for gi_, (g0, ng) in enumerate(groups):
                if gi_ + 1 < len(groups):
                    ks[gi_ + 1] = emit_T(*groups[gi_ + 1])
                emit_PV(g0, ng, ks[gi_])
            P.op("dve", lambda e: e.reciprocal(out=rsum[s][:, :], in_=rsum[s][:, :]), reads=[rsum_b[s]], writes=[rsum_b[s]])
            P.op("dve", lambda e: e.tensor_scalar(out=ost[osl][:, slot, :], in0=ps_o[oi][:, :], scalar1=rsum[s][:, 0:1], scalar2=None, op0=ALU.mult),
                 reads=[ps_o_b[oi], rsum_b[s]], writes=[ost_b[osl]])
            if slot == OG - 1:
                q0 = (qt - OG + 1) * 128
                for j in range(OG):
                    P.op("pe", lambda e, j=j: e.transpose(psO[:, j * 128:(j + 1) * 128], ost[osl][:, j, :], idb[:, :]),
                         reads=[ost_b[osl], idb_b], writes=[psO_b])
                P.op("act", lambda e: e.copy(out=oTs[osl][:, :], in_=psO[:, :]), reads=[psO_b], writes=[oTs_b[osl]])
                piece, c0 = divmod(q0, TT)
                P.op("pool", lambda e: e.dma_start(out=srcA.ap()[piece, p * DH:(p + 1) * DH, c0:c0 + OG * 128], in_=oTs[osl][:, :]),
                     reads=[oTs_b[osl]], writes=[srcA_b[piece]], dma=True)
                if p == NP - 1 and (q0 + OG * 128) % TT == 0:
                    P.op("pool", lambda e: e.collective_compute("AllGather", ALU.bypass, replica_groups=GROUPS,
                                                                ins=[srcA.ap()[piece]], outs=[dstA.ap()[(piece + 1) * KCD * 128:(piece + 2) * KCD * 128, :]]),
                         reads=[srcA_b[piece]], writes=[dstA_b], dma=True, inc=1)

        for p in range(NP):
            prologue(p)
            front(p, 0)
            for qt in range(NQT):
                if qt + 1 < NQT:
                    front(p, qt + 1)
                back(p, qt)
        P.end_phase()


def build_fused(cfg):
    S, D, F, TT, NG = cfg["S"], cfg["D"], cfg["F"], cfg["TT"], cfg["NG"]
    KC = D // 128
    FC = F // 128
    Q = S // 4
    NTQ = Q // TT
    NPCS = S // TT
    DQ = D // 4
    KQ = KC // 4
    DK, DV, DH = 256, 512, 128
    NP2 = DQ // DV
    NP4 = DQ // DH
    HL = 2
    PH = cfg.get("phases", "ABCDEF")
    nc = bass.Bass("TRN2", target_bir_lowering=False)

    def din(name, shape, dt=F32):
        return nc.dram_tensor(name, list(shape), dt, kind="ExternalInput")

    def dint(name, shape, dt=F32):
        return nc.dram_tensor(name, list(shape), dt, kind="Internal")

    xT_all = din("xT_all", (NPCS, D, TT))
    xq = din("xq", (NTQ, D, TT + HL))
    gA = din("gA", (128, KC))
    NFA = NP2 * 4 + 1
    NGA = NP2 * 3
    wA_fm = din("wA_fm", (NFA, 128, KC, 128))
    wA_tm = din("wA_tm", (NGA, 128, KC, 512))
    gbias = din("gbias", (128, NP2 * 2))
    gain = din("gain", (64, NP2 * DV))
    negmask = din("negmask", (64, 64))
    ident = din("ident", (128, 128))
    cmask = din("cmask", (128, 512))
    w_mix = [din(f"w_mix{i}", (KC, 128, KC, 128)) for i in range(2)]
    ffn = [dict(g=din(f"g_ffn{i}", (128, KC)), w_up=din(f"w_up{i}", (2 * FC, 128, KC, 128)),
                cw=din(f"cw{i}", (128, 3, 2 * FC)), cb=din(f"cb{i}", (128, 2 * FC)),
                w_dn=din(f"w_dn{i}", (KC, 128, FC, 128)), F=F, NG=NG) for i in range(2)]
    gD = din("gD", (128, KC))
    NGD = NP4 * DH // 512
    wD_fm = din("wD_fm", (2 * NP4, 128, KC, 128))
    wD_tm = din("wD_tm", (NGD, 128, KC, 512))
    g_fin = din("g_fin", (128, KC))
    xoT = nc.dram_tensor("xoT", [D, Q], F32, kind="ExternalOutput")

    qT2 = dint("qT2", (NP2, DK, S)); kT2 = dint("kT2", (NP2, DK, S))
    ktm2 = dint("ktm2", (NP2, S, DK)); vtm2 = dint("vtm2", (NP2, S, DV)); otm2 = dint("otm2", (NP2, S, DV))
    gi = dint("gi", (NP2, S)); gf = dint("gf", (NP2, S))
    gs_u = dint("gs_u", (NP2, S)); gs_a = dint("gs_a", (NP2, S)); gs_ao = dint("gs_ao", (NP2, S // 64))
    srcH = dint("srcH", (NPCS, DQ, TT), BF16); dstH = dint("dstH", ((NPCS + 1) * D, TT), BF16)
    x1s = dint("x1s", (D, HL + Q)); srcB = dint("srcB", (D, HL)); dstB = dint("dstB", (5 * D, HL))
    srcX = dint("srcX", (NTQ * 4, DQ, TT), BF16); dstX = dint("dstX", (NTQ * 4, 4 * DQ, TT), BF16)
    qT4 = dint("qT4", (NP4, DH, S)); kT4 = dint("kT4", (NP4, DH, S)); vtm4 = dint("vtm4", (S, NP4 * DH))
    srcA = dint("srcA", (NPCS, DQ, TT), BF16); dstA = dint("dstA", ((NPCS + 1) * D, TT), BF16)

    def w2d(t, i):
        return t[i].rearrange("p k j -> p (k j)")

    def gathered_mix(dst, w):
        return dict(
            w=w,
            main=lambda e, tt: dst.ap()[bass.ds((P.rank(e) * NTQ + 1 + tt) * D, D), :],
            halo=lambda e, tt: dst.ap()[bass.ds((P.rank(e) * NTQ + tt) * D, D), TT - HL:TT])

    with ExitStack() as gst:
        P = Prog(nc, gst)

        fmA = []
        for p in range(NP2):
            for j in range(2):
                fmA.append((w2d(wA_fm, p * 4 + j),
                            lambda tt, ev, p=p, j=j: [(qT2.ap()[p, j * 128:(j + 1) * 128, tt * TT:(tt + 1) * TT], ev[:, 0:TT])]))
            for j in range(2):
                fmA.append((w2d(wA_fm, p * 4 + 2 + j),
                            lambda tt, ev, p=p, j=j: [(kT2.ap()[p, j * 128:(j + 1) * 128, tt * TT:(tt + 1) * TT], ev[:, 0:TT])]))
        fmA.append((w2d(wA_fm, NP2 * 4),
                    lambda tt, ev: [(gi.ap()[0:NP2, tt * TT:(tt + 1) * TT], ev[0:NP2, 0:TT]),
                                    (gf.ap()[0:NP2, tt * TT:(tt + 1) * TT], ev[NP2:2 * NP2, 0:TT])]))
        tmA = []
        for p in range(NP2):
            for j, (dt_, ncols) in enumerate(((ktm2, DK), (vtm2, DV), (otm2, DV))):
                tmA.append((w2d(wA_tm, p * 3 + j), ncols,
                            lambda tt, tb, p=p, dt_=dt_: dt_.ap()[p, tt * TT + tb * 128:tt * TT + (tb + 1) * 128, :]))
        if "A" in PH: emit_dense(nc, P, dict(name="A", NT=NPCS, TT=TT, HL=0, D=D, src="x",
                               x_ap=lambda tt, kc: xT_all[tt, kc * 128:(kc + 1) * 128, :],
                               nxt=dict(g=gA, fm=fmA, tm=tmA)))

        if "B" in PH: emit_mlstm(nc, P, dict(S=S, NP=NP2, TT=TT, KC=KC),
                   dict(qT=qT2, kT=kT2, ktm=ktm2, vtm=vtm2, otm=otm2, gi=gi, gf=gf, gbias=gbias, gain=gain,
                        negmask=negmask, ident=ident, gs_u=gs_u, gs_a=gs_a, gs_ao=gs_ao, srcH=srcH, dstH=dstH, dstB=dstB))

        x1s_b = P.buf("x1s")
        srcB_b = P.buf("srcB")
        dstB_b = P.buf("dstB")
        srcX_b = P.bufs(NTQ * 4, "srcX")
        dstX_b = P.buf("dstX")

        def after_ffn_C(tt, x1, x1_b, HL_, TW):
            for kc in range(KC):
                P.op("sp", lambda e, kc=kc: e.dma_start(out=x1s.ap()[kc * 128:(kc + 1) * 128, HL + tt * TT:HL + (tt + 1) * TT],
                                                        in_=x1[:, kc, HL_:TW]),
                     reads=[x1_b[kc]], writes=[x1s_b], dma=True)
            if tt == NTQ - 1:
                P.op("sp", lambda e: e.dma_start(out=srcB.ap().rearrange("(k p) c -> p k c", p=128), in_=x1[:, :, TW - HL:TW]),
                     reads=x1_b, writes=[srcB_b], dma=True)
                P.op("pool", lambda e: e.collective_compute("AllGather", ALU.bypass, replica_groups=GROUPS,
                                                            ins=[srcB.ap()], outs=[dstB.ap()[D:5 * D, :]]),
                     reads=[srcB_b], writes=[dstB_b], dma=True, inc=1)
                P.op("sp", lambda e: e.dma_start(out=x1s.ap()[:, 0:HL], in_=dstB.ap()[bass.ds(P.rank(e) * D, D), :]),
                     reads=[dstB_b], writes=[x1s_b], dma=True)

        def hn_store_C(tt, hn, hn_b, HL_, TW):
            for fb in range(4):
                c = tt * 4 + fb
                P.op("sp", lambda e, c=c, fb=fb: e.dma_start(out=srcX.ap()[c].rearrange("(k p) t -> p k t", p=128),
                                                             in_=hn[:, fb * KQ:(fb + 1) * KQ, HL_:TW]),
                     reads=hn_b[fb * KQ:(fb + 1) * KQ], writes=[srcX_b[c]], dma=True)
                P.op("pool", lambda e, c=c: e.collective_compute("AllGather", ALU.bypass, replica_groups=GROUPS,
                                                                 ins=[srcX.ap()[c]], outs=[dstX.ap()[c]]),
                     reads=[srcX_b[c]], writes=[dstX_b], dma=True, inc=1)

        if "C" in PH: emit_dense(nc, P, dict(name="C", NT=NTQ, TT=TT, HL=HL, D=D, src="x",
                               x_ap=lambda tt, kc: xq[tt, kc * 128:(kc + 1) * 128, :],
                               mix=gathered_mix(dstH, w_mix[0]), ffn=ffn[0], after_ffn=after_ffn_C, NWB=6,
                               nxt=dict(g=gD, hn_store=hn_store_C)))

        def hn_load_D(tg):
            r, tt = divmod(tg, NTQ)
            return [(fb * KQ, KQ, dstX.ap()[tt * 4 + fb, r * DQ:(r + 1) * DQ, :]) for fb in range(4)]

        fmD = []
        for p in range(NP4):
            fmD.append((w2d(wD_fm, p), lambda tg, ev, p=p: [(qT4.ap()[p, :, tg * TT:(tg + 1) * TT], ev[:, 0:TT])]))
        for p in range(NP4):
            fmD.append((w2d(wD_fm, NP4 + p), lambda tg, ev, p=p: [(kT4.ap()[p, :, tg * TT:(tg + 1) * TT], ev[:, 0:TT])]))
        tmD = [(w2d(wD_tm, g), 512, lambda tg, tb, g=g: vtm4.ap()[tg * TT + tb * 128:tg * TT + (tb + 1) * 128, g * 512:(g + 1) * 512])
               for g in range(NGD)]
        if "D" in PH: emit_dense(nc, P, dict(name="D", NT=NPCS, TT=TT, HL=0, D=D, src="hn", hn_load=hn_load_D, NWB=6,
                               nxt=dict(g=None, fm=fmD, tm=tmD)))

        if "E" in PH: emit_moba(nc, P, dict(S=S, NP=NP4, TT=TT, KC=KC),
                  dict(qT=qT4, kT=kT4, vtm=vtm4, cmask=cmask, ident=ident, srcA=srcA, dstA=dstA))

        if "F" in PH: emit_dense(nc, P, dict(name="F", NT=NTQ, TT=TT, HL=HL, D=D, src="x",
                               x_ap=lambda tt, kc: x1s.ap()[kc * 128:(kc + 1) * 128, tt * TT:(tt + 1) * TT + HL],
                               mix=gathered_mix(dstA, w_mix[1]), ffn=ffn[1], NWB=6,
                               final=dict(g=g_fin, out=lambda tt, kc: xoT.ap()[kc * 128:(kc + 1) * 128, tt * TT:(tt + 1) * TT])))
        if SINGLE_BLOCK:
            P.emit()
        print("fused program built; dma semaphores:", P.ndsem, flush=True)
    return nc


_progs = {}


def _prog(key, cfg):
    if key not in _progs:
        _progs[key] = build_fused(cfg)
    return _progs[key]


def run_fused(cfg, x, norm_mix, norm_ffn, a_w_in, a_gate_bias, a_head_norm, a_w_out,
              b_w_qkv, b_w_out, ffn_w_up, ffn_conv_w, ffn_conv_b, ffn_w_down, final_norm, trace=False):
    f32 = np.float32
    S, D, F, TT = cfg["S"], cfg["D"], cfg["F"], cfg["TT"]
    B = x.shape[0]
    assert B * 4 == N_CORES
    Q = S // 4
    NTQ = Q // TT
    NPCS = S // TT
    DQ = D // 4
    DK, DV, DH = 256, 512, 128
    MLH, MBH = D // DV, D // DH
    NP2, NP4 = DQ // DV, DQ // DH
    HL = 2
    nc = _prog((S, D, F, TT, cfg.get("phases", "ABCDEF")), cfg)

    negmask = np.where(np.arange(64)[None, :] >= np.arange(64)[:, None], 0.0, -30000.0).astype(f32)
    ident = np.eye(128, dtype=f32)
    cmask = np.zeros((128, 512), f32)
    for o in range(2):
        cmask[:, o * 256:(o + 1) * 256] = np.where(np.arange(256)[None, :] <= o * 128 + np.arange(128)[:, None], 0.0, -30000.0)
    shared = dict(gA=lay_vec(norm_mix[0]), gD=lay_vec(norm_mix[1]), g_fin=lay_vec(final_norm),
                  negmask=negmask, ident=ident, cmask=cmask,
                  w_mix0=lay_w(a_w_out[0]), w_mix1=lay_w(b_w_out[0]))
    for i in range(2):
        shared[f"g_ffn{i}"] = lay_vec(norm_ffn[i])
        shared[f"w_up{i}"] = lay_w(ffn_w_up[i])
        shared[f"cw{i}"] = np.ascontiguousarray(np.stack([lay_vec(ffn_conv_w[i, j]) for j in range(3)], axis=1))
        shared[f"cb{i}"] = lay_vec(ffn_conv_b[i])
        shared[f"w_dn{i}"] = lay_w(ffn_w_down[i])

    s1, s2 = MLH * DK, 2 * MLH * DK
    s3 = s2 + MLH * DV
    s4 = s3 + D
    Wa = a_w_in[0]
    Wq = b_w_qkv[0]
    per_rank = []
    for r in range(4):
        heads = [r * NP2 + p for p in range(NP2)]
        cols = []
        for h in heads:
            cols.append(Wa[:, h * DK:(h + 1) * DK])
            cols.append(Wa[:, s1 + h * DK:s1 + (h + 1) * DK])
        gate_cols = np.zeros((D, 128), f32)
        for p, h in enumerate(heads):
            gate_cols[:, p] = Wa[:, s4 + h]
            gate_cols[:, NP2 + p] = Wa[:, s4 + MLH + h]
        cols.append(gate_cols)
        wA_fm = lay_w(np.concatenate(cols, axis=1))
        tcols = []
        for h in heads:
            kpad = np.zeros((D, 512), f32)
            kpad[:, :DK] = Wa[:, s1 + h * DK:s1 + (h + 1) * DK]
            tcols += [kpad, Wa[:, s2 + h * DV:s2 + (h + 1) * DV], Wa[:, s3 + h * DV:s3 + (h + 1) * DV]]
        wA_tm = lay_w_tm(np.concatenate(tcols, axis=1))
        gb = np.array([[a_gate_bias[0, h], a_gate_bias[0, MLH + h]] for h in heads], f32).reshape(1, -1)
        gn = np.concatenate([a_head_norm[0, h * DV:(h + 1) * DV] for h in heads]).reshape(1, -1)
        mh = [r * NP4 + p for p in range(NP4)]
        wD_fm = lay_w(np.concatenate([Wq[:, h * DH:(h + 1) * DH] for h in mh] +
                                     [Wq[:, D + h * DH:D + (h + 1) * DH] for h in mh], axis=1))
        wD_tm = lay_w_tm(np.concatenate([Wq[:, 2 * D + h * DH:2 * D + (h + 1) * DH] for h in mh], axis=1))
        per_rank.append(dict(wA_fm=wA_fm, wA_tm=wA_tm,
                             gbias=np.ascontiguousarray(np.broadcast_to(gb, (128, gb.shape[1]))),
                             gain=np.ascontiguousarray(np.broadcast_to(gn, (64, gn.shape[1]))),
                             wD_fm=wD_fm, wD_tm=wD_tm))
    maps = []
    for b in range(B):
        xTb = np.ascontiguousarray(x[b].T)
        xT_all = np.ascontiguousarray(xTb.reshape(D, NPCS, TT).transpose(1, 0, 2))
        for r in range(4):
            xq = np.zeros((NTQ, D, TT + HL), f32)
            for tt in range(NTQ):
                s0 = r * Q + tt * TT
                if s0 == 0:
                    xq[tt, :, HL:] = xTb[:, 0:TT]
                else:
                    xq[tt] = xTb[:, s0 - HL:s0 + TT]
            maps.append(dict(shared, **per_rank[r], xT_all=xT_all, xq=xq))
        del xTb
    res = run_bass_kernel_spmd(nc, maps, core_ids=list(range(N_CORES)), trace=trace)
    out = np.empty((B, S, D), f32)
    for c in range(N_CORES):
        b, r = divmod(c, 4)
        out[b, r * Q:(r + 1) * Q, :] = res.results[c]["xoT"].T
    return out, res


def kernel(x, norm_mix, norm_ffn, a_w_in, a_gate_bias, a_head_norm, a_w_out,
           b_w_qkv, b_w_out, ffn_w_up, ffn_conv_w, ffn_conv_b, ffn_w_down, final_norm):
    f32 = np.float32
    args = [x, norm_mix, norm_ffn, a_w_in, a_gate_bias, a_head_norm, a_w_out, b_w_qkv, b_w_out,
            ffn_w_up, ffn_conv_w, ffn_conv_b, ffn_w_down, final_norm]
    args = [np.asarray(a, f32) for a in args]
    cfg = dict(S=8192, D=4096, F=11008, TT=512, NG=4)
    out, _ = run_fused(cfg, *args)
    return out
```

```python
import numpy as np
from contextlib import ExitStack
import concourse.bass as bass
import concourse.mybir as mybir
from concourse.bass_utils import run_bass_kernel_spmd

F32 = mybir.dt.float32
BF16 = mybir.dt.bfloat16
AF = mybir.ActivationFunctionType
ALU = mybir.AluOpType
AX = mybir.AxisListType

NORM_EPS = 1e-6
N_CORES = 8
GROUPS = [[0, 1, 2, 3], [4, 5, 6, 7]]
SINGLE_BLOCK = True


class Buf:
    __slots__ = ("name", "last_w", "readers", "dsem", "dcount", "slot")

    def __init__(self, name):
        self.name = name
        self.last_w = None
        self.readers = []
        self.dsem = None
        self.dcount = 0
        self.slot = None


class Op:
    __slots__ = ("eng", "fn", "waits", "signal", "idx", "is_dma", "dsem", "dcount", "count", "inc")

    def __init__(self, eng, fn, idx, is_dma):
        self.eng = eng
        self.fn = fn
        self.idx = idx
        self.is_dma = is_dma
        self.waits = []
        self.signal = False
        self.dsem = None
        self.dcount = 0
        self.count = 0
        self.inc = 16


ENGS = ("pe", "act", "dve", "pool", "sp")


class Prog:
    def __init__(self, nc, stack):
        self.nc = nc
        self.stack = stack
        self.ops = {e: [] for e in ENGS}
        self.esem = {e: stack.enter_context(nc.semaphore("es_" + e)) for e in ENGS}
        self.seen_e = {e: {} for e in ENGS}
        self.seen_d = {e: {} for e in ENGS}
        self.nidx = {e: 0 for e in ENGS}
        self.nsig = {e: 0 for e in ENGS}
        self.pstart = {e: 0 for e in ENGS}
        self.free_slots = []
        self.phase_bufs = []
        self.nbuf = 0
        self.ndsem = 0
        self._rank = None

    def buf(self, name="b"):
        self.nbuf += 1
        b = Buf(f"{name}{self.nbuf}")
        self.phase_bufs.append(b)
        return b

    def bufs(self, n, name="b"):
        return [self.buf(name) for _ in range(n)]

    def _get_dsem(self, b):
        if b.dsem is None:
            if self.free_slots:
                slot = self.free_slots.pop()
            else:
                self.ndsem += 1
                slot = [self.stack.enter_context(self.nc.semaphore(f"ds{self.ndsem}")), 0]
            b.slot = slot
            b.dsem = slot[0]
            b.dcount = slot[1]
        return b.dsem

    def rank(self, engine):
        if self._rank is None:
            self._rank = engine.partition_id() % 4
        return self._rank

    def op(self, eng, fn, reads=(), writes=(), dma=False, sem_buf=None, inc=16):
        lst = self.ops[eng]
        o = Op(eng, fn, self.nidx[eng], dma)
        self.nidx[eng] += 1
        o.inc = inc
        deps = []
        for b in reads:
            if b.last_w is not None:
                deps.append(b.last_w)
        for b in writes:
            if b.last_w is not None:
                deps.append(b.last_w)
            deps.extend(b.readers)
        emax = {}
        dmax = {}
        for d in deps:
            if d.is_dma:
                k = id(d.dsem)
                if k not in dmax or dmax[k][1] < d.dcount:
                    dmax[k] = (d.dsem, d.dcount)
            else:
                if d.eng == eng and not dma:
                    if eng == "pe":
                        continue
                if d.eng not in emax or emax[d.eng].idx < d.idx:
                    emax[d.eng] = d
        se = self.seen_e[eng]
        for e2, d in emax.items():
            if se.get(e2, -1) >= d.idx:
                continue
            se[e2] = d.idx
            d.signal = True
            o.waits.append(("e", d))
        sd = self.seen_d[eng]
        for k, (sem, cnt) in dmax.items():
            if sd.get(k, -1) >= cnt:
                continue
            sd[k] = cnt
            o.waits.append(("d", sem, cnt))
        if dma:
            sb = sem_buf
            if sb is None:
                sb = writes[0] if writes else reads[0]
            o.dsem = self._get_dsem(sb)
            sb.dcount += inc
            o.dcount = sb.dcount
        for b in reads:
            b.readers.append(o)
        for b in writes:
            b.last_w = o
            b.readers = []
        lst.append(o)
        return o

    def barrier(self):
        lasts = {}
        for e in ENGS:
            for p in reversed(self.ops[e][self.pstart[e]:]):
                if not p.is_dma and p.fn is not None:
                    lasts[e] = p
                    p.signal = True
                    break
        dm = {}
        for e in ENGS:
            for p in self.ops[e][self.pstart[e]:]:
                if p.is_dma:
                    k = id(p.dsem)
                    if k not in dm or dm[k][1] < p.dcount:
                        dm[k] = (p.dsem, p.dcount)
        for e in ENGS:
            o = Op(e, None, self.nidx[e], False)
            self.nidx[e] += 1
            for e2, last in lasts.items():
                if self.seen_e[e].get(e2, -1) < last.idx:
                    o.waits.append(("e", last))
                    self.seen_e[e][e2] = last.idx
            for k, (sem, cnt) in dm.items():
                if self.seen_d[e].get(k, -1) < cnt:
                    o.waits.append(("d", sem, cnt))
                    self.seen_d[e][k] = cnt
            self.ops[e].append(o)
        self.pstart = {e: len(self.ops[e]) for e in ENGS}

    def end_phase(self):
        self.barrier()
        print("phase ops", {e: len(self.ops[e]) for e in ENGS}, "nsig(before emit)", dict(self.nsig),
              "max dma sem", max([sl[1] for sl in self.free_slots] + [b.dcount for b in self.phase_bufs] + [0]), flush=True)
        if not SINGLE_BLOCK:
            self.emit()
        for b in self.phase_bufs:
            if b.slot is not None:
                b.slot[1] = b.dcount
                self.free_slots.append(b.slot)
                b.slot = None
        self.phase_bufs = []

    def emit(self):
        nc = self.nc
        for e in ENGS:
            c = self.nsig[e]
            for o in self.ops[e]:
                if o.signal:
                    c += 1
                    o.count = c
            self.nsig[e] = c
        handles = {"pe": "tensor", "act": "scalar", "dve": "vector", "pool": "gpsimd", "sp": "sync"}
        with nc.Block() as block:
            for e in ENGS:
                if not self.ops[e]:
                    continue

                def body(engine, e=e):
                    self._rank = None
                    for o in self.ops[e]:
                        for w in o.waits:
                            if w[0] == "e":
                                engine.wait_ge(self.esem[w[1].eng], w[1].count)
                            else:
                                engine.wait_ge(w[1], w[2])
                        if o.fn is None:
                            continue
                        ins = o.fn(engine)
                        if o.is_dma:
                            ins.then_inc(o.dsem, o.inc)
                        elif o.signal:
                            ins.then_inc(self.esem[e], 1)
                    self._rank = None

                getattr(block, handles[e])(body)
        self.ops = {e: [] for e in ENGS}
        self.pstart = {e: 0 for e in ENGS}


def lay_w(W):
    K, N = W.shape
    NC = -(-N // 128)
    if NC * 128 != N:
        Wp = np.zeros((K, NC * 128), W.dtype)
        Wp[:, :N] = W
        W = Wp
    KC = K // 128
    return np.ascontiguousarray(W.reshape(KC, 128, NC, 128).transpose(2, 1, 0, 3))


def lay_w_tm(W, gcols=512):
    K, N = W.shape
    NG = -(-N // gcols)
    if NG * gcols != N:
        Wp = np.zeros((K, NG * gcols), W.dtype)
        Wp[:, :N] = W
        W = Wp
    KC = K // 128
    return np.ascontiguousarray(W.reshape(KC, 128, NG, gcols).transpose(2, 1, 0, 3))


def lay_vec(v):
    n = v.shape[0]
    NC = -(-n // 128)
    if NC * 128 != n:
        vp = np.zeros((NC * 128,), v.dtype)
        vp[:n] = v
        v = vp
    return np.ascontiguousarray(v.reshape(NC, 128).T)


def _split_last(n, cap=2048):
    for b in range(min(n, cap), 0, -1):
        if n % b == 0:
            return b
    return 1


def emit_dense(nc, P, cfg):
    NT, TT, HL, D = cfg["NT"], cfg["TT"], cfg["HL"], cfg["D"]
    KC = D // 128
    TW = TT + HL
    src = cfg["src"]
    mix, ffn, nxt, final = cfg.get("mix"), cfg.get("ffn"), cfg.get("nxt"), cfg.get("final")
    fm = nxt["fm"] if nxt and nxt.get("fm") else []
    tm = nxt["tm"] if nxt and nxt.get("tm") else []
    has_x = src == "x"
    F = ffn["F"] if ffn else 0
    FC = F // 128
    NG = ffn["NG"] if ffn else 1

    with ExitStack() as st:
        pfx = cfg.get("name", "d") + "_"

        def sb(name, shape, dt):
            return st.enter_context(nc.sbuf_tensor(pfx + name, list(shape), dt))

        def ps(name, shape, dt=F32):
            return st.enter_context(nc.psum_tensor(pfx + name, list(shape), dt))

        if has_x:
            x1 = sb("x1", (128, KC, TW), F32)
            x1_b = P.bufs(KC, "x1_")
            ones = sb("ones", (128, 128), F32)
            ones_b = P.buf("ones")
            sq = [sb(f"sq{i}", (128, TW), F32) for i in range(2)]
            sq_b = P.bufs(2, "sq")
            rstd = sb("rstd", (128, TW), F32)
            rstd_b = P.buf("rstd")
            eps_t = sb("eps_t", (128, 1), F32)
            eps_b = P.buf("eps")
        hn = sb("hn", (128, KC, TW), BF16)
        hn_b = P.bufs(KC, "hn_")
        NWB = cfg.get("NWB", 3)
        if ffn:
            gsz = [FC // NG + (1 if i < FC % NG else 0) for i in range(NG)]
            gst = [sum(gsz[:i]) for i in range(NG)]
            FG = max(gsz)
        KW = max(KC, FG if ffn else 0)
        wt = [sb(f"wt{i}", (128, KW, 128), BF16) for i in range(NWB)]
        wt_b = P.bufs(NWB, "wt")
        NEV = 3
        EVW = max(TT, 512) if tm else TT
        if fm or tm or final:
            ev = [sb(f"ev{i}", (128, EVW), F32) for i in range(NEV)]
            ev_b = P.bufs(NEV, "ev")
        if tm:
            wtm = [sb(f"wtm{i}", (128, KC, 512), BF16) for i in range(2)]
            wtm_b = P.bufs(2, "wtm")
        gvec = {}
        if ffn:
            gt = sb("gt", (128, FG, TT), BF16)
            gt_b = P.bufs(FG, "gt_")
            ucat = [sb(f"uc{i}", (128, TW), F32) for i in range(4)]
            ucat_b = P.bufs(4, "uc")
            cacc = [sb(f"ca{i}", (128, TT), F32) for i in range(4)]
            cacc_b = P.bufs(4, "ca")
            cw_s = sb("cw_s", (128, 3, 2 * FC), F32)
            cb_s = sb("cb_s", (128, 2 * FC), F32)
            cwb = P.buf("cw")
            sv = sb("sv", (128, 2 * FC, 2), F32)
            sv_b = P.bufs(2 * FC, "sv")
            gvec["ffn"] = (sb("g_ffn_s", (128, KC), F32), P.buf("gffn"), ffn["g"])
        if mix:
            hb = hn
            hb_b = hn_b
        if nxt and nxt.get("g") is not None:
            gvec["nxt"] = (sb("g_nxt_s", (128, KC), F32), P.buf("gnxt"), nxt["g"])
        if final:
            gvec["fin"] = (sb("g_fin_s", (128, KC), F32), P.buf("gfin"), final["g"])
        NPS = 6
        acc = [ps(f"acc{i}", (128, 512)) for i in range(NPS)]
        acc_b = P.bufs(NPS, "acc")
        if has_x:
            ssp = ps("ssp", (128, 512))
            ssp_b = P.buf("ssp")
        if HL:
            hal = ps("hal", (128, 512))
            hal_b = P.bufs(8, "hal")

        if has_x:
            P.op("pool", lambda e: e.memset(ones[:, :], 1.0), writes=[ones_b])
            P.op("pool", lambda e: e.memset(eps_t[:, :], NORM_EPS), writes=[eps_b])
        for k, (t, b, d) in gvec.items():
            P.op("sp", lambda e, t=t, d=d: e.dma_start(out=t[:, :], in_=d.ap()), writes=[b], dma=True)
        if ffn:
            P.op("sp", lambda e: e.dma_start(out=cw_s[:, :, :], in_=ffn["cw"].ap()), writes=[cwb], dma=True)
            P.op("sp", lambda e: e.dma_start(out=cb_s[:, :], in_=ffn["cb"].ap()), writes=[cwb], dma=True)

        cnt = {"w": 0, "acc": 0, "ev": 0, "sq": 0, "hal": 0, "uc": 0, "ca": 0, "wtm": 0}

        def load_w(dram_ap_2d, nk):
            i = cnt["w"] % NWB
            cnt["w"] += 1
            n = nk * 128
            bsz = _split_last(n)
            s_ = dram_ap_2d.rearrange("p (a b) -> p a b", b=bsz)
            dst = wt[i][:, 0:nk, :].rearrange("p k j -> p (k j)").rearrange("p (a b) -> p a b", b=bsz)
            P.op("pool", lambda e: e.dma_start(out=dst, in_=s_), writes=[wt_b[i]], dma=True)
            return i

        def load_wtm(dram_ap_2d):
            i = cnt["wtm"] % 2
            cnt["wtm"] += 1
            hk = KC // 2
            for h in range(2):
                s_ = dram_ap_2d[:, h * hk * 512:(h + 1) * hk * 512].rearrange("p (a b) -> p a b", b=512)
                dst = wtm[i][:, h * hk:(h + 1) * hk, :]
                P.op("pool", lambda e, dst=dst, s_=s_: e.dma_start(out=dst, in_=s_), writes=[wtm_b[i]], dma=True)
            return i

        def next_acc():
            i = cnt["acc"] % NPS
            cnt["acc"] += 1
            return i

        def compute_rstd(do_halo=True):
            hs = None
            do_halo = do_halo and HL
            if do_halo:
                hs = cnt["hal"] % 8
                cnt["hal"] += 1
            for kc in range(KC):
                i = cnt["sq"] % 2
                cnt["sq"] += 1
                P.op("act", lambda e, kc=kc, i=i: e.activation(out=sq[i][:, :], in_=x1[:, kc, :], func=AF.Square),
                     reads=[x1_b[kc]], writes=[sq_b[i]])
                P.op("pe", lambda e, kc=kc, i=i: e.matmul(ssp[:, 0:TT], ones[:, :], sq[i][:, HL:TW],
                                                          start=(kc == 0), stop=(kc == KC - 1)),
                     reads=[ones_b, sq_b[i]], writes=[ssp_b])
                if do_halo:
                    P.op("pe", lambda e, kc=kc, i=i, hs=hs: e.matmul(hal[:, hs * HL:(hs + 1) * HL], ones[:, :], sq[i][:, 0:HL],
                                                                     start=(kc == 0), stop=(kc == KC - 1)),
                         reads=[ones_b, sq_b[i]], writes=[hal_b[hs]])
            P.op("act", lambda e: e.activation(out=rstd[:, HL:TW], in_=ssp[:, 0:TT], func=AF.Sqrt,
                                               bias=eps_t[:, 0:1], scale=1.0 / D),
                 reads=[ssp_b, eps_b], writes=[rstd_b])
            if do_halo:
                P.op("act", lambda e, hs=hs: e.activation(out=rstd[:, 0:HL], in_=hal[:, hs * HL:(hs + 1) * HL], func=AF.Sqrt,
                                                          bias=eps_t[:, 0:1], scale=1.0 / D),
                     reads=[hal_b[hs], eps_b], writes=[rstd_b])
            P.op("dve", lambda e: e.reciprocal(out=rstd[:, :], in_=rstd[:, :]), reads=[rstd_b], writes=[rstd_b])

        def rmsnorm_to_hn(gkey, do_halo=True):
            gt_, gb_, _ = gvec[gkey]
            compute_rstd(do_halo)
            for kc in range(KC):
                P.op("dve", lambda e, kc=kc: e.scalar_tensor_tensor(
                    out=hn[:, kc, :], in0=x1[:, kc, :], scalar=gt_[:, kc:kc + 1], in1=rstd[:, :],
                    op0=ALU.mult, op1=ALU.mult),
                    reads=[x1_b[kc], gb_, rstd_b], writes=[hn_b[kc]])

        def gemm(wslot, nk, rhs_fn, rhs_bufs, acc_i, halo_slot=None, halo_rhs_fn=None):
            for k in range(nk):
                P.op("pe", lambda e, k=k: e.matmul(acc[acc_i][:, 0:TT], wt[wslot][:, k, :], rhs_fn(k),
                                                   start=(k == 0), stop=(k == nk - 1)),
                     reads=[wt_b[wslot], rhs_bufs[k]], writes=[acc_b[acc_i]])
            if halo_slot is not None:
                for k in range(nk):
                    P.op("pe", lambda e, k=k: e.matmul(hal[:, halo_slot * HL:(halo_slot + 1) * HL],
                                                       wt[wslot][:, k, :], halo_rhs_fn(k),
                                                       start=(k == 0), stop=(k == nk - 1)),
                         reads=[wt_b[wslot], rhs_bufs[k]], writes=[hal_b[halo_slot]])

        for tt in range(NT):
            first = tt == 0
            if has_x:
                for kc in range(KC):
                    P.op("sp", lambda e, kc=kc, tt=tt: e.dma_start(out=x1[:, kc, :], in_=cfg["x_ap"](tt, kc)),
                         writes=[x1_b[kc]], dma=True)
            else:
                for (k0, nk, ap) in cfg["hn_load"](tt):
                    P.op("sp", lambda e, k0=k0, nk=nk, ap=ap: e.dma_start(
                        out=hn[:, k0:k0 + nk, :], in_=ap.rearrange("(k p) t -> p k t", p=128)),
                        writes=hn_b[k0:k0 + nk], dma=True)
            if mix:
                P.op("sp", lambda e, tt=tt: e.dma_start(out=hb[:, :, HL:TW],
                                                        in_=mix["main"](e, tt).rearrange("(k p) t -> p k t", p=128)),
                     writes=hb_b, dma=True)
                P.op("sp", lambda e, tt=tt: e.dma_start(out=hb[:, :, 0:HL],
                                                        in_=mix["halo"](e, tt).rearrange("(k p) t -> p k t", p=128)),
                     writes=hb_b, dma=True)
                for dc in range(KC):
                    ws = load_w(mix["w"][dc].rearrange("p k j -> p (k j)"), KC)
                    ai = next_acc()
                    hs = None
                    if HL and first:
                        hs = cnt["hal"] % 8
                        cnt["hal"] += 1
                    gemm(ws, KC, lambda k: hb[:, k, HL:TW], hb_b, ai, hs, lambda k: hb[:, k, 0:HL])
                    P.op("dve", lambda e, dc=dc, ai=ai: e.tensor_add(out=x1[:, dc, HL:TW], in0=x1[:, dc, HL:TW], in1=acc[ai][:, 0:TT]),
                         reads=[acc_b[ai], x1_b[dc]], writes=[x1_b[dc]])
                    if HL and first:
                        P.op("dve", lambda e, dc=dc, hs=hs: e.tensor_add(out=x1[:, dc, 0:HL], in0=x1[:, dc, 0:HL],
                                                                        in1=hal[:, hs * HL:(hs + 1) * HL]),
                             reads=[hal_b[hs], x1_b[dc]], writes=[x1_b[dc]])
            if ffn:
                w_up, w_dn = ffn["w_up"], ffn["w_dn"]
                rmsnorm_to_hn("ffn", first)
                for g in range(NG):
                    for fl in range(gsz[g]):
                        fc = gst[g] + fl
                        res = []
                        for half in range(2):
                            col = fc + half * FC
                            ws = load_w(w_up[col].rearrange("p k j -> p (k j)"), KC)
                            ai = next_acc()
                            hs = None
                            if first:
                                hs = cnt["hal"] % 8
                                cnt["hal"] += 1
                            gemm(ws, KC, lambda k: hn[:, k, HL:TW], hn_b, ai, hs, lambda k: hn[:, k, 0:HL])
                            ui = cnt["uc"] % 4
                            cnt["uc"] += 1
                            ci = cnt["ca"] % 4
                            cnt["ca"] += 1
                            P.op("act", lambda e, ui=ui, ai=ai: e.copy(out=ucat[ui][:, HL:TW], in_=acc[ai][:, 0:TT]),
                                 reads=[acc_b[ai]], writes=[ucat_b[ui]])
                            if first:
                                P.op("act", lambda e, ui=ui, hs=hs: e.copy(out=ucat[ui][:, 0:HL], in_=hal[:, hs * HL:(hs + 1) * HL]),
                                     reads=[hal_b[hs]], writes=[ucat_b[ui]])
                            else:
                                P.op("act", lambda e, ui=ui, col=col: e.copy(out=ucat[ui][:, 0:HL], in_=sv[:, col, :]),
                                     reads=[sv_b[col]], writes=[ucat_b[ui]])
                            if tt < NT - 1:
                                P.op("dve", lambda e, ui=ui, col=col: e.tensor_copy(out=sv[:, col, :], in_=ucat[ui][:, TW - HL:TW]),
                                     reads=[ucat_b[ui]], writes=[sv_b[col]])
                            P.op("act", lambda e, ai=ai, ci=ci, col=col: e.activation(
                                out=cacc[ci][:, :], in_=acc[ai][:, 0:TT], func=AF.Identity,
                                scale=cw_s[:, 2, col:col + 1], bias=cb_s[:, col:col + 1]),
                                reads=[acc_b[ai], cwb], writes=[cacc_b[ci]])
                            P.op("dve", lambda e, ui=ui, ci=ci, col=col: e.scalar_tensor_tensor(
                                out=cacc[ci][:, :], in0=ucat[ui][:, 1:TW - 1], scalar=cw_s[:, 1, col:col + 1],
                                in1=cacc[ci][:, :], op0=ALU.mult, op1=ALU.add),
                                reads=[ucat_b[ui], cwb, cacc_b[ci]], writes=[cacc_b[ci]])
                            P.op("dve", lambda e, ui=ui, ci=ci, col=col: e.scalar_tensor_tensor(
                                out=cacc[ci][:, :], in0=ucat[ui][:, 0:TW - 2], scalar=cw_s[:, 0, col:col + 1],
                                in1=cacc[ci][:, :], op0=ALU.mult, op1=ALU.add),
                                reads=[ucat_b[ui], cwb, cacc_b[ci]], writes=[cacc_b[ci]])
                            res.append(ci)
                        cg, cu = res
                        P.op("act", lambda e, cg=cg: e.activation(out=cacc[cg][:, :], in_=cacc[cg][:, :], func=AF.Silu),
                             reads=[cacc_b[cg]], writes=[cacc_b[cg]])
                        P.op("dve", lambda e, cg=cg, cu=cu, fl=fl: e.tensor_tensor(
                            out=gt[:, fl, :], in0=cacc[cg][:, :], in1=cacc[cu][:, :], op=ALU.mult),
                            reads=[cacc_b[cg], cacc_b[cu]], writes=[gt_b[fl]])
                    for dc in range(KC):
                        ws = load_w(w_dn[dc, :, gst[g]:gst[g] + gsz[g], :].rearrange("p k j -> p (k j)"), gsz[g])
                        ai = next_acc()
                        gemm(ws, gsz[g], lambda k: gt[:, k, :], gt_b, ai)
                        P.op("dve", lambda e, dc=dc, ai=ai: e.tensor_add(out=x1[:, dc, HL:TW], in0=x1[:, dc, HL:TW], in1=acc[ai][:, 0:TT]),
                             reads=[acc_b[ai], x1_b[dc]], writes=[x1_b[dc]])
            if cfg.get("after_ffn"):
                cfg["after_ffn"](tt, x1, x1_b, HL, TW)
            if nxt:
                if nxt.get("g") is not None:
                    rmsnorm_to_hn("nxt", False)
                if nxt.get("hn_store"):
                    nxt["hn_store"](tt, hn, hn_b, HL, TW)
                for oc, (w_ap, out_fn) in enumerate(fm):
                    ws = load_w(w_ap, KC)
                    ai = next_acc()
                    gemm(ws, KC, lambda k: hn[:, k, HL:TW], hn_b, ai)
                    i = cnt["ev"] % NEV
                    cnt["ev"] += 1
                    if oc % 2 == 0:
                        P.op("act", lambda e, i=i, ai=ai: e.copy(out=ev[i][:, 0:TT], in_=acc[ai][:, 0:TT]),
                             reads=[acc_b[ai]], writes=[ev_b[i]])
                    else:
                        P.op("dve", lambda e, i=i, ai=ai: e.tensor_copy(out=ev[i][:, 0:TT], in_=acc[ai][:, 0:TT]),
                             reads=[acc_b[ai]], writes=[ev_b[i]])
                    for (dst, s_) in out_fn(tt, ev[i]):
                        P.op("sp", lambda e, dst=dst, s_=s_: e.dma_start(out=dst, in_=s_), reads=[ev_b[i]], dma=True)
                for gi_, (w_ap, ncols, out_fn) in enumerate(tm):
                    wsl = load_wtm(w_ap)
                    for tb in range(TT // 128):
                        ai = next_acc()
                        for k in range(KC):
                            P.op("pe", lambda e, k=k, ai=ai, tb=tb, wsl=wsl, ncols=ncols: e.matmul(
                                acc[ai][:, 0:ncols], hn[:, k, HL + tb * 128:HL + (tb + 1) * 128], wtm[wsl][:, k, 0:ncols],
                                start=(k == 0), stop=(k == KC - 1)),
                                reads=[wtm_b[wsl], hn_b[k]], writes=[acc_b[ai]])
                        i = cnt["ev"] % NEV
                        cnt["ev"] += 1
                        if (gi_ + tb) % 2 == 0:
                            P.op("act", lambda e, i=i, ai=ai, ncols=ncols: e.copy(out=ev[i][:, 0:ncols], in_=acc[ai][:, 0:ncols]),
                                 reads=[acc_b[ai]], writes=[ev_b[i]])
                        else:
                            P.op("dve", lambda e, i=i, ai=ai, ncols=ncols: e.tensor_copy(out=ev[i][:, 0:ncols], in_=acc[ai][:, 0:ncols]),
                                 reads=[acc_b[ai]], writes=[ev_b[i]])
                        dst = out_fn(tt, tb)
                        P.op("sp", lambda e, dst=dst, i=i, ncols=ncols: e.dma_start(out=dst, in_=ev[i][:, 0:ncols]),
                             reads=[ev_b[i]], dma=True)
            if final:
                gt_, gb_, _ = gvec["fin"]
                compute_rstd(False)
                for kc in range(KC):
                    i = cnt["ev"] % NEV
                    cnt["ev"] += 1
                    P.op("dve", lambda e, kc=kc, i=i: e.scalar_tensor_tensor(
                        out=ev[i][:, 0:TT], in0=x1[:, kc, HL:TW], scalar=gt_[:, kc:kc + 1], in1=rstd[:, HL:TW],
                        op0=ALU.mult, op1=ALU.mult),
                        reads=[x1_b[kc], gb_, rstd_b], writes=[ev_b[i]])
                    P.op("sp", lambda e, kc=kc, i=i, tt=tt: e.dma_start(out=final["out"](tt, kc), in_=ev[i][:, 0:TT]),
                         reads=[ev_b[i]], dma=True)
        P.end_phase()
        print("dense phase", cfg.get("name"), "sbuf bytes remaining", nc.sbuf_bytes_remaining, flush=True)

def emit_mlstm(nc, P, cfg, io):
    S, NP, TT, KCD = cfg["S"], cfg["NP"], cfg["TT"], cfg["KC"]
    L, DK, DV, GC = 64, 256, 512, 4
    NCH = S // L
    NGR = NCH // GC
    GT = GC * L
    KSC = DK ** -0.5
    CAP = 15.0
    assert NCH <= 128 and TT % GT == 0
    qT, kT, ktm, vtm, otm, gi, gf = (io[k] for k in ("qT", "kT", "ktm", "vtm", "otm", "gi", "gf"))
    gbias, gain, negmask, ident = (io[k] for k in ("gbias", "gain", "negmask", "ident"))
    gs_u, gs_a, gs_ao = io["gs_u"], io["gs_a"], io["gs_ao"]
    srcH, dstH = io["srcH"], io["dstH"]
    NPC = S // TT

    with ExitStack() as st:

        def sb(name, shape, dt=F32):
            return st.enter_context(nc.sbuf_tensor("ml_" + name, list(shape), dt))

        def ps(name, shape, dt=F32):
            return st.enter_context(nc.psum_tensor("ml_" + name, list(shape), dt))

        idt = sb("idt", (128, 128)); idt_b = P.buf("idt")
        nm = sb("nm", (64, 64)); nm_b = P.buf("nm")
        gb = sb("gb", (128, NP * 2)); gb_b = P.buf("gb")
        gn = sb("gn", (64, NP * DV)); gn_b = P.buf("gn")
        one = sb("one", (128, 64)); one_b = P.buf("one")
        onec = sb("onec", (64, 1), BF16); onec_b = P.buf("onec")
        epsc = sb("epsc", (128, 1)); epsc_b = P.buf("epsc")
        P.op("sp", lambda e: e.dma_start(out=idt[:, :], in_=ident.ap()), writes=[idt_b], dma=True)
        P.op("sp", lambda e: e.dma_start(out=nm[:, :], in_=negmask.ap()), writes=[nm_b], dma=True)
        P.op("sp", lambda e: e.dma_start(out=gb[:, :], in_=gbias.ap()), writes=[gb_b], dma=True)
        P.op("sp", lambda e: e.dma_start(out=gn[:, :], in_=gain.ap()), writes=[gn_b], dma=True)
        P.op("pool", lambda e: e.memset(one[:, :], 1.0), writes=[one_b])
        P.op("pool", lambda e: e.memset(onec[:, :], 1.0), writes=[onec_b])
        P.op("pool", lambda e: e.memset(epsc[:, :], NORM_EPS), writes=[epsc_b])
        idb = sb("idb", (64, 64), BF16); idb_b = P.buf("idb")
        P.op("act", lambda e: e.copy(out=idb[:, :], in_=idt[0:64, 0:64]), reads=[idt_b], writes=[idb_b])
        zt = sb("zt", (128, KCD * 2), BF16); zt_b = P.buf("zt")
        P.op("pool", lambda e: e.memset(zt[:, :], 0.0), writes=[zt_b])
        dstH_b = P.buf("dstH")
        srcH_b = P.bufs(NPC, "srcH")
        P.op("sp", lambda e: e.dma_start(out=dstH.ap()[0:KCD * 128, TT - 2:TT].rearrange("(k p) c -> p k c", p=128),
                                         in_=zt[:, :].rearrange("p (k c) -> p k c", c=2)),
             reads=[zt_b], writes=[dstH_b], dma=True)
        if io.get("dstB") is not None:
            dstB_b0 = P.buf("dstB0")
            P.op("pool", lambda e: e.dma_start(out=io["dstB"].ap()[0:KCD * 128, :].rearrange("(k p) c -> p k c", p=128),
                                               in_=zt[:, :].rearrange("p (k c) -> p k c", c=2)),
                 reads=[zt_b], writes=[dstB_b0], dma=True)

        ps_s = [ps(f"ps_s{i}", (64, 64)) for i in range(2)]; ps_s_b = P.bufs(2, "pss")
        ps_num = [ps(f"ps_num{i}", (64, 512)) for i in range(2)]; ps_num_b = P.bufs(2, "psn")
        ps_c = [ps(f"ps_c{i}", (128, 512)) for i in range(2)]; ps_c_b = P.bufs(2, "psc")
        ps_sm = ps("ps_sm", (128, 512))
        psT = ps("psT", (128, 4, GT), BF16); psT_b = P.buf("psT")
        ps_den_b = P.bufs(2, "psd")
        ps_n_b = P.bufs(2, "psnn")

        gi_t = sb("gi_t", (NCH, NP, L)); gf_t = sb("gf_t", (NCH, NP, L))
        G = P.buf("gates")
        for p in range(NP):
            P.op("sp", lambda e, p=p: e.dma_start(out=gi_t[:, p, :], in_=gi[p].rearrange("(c j) -> c j", j=L)), writes=[G], dma=True)
            P.op("sp", lambda e, p=p: e.dma_start(out=gf_t[:, p, :], in_=gf[p].rearrange("(c j) -> c j", j=L)), writes=[G], dma=True)
        ti = sb("ti", (NCH, NP, L)); tf = sb("tf", (NCH, NP, L))
        e1 = sb("e1", (NCH, NP, L)); lfn = sb("lfn", (NCH, NP, L))
        bb = sb("bb", (NCH, NP, L)); ww = sb("ww", (NCH, NP, L)); cm = sb("cm", (NCH, NP, L))
        uu = sb("uu", (NCH, NP, L)); aa = sb("aa", (NCH, NP, L)); ub = sb("ub", (NCH, NP, L))
        enm = sb("enm", (NCH, NP, L)); wi = sb("wi", (NCH, NP, L))
        bl = sb("bl", (NCH, NP)); md = sb("md", (NCH, NP)); mn = sb("mn", (NCH, NP)); mp = sb("mp", (NCH, NP))
        bmn = sb("bmn", (NCH, NP)); aot = sb("aot", (NCH, NP)); ao = sb("ao", (NCH, NP))
        blT = sb("blT", (NP, NCH)); mdT = sb("mdT", (NP, NCH)); mnT = sb("mnT", (NP, NCH)); mpT = sb("mpT", (NP, NCH))
        w_col = sb("w_col", (64, NP, NCH)); wik_col = sb("wik_col", (64, NP, NCH)); enm_col = sb("enm_col", (64, NP, NCH))

        GS = P.buf("gs_dram")

        def g_op(eng, fn, extra_r=()):
            P.op(eng, fn, reads=[G] + list(extra_r), writes=[G])

        for p in range(NP):
            g_op("dve", lambda e, p=p: e.tensor_scalar(out=ti[:, p, :], in0=gi_t[:, p, :], scalar1=gb[0:NCH, 2 * p:2 * p + 1],
                                                       scalar2=1.0 / CAP, op0=ALU.add, op1=ALU.mult), [gb_b])
            g_op("dve", lambda e, p=p: e.tensor_scalar(out=tf[:, p, :], in0=gf_t[:, p, :], scalar1=gb[0:NCH, 2 * p + 1:2 * p + 2],
                                                       scalar2=1.0 / CAP, op0=ALU.add, op1=ALU.mult), [gb_b])
        g_op("act", lambda e: e.activation(out=ti[:, :, :], in_=ti[:, :, :], func=AF.Tanh))
        g_op("act", lambda e: e.activation(out=tf[:, :, :], in_=tf[:, :, :], func=AF.Tanh))
        g_op("act", lambda e: e.activation(out=e1[:, :, :], in_=tf[:, :, :], func=AF.Exp, scale=-CAP))
        g_op("act", lambda e: e.activation(out=lfn[:, :, :], in_=e1[:, :, :], func=AF.Ln, bias=one[0:NCH, 0:1], scale=1.0), [one_b])
        for p in range(NP):
            g_op("dve", lambda e, p=p: e.tensor_tensor_scan(out=bb[:, p, :], data0=one[0:NCH, 0:L], data1=lfn[:, p, :], initial=0.0,
                                                            op0=ALU.mult, op1=ALU.subtract), [one_b])
            g_op("dve", lambda e, p=p: e.scalar_tensor_tensor(out=ww[:, p, :], in0=ti[:, p, :], scalar=CAP, in1=bb[:, p, :],
                                                              op0=ALU.mult, op1=ALU.subtract))
            g_op("dve", lambda e, p=p: e.tensor_tensor_scan(out=cm[:, p, :], data0=ww[:, p, :], data1=ww[:, p, :], initial=-1e30,
                                                            op0=ALU.max, op1=ALU.max))
            g_op("dve", lambda e, p=p: e.tensor_copy(out=bl[:, p:p + 1], in_=bb[:, p, L - 1:L]))
            g_op("dve", lambda e, p=p: e.tensor_tensor(out=md[:, p:p + 1], in0=bb[:, p, L - 1:L], in1=cm[:, p, L - 1:L], op=ALU.add))
        g_op("pe", lambda e: e.transpose(ps_num[0][0:NP, 0:NCH], bl[:, :], idt[0:NCH, 0:NCH]), [idt_b])
        g_op("dve", lambda e: e.tensor_copy(out=blT[:, :], in_=ps_num[0][0:NP, 0:NCH]))
        g_op("pe", lambda e: e.transpose(ps_num[0][0:NP, 0:NCH], md[:, :], idt[0:NCH, 0:NCH]), [idt_b])
        g_op("dve", lambda e: e.tensor_copy(out=mdT[:, :], in_=ps_num[0][0:NP, 0:NCH]))
        g_op("dve", lambda e: e.tensor_tensor_scan(out=mnT[:, :], data0=blT[:, :], data1=mdT[:, :], initial=0.0,
                                                   op0=ALU.add, op1=ALU.max))
        g_op("dve", lambda e: e.memset(mpT[:, :], 0.0))
        if NCH > 1:
            g_op("dve", lambda e: e.tensor_copy(out=mpT[:, 1:NCH], in_=mnT[:, 0:NCH - 1]))
        g_op("pe", lambda e: e.transpose(ps_c[0][0:NCH, 0:NP], mnT[:, :], idt[0:NP, 0:NP]), [idt_b])
        g_op("dve", lambda e: e.tensor_copy(out=mn[:, :], in_=ps_c[0][0:NCH, 0:NP]))
        g_op("pe", lambda e: e.transpose(ps_c[0][0:NCH, 0:NP], mpT[:, :], idt[0:NP, 0:NP]), [idt_b])
        g_op("dve", lambda e: e.tensor_copy(out=mp[:, :], in_=ps_c[0][0:NCH, 0:NP]))
        g_op("dve", lambda e: e.tensor_tensor(out=bmn[:, :], in0=bl[:, :], in1=mn[:, :], op=ALU.subtract))
        g_op("dve", lambda e: e.tensor_tensor(out=aot[:, :], in0=mp[:, :], in1=bmn[:, :], op=ALU.add))
        g_op("act", lambda e: e.activation(out=ao[:, :], in_=aot[:, :], func=AF.Exp))
        for p in range(NP):
            g_op("dve", lambda e, p=p: e.tensor_scalar(out=uu[:, p, :], in0=cm[:, p, :], scalar1=mp[:, p:p + 1], scalar2=-1.0,
                                                       op0=ALU.max, op1=ALU.mult))
            g_op("act", lambda e, p=p: e.activation(out=aa[:, p, :], in_=uu[:, p, :], func=AF.Exp, bias=mp[:, p:p + 1], scale=1.0))
            g_op("dve", lambda e, p=p: e.tensor_tensor(out=ub[:, p, :], in0=uu[:, p, :], in1=bb[:, p, :], op=ALU.subtract))
            g_op("act", lambda e, p=p: e.activation(out=enm[:, p, :], in_=ub[:, p, :], func=AF.Exp))
            g_op("act", lambda e, p=p: e.activation(out=wi[:, p, :], in_=ww[:, p, :], func=AF.Exp, bias=bmn[:, p:p + 1], scale=1.0))
            for src, dst, scl in ((ww, w_col, 1.0), (wi, wik_col, KSC), (enm, enm_col, 1.0)):
                g_op("pe", lambda e, p=p, src=src: e.transpose(ps_num[0][0:L, 0:NCH], src[:, p, :], idt[0:NCH, 0:NCH]), [idt_b])
                g_op("act", lambda e, p=p, dst=dst, scl=scl: e.mul(out=dst[:, p, :], in_=ps_num[0][0:L, 0:NCH], mul=scl))
            P.op("sp", lambda e, p=p: e.dma_start(out=gs_u.ap()[p].rearrange("(c j) -> c j", j=L), in_=uu[:, p, :]),
                 reads=[G], writes=[GS], dma=True, sem_buf=GS)
            P.op("sp", lambda e, p=p: e.dma_start(out=gs_a.ap()[p].rearrange("(c j) -> c j", j=L), in_=aa[:, p, :]),
                 reads=[G], writes=[GS], dma=True, sem_buf=GS)
            P.op("sp", lambda e, p=p: e.dma_start(out=gs_ao.ap()[p].rearrange("(c o) -> c o", o=1), in_=ao[:, p:p + 1]),
                 reads=[G], writes=[GS], dma=True, sem_buf=GS)

        ao_bc = sb("ao_bc", (128, NCH)); ao_bc_b = P.buf("aobc")
        Cst = sb("Cst", (128, 2, DV)); Cst_b = P.buf("Cst")
        nst = sb("nst", (128, 2)); nst_b = P.buf("nst")
        Cbf = [sb(f"Cbf{i}", (128, 2, DV), BF16) for i in range(2)]; Cbf_b = P.bufs(2, "Cbf")
        nbf = [sb(f"nbf{i}", (128, 2), BF16) for i in range(2)]; nbf_b = P.bufs(2, "nbf")
        qg = [sb(f"qg{i}", (128, 2, GT)) for i in range(2)]; qg_b = P.bufs(2, "qg")
        kg = [sb(f"kg{i}", (128, 2, GT)) for i in range(2)]; kg_b = P.bufs(2, "kg")
        Ag = [sb(f"Ag{i}", (128, GT)) for i in range(2)]; Ag_b = P.bufs(2, "Ag")
        ug = [sb(f"ug{i}", (64, GT)) for i in range(2)]; ug_b = P.bufs(2, "ug")
        ktg = [sb(f"ktg{i}", (64, GC, DK)) for i in range(2)]; ktg_b = P.bufs(2, "ktg")
        vg = [sb(f"vg{i}", (64, GC, DV)) for i in range(2)]; vg_b = P.bufs(2, "vg")
        og = [sb(f"og{i}", (64, GC, DV)) for i in range(2)]; og_b = P.bufs(2, "og")
        qb = [sb(f"qb{i}", (128, 2, GT), BF16) for i in range(2)]; qb_b = P.bufs(2, "qb")
        kb = [sb(f"kb{i}", (128, 2, GT), BF16) for i in range(2)]; kb_b = P.bufs(2, "kb")
        qtb = [sb(f"qtb{i}", (128, 2, GT), BF16) for i in range(2)]; qtb_b = P.bufs(2, "qtb")
        vb = [sb(f"vb{i}", (64, GC, DV), BF16) for i in range(2)]; vb_b = P.bufs(2, "vb")
        kwb = [sb(f"kwb{i}", (64, GC, DK), BF16) for i in range(2)]; kwb_b = P.bufs(2, "kwb")
        sg = [sb(f"sg{i}", (64, GC, DV)) for i in range(2)]; sg_b = P.bufs(2, "sg")
        hog = [sb(f"hog{i}", (64, GC, DV), BF16) for i in range(2)]; hog_b = P.bufs(2, "hog")
        hTs = [sb(f"hTs{i}", (128, 4, GT), BF16) for i in range(2)]; hTs_b = P.bufs(2, "hTs")
        X = [sb(f"X{i}", (64, 64)) for i in range(2)]; X_b = P.bufs(2, "X")
        E = [sb(f"E{i}", (64, 64)) for i in range(2)]; E_b = P.bufs(2, "E")
        sDT = [sb(f"sDT{i}", (64, 64), BF16) for i in range(2)]; sDT_b = P.bufs(2, "sDT")
        dn = [sb(f"dn{i}", (64, 1)) for i in range(2)]; dn_b = P.bufs(2, "dn")
        hraw = [sb(f"hraw{i}", (64, DV)) for i in range(2)]; hraw_b = P.bufs(2, "hraw")
        junk = [sb(f"junk{i}", (64, DV)) for i in range(2)]; junk_b = P.bufs(2, "junk")
        ss = [sb(f"ss{i}", (64, 1)) for i in range(2)]; ss_b = P.bufs(2, "ss")
        hn_ = [sb(f"hn_{i}", (64, DV)) for i in range(2)]; hn_b = P.bufs(2, "hn")

        def bc_ap(t, off, nparts, n):
            return bass.AP(t, off, [[0, nparts], [1, n]])

        def load_group(p, g):
            gg = p * NGR + g
            i = gg % 2
            t0 = g * GT
            P.op("sp", lambda e: e.dma_start(out=qg[i][:, :, :], in_=qT[p, :, t0:t0 + GT].rearrange("(k d) t -> d k t", d=128)),
                 writes=[qg_b[i]], dma=True)
            P.op("sp", lambda e: e.dma_start(out=kg[i][:, :, :], in_=kT[p, :, t0:t0 + GT].rearrange("(k d) t -> d k t", d=128)),
                 writes=[kg_b[i]], dma=True)
            P.op("sp", lambda e: e.dma_start(out=Ag[i][:, :], in_=bc_ap(gs_a, p * S + t0, 128, GT)), reads=[GS], writes=[Ag_b[i]], dma=True)
            P.op("sp", lambda e: e.dma_start(out=ug[i][:, :], in_=bc_ap(gs_u, p * S + t0, 64, GT)), reads=[GS], writes=[ug_b[i]], dma=True)
            P.op("sp", lambda e: e.dma_start(out=ktg[i][:, :, :], in_=ktm[p, t0:t0 + GT, :].rearrange("(n i) d -> i n d", i=L)),
                 writes=[ktg_b[i]], dma=True)
            P.op("sp", lambda e: e.dma_start(out=vg[i][:, :, :], in_=vtm[p, t0:t0 + GT, :].rearrange("(n i) d -> i n d", i=L)),
                 writes=[vg_b[i]], dma=True)
            P.op("sp", lambda e: e.dma_start(out=og[i][:, :, :], in_=otm[p, t0:t0 + GT, :].rearrange("(n i) d -> i n d", i=L)),
                 writes=[og_b[i]], dma=True)

        def prep_group(p, g):
            gg = p * NGR + g
            i = gg % 2
            P.op("act", lambda e: e.copy(out=qb[i][:, :, :], in_=qg[i][:, :, :]), reads=[qg_b[i]], writes=[qb_b[i]])
            P.op("pool", lambda e: e.tensor_copy(out=kb[i][:, :, :], in_=kg[i][:, :, :]), reads=[kg_b[i]], writes=[kb_b[i]])
            for k in range(2):
                P.op("dve", lambda e, k=k: e.tensor_tensor(out=qtb[i][:, k, :], in0=qg[i][:, k, :], in1=Ag[i][:, :], op=ALU.mult),
                     reads=[qg_b[i], Ag_b[i]], writes=[qtb_b[i]])
            P.op("pool", lambda e: e.tensor_copy(out=vb[i][:, :, :], in_=vg[i][:, :, :]), reads=[vg_b[i]], writes=[vb_b[i]])
            for n in range(GC):
                c = g * GC + n
                P.op("act", lambda e, n=n, c=c: e.activation(out=kwb[i][:, n, :], in_=ktg[i][:, n, :], func=AF.Identity,
                                                             scale=wik_col[:, p, c:c + 1]),
                     reads=[ktg_b[i], G], writes=[kwb_b[i]])
            P.op("act", lambda e: e.activation(out=sg[i][:, :, :], in_=og[i][:, :, :], func=AF.Sigmoid), reads=[og_b[i]], writes=[sg_b[i]])

        def stage_S(p, c, n_glob):
            g, n = divmod(c, GC)
            i = (p * NGR + g) % 2
            j = n_glob % 2
            j0 = n * L
            for k in range(2):
                P.op("pe", lambda e, k=k: e.matmul(ps_s[j][:, :], kb[i][:, k, j0:j0 + L], qb[i][:, k, j0:j0 + L], start=(k == 0), stop=(k == 1)),
                     reads=[kb_b[i], qb_b[i]], writes=[ps_s_b[j]])
            P.op("dve", lambda e: e.scalar_tensor_tensor(out=X[j][:, :], in0=ug[i][:, j0:j0 + L], scalar=w_col[:, p, c:c + 1], in1=nm[:, :],
                                                         op0=ALU.add, op1=ALU.add),
                 reads=[ug_b[i], G, nm_b], writes=[X_b[j]])
            P.op("act", lambda e: e.activation(out=E[j][:, :], in_=X[j][:, :], func=AF.Exp), reads=[X_b[j]], writes=[E_b[j]])
            P.op("dve", lambda e: e.scalar_tensor_tensor(out=sDT[j][:, :], in0=ps_s[j][:, :], scalar=KSC, in1=E[j][:, :],
                                                         op0=ALU.mult, op1=ALU.mult),
                 reads=[ps_s_b[j], E_b[j]], writes=[sDT_b[j]])

        def stage_H(p, c, n_glob):
            g, n = divmod(c, GC)
            i = (p * NGR + g) % 2
            j = n_glob % 2
            cb = c % 2
            j0 = n * L
            for k in range(2):
                P.op("pe", lambda e, k=k: e.matmul(ps_num[j][:, :], qtb[i][:, k, j0:j0 + L], Cbf[cb][:, k, :], start=(k == 0), stop=False),
                     reads=[qtb_b[i], Cbf_b[cb]], writes=[ps_num_b[j]])
            P.op("pe", lambda e: e.matmul(ps_num[j][:, :], sDT[j][:, :], vb[i][:, n, :], start=False, stop=True),
                 reads=[sDT_b[j], vb_b[i]], writes=[ps_num_b[j]])
            for k in range(2):
                P.op("pe", lambda e, k=k: e.matmul(ps_sm[0:64, j:j + 1], qtb[i][:, k, j0:j0 + L], nbf[cb][:, k:k + 1], start=(k == 0), stop=False),
                     reads=[qtb_b[i], nbf_b[cb]], writes=[ps_den_b[j]])
            P.op("pe", lambda e: e.matmul(ps_sm[0:64, j:j + 1], sDT[j][:, :], onec[:, :], start=False, stop=True),
                 reads=[sDT_b[j], onec_b], writes=[ps_den_b[j]])
            for k in range(2):
                P.op("pe", lambda e, k=k: e.matmul(ps_c[k][:, :], kwb[i][:, n, k * 128:(k + 1) * 128], vb[i][:, n, :], start=True, stop=True),
                     reads=[kwb_b[i], vb_b[i]], writes=[ps_c_b[k]])
                P.op("pe", lambda e, k=k: e.matmul(ps_sm[:, 2 + k:3 + k], kwb[i][:, n, k * 128:(k + 1) * 128], onec[:, :], start=True, stop=True),
                     reads=[kwb_b[i], onec_b], writes=[ps_n_b[k]])
            nb = 1 - cb
            for k in range(2):
                P.op("dve", lambda e, k=k: e.scalar_tensor_tensor(out=Cst[:, k, :], in0=Cst[:, k, :], scalar=ao_bc[:, c:c + 1], in1=ps_c[k][:, :],
                                                                  op0=ALU.mult, op1=ALU.add),
                     reads=[Cst_b, ao_bc_b, ps_c_b[k]], writes=[Cst_b])
                P.op("dve", lambda e, k=k: e.scalar_tensor_tensor(out=nst[:, k:k + 1], in0=nst[:, k:k + 1], scalar=ao_bc[:, c:c + 1],
                                                                  in1=ps_sm[:, 2 + k:3 + k], op0=ALU.mult, op1=ALU.add),
                     reads=[nst_b, ao_bc_b, ps_n_b[k]], writes=[nst_b])
            P.op("act", lambda e: e.copy(out=Cbf[nb][:, 0, :], in_=Cst[:, 0, :]), reads=[Cst_b], writes=[Cbf_b[nb]])
            P.op("pool", lambda e: e.tensor_copy(out=Cbf[nb][:, 1, :], in_=Cst[:, 1, :]), reads=[Cst_b], writes=[Cbf_b[nb]])
            P.op("pool", lambda e: e.tensor_copy(out=nbf[nb][:, :], in_=nst[:, :]), reads=[nst_b], writes=[nbf_b[nb]])
            P.op("dve", lambda e: e.tensor_copy(out=dn[j][:, :], in_=ps_sm[0:64, j:j + 1]),
                 reads=[ps_den_b[j]], writes=[dn_b[j]])
            P.op("dve", lambda e: e.scalar_tensor_tensor(out=dn[j][:, :], in0=dn[j][:, :], scalar=-1.0, in1=dn[j][:, :],
                                                         op0=ALU.mult, op1=ALU.max),
                 reads=[dn_b[j]], writes=[dn_b[j]])
            P.op("dve", lambda e: e.tensor_tensor(out=dn[j][:, :], in0=dn[j][:, :], in1=enm_col[:, p, c:c + 1], op=ALU.max),
                 reads=[dn_b[j], G], writes=[dn_b[j]])
            P.op("dve", lambda e: e.reciprocal(out=dn[j][:, :], in_=dn[j][:, :]), reads=[dn_b[j]], writes=[dn_b[j]])
            P.op("dve", lambda e: e.tensor_scalar(out=hraw[j][:, :], in0=ps_num[j][:, :], scalar1=dn[j][:, 0:1], scalar2=None, op0=ALU.mult),
                 reads=[ps_num_b[j], dn_b[j]], writes=[hraw_b[j]])
            P.op("act", lambda e: e.activation(out=junk[j][:, :], in_=hraw[j][:, :], func=AF.Square, accum_out=ss[j][:, :]),
                 reads=[hraw_b[j]], writes=[junk_b[j], ss_b[j]])
            P.op("act", lambda e: e.activation(out=ss[j][:, :], in_=ss[j][:, :], func=AF.Sqrt, bias=epsc[0:64, 0:1], scale=1.0 / DV),
                 reads=[ss_b[j], epsc_b], writes=[ss_b[j]])
            P.op("dve", lambda e: e.reciprocal(out=ss[j][:, :], in_=ss[j][:, :]), reads=[ss_b[j]], writes=[ss_b[j]])
            P.op("dve", lambda e: e.scalar_tensor_tensor(out=hn_[j][:, :], in0=hraw[j][:, :], scalar=ss[j][:, 0:1], in1=gn[:, p * DV:(p + 1) * DV],
                                                         op0=ALU.mult, op1=ALU.mult),
                 reads=[hraw_b[j], ss_b[j], gn_b], writes=[hn_b[j]])
            P.op("pool", lambda e: e.tensor_tensor(out=hog[i][:, n, :], in0=hn_[j][:, :], in1=sg[i][:, n, :], op=ALU.mult),
                 reads=[hn_b[j], sg_b[i]], writes=[hog_b[i]])

        def store_group(p, g):
            i = (p * NGR + g) % 2
            t0 = g * GT
            for n in range(GC):
                for fc in range(4):
                    P.op("pe", lambda e, n=n, fc=fc: e.transpose(psT[:, fc, n * L:(n + 1) * L], hog[i][:, n, fc * 128:(fc + 1) * 128], idb[:, :]),
                         reads=[hog_b[i], idb_b], writes=[psT_b])
            P.op("act", lambda e: e.copy(out=hTs[i][:, :, :], in_=psT[:, :, :]), reads=[psT_b], writes=[hTs_b[i]])
            piece, c0 = divmod(t0, TT)
            P.op("pool", lambda e: e.dma_start(out=srcH.ap()[piece, p * DV:(p + 1) * DV, c0:c0 + GT].rearrange("(f q) t -> q f t", q=128),
                                               in_=hTs[i][:, :, :]),
                 reads=[hTs_b[i]], writes=[srcH_b[piece]], dma=True)
            if p == NP - 1 and (t0 + GT) % TT == 0:
                P.op("pool", lambda e: e.collective_compute("AllGather", ALU.bypass, replica_groups=GROUPS,
                                                            ins=[srcH.ap()[piece]], outs=[dstH.ap()[(piece + 1) * KCD * 128:(piece + 2) * KCD * 128, :]]),
                     reads=[srcH_b[piece]], writes=[dstH_b], dma=True, inc=1)

        n_glob = 0
        seq = [(p, c) for p in range(NP) for c in range(NCH)]
        load_group(0, 0)
        for p in range(NP):
            P.op("sp", lambda e, p=p: e.dma_start(out=ao_bc[:, :], in_=bc_ap(gs_ao, p * NCH, 128, NCH)), reads=[GS], writes=[ao_bc_b], dma=True)
            P.op("dve", lambda e: e.memset(Cst[:, :, :], 0.0), writes=[Cst_b])
            P.op("dve", lambda e: e.memset(nst[:, :], 0.0), writes=[nst_b])
            P.op("pool", lambda e: e.memset(Cbf[0][:, :, :], 0.0), writes=[Cbf_b[0]])
            P.op("pool", lambda e: e.memset(nbf[0][:, :], 0.0), writes=[nbf_b[0]])
            for g in range(NGR):
                if g + 1 < NGR:
                    load_group(p, g + 1)
                elif p + 1 < NP:
                    load_group(p + 1, 0)
                if g == 0:
                    prep_group(p, 0)
                    stage_S(p, 0, n_glob)
                for n in range(GC):
                    c = g * GC + n
                    if n + 1 < GC:
                        stage_S(p, c + 1, n_glob + 1)
                    elif g + 1 < NGR:
                        prep_group(p, g + 1)
                        stage_S(p, c + 1, n_glob + 1)
                    stage_H(p, c, n_glob)
                    n_glob += 1
                store_group(p, g)
        P.end_phase()


def emit_moba(nc, P, cfg, io):
    S, NP, TT, KCD = cfg["S"], cfg["NP"], cfg["TT"], cfg["KC"]
    DH, BS = 128, 256
    NB = S // BS
    NQT = S // 128
    GW = max(NB, 8)
    SC = DH ** -0.5
    BIG = 30000.0
    PIECE = min(2048, S)
    NPC = S // PIECE
    OG = min(4, TT // 128)
    qT, kT, vtm, cmask, ident = (io[k] for k in ("qT", "kT", "vtm", "cmask", "ident"))
    srcA, dstA = io["srcA"], io["dstA"]

    with ExitStack() as st:

        def sb(name, shape, dt=F32):
            return st.enter_context(nc.sbuf_tensor("mo_" + name, list(shape), dt))

        def ps(name, shape, dt=F32):
            return st.enter_context(nc.psum_tensor("mo_" + name, list(shape), dt))

        idf = sb("idf", (128, 128)); idf_b = P.buf("idf")
        idb = sb("idb", (128, 128), BF16); idb_b = P.buf("idb")
        cm = sb("cm", (128, 2 * BS)); cm_b = P.buf("cm")
        P.op("sp", lambda e: e.dma_start(out=idf[:, :], in_=ident.ap()), writes=[idf_b], dma=True)
        P.op("sp", lambda e: e.dma_start(out=cm[:, :], in_=cmask.ap()), writes=[cm_b], dma=True)
        P.op("act", lambda e: e.copy(out=idb[:, :], in_=idf[:, :]), reads=[idf_b], writes=[idb_b])
        zt = sb("zt", (128, KCD * 2), BF16); zt_b = P.buf("zt")
        P.op("pool", lambda e: e.memset(zt[:, :], 0.0), writes=[zt_b])
        dstA_b = P.buf("dstA")
        srcA_b = P.bufs(S // TT, "srcA")
        P.op("sp", lambda e: e.dma_start(out=dstA.ap()[0:KCD * 128, TT - 2:TT].rearrange("(k p) c -> p k c", p=128),
                                         in_=zt[:, :].rearrange("p (k c) -> p k c", c=2)),
             reads=[zt_b], writes=[dstA_b], dma=True)

        stage = [sb(f"stage{i}", (128, PIECE)) for i in range(2)]; stage_b = P.bufs(2, "stage")
        qb = sb("qb", (128, S), BF16); qb_b = P.buf("qb")
        kb = sb("kb", (128, S), BF16); kb_b = P.buf("kb")
        vb = sb("vb", (128, NQT, DH), BF16); vb_b = P.buf("vb")
        ksum = sb("ksum", (128, NB)); km = sb("km", (128, NB)); km_b = P.buf("km")
        gate_all = sb("gate_all", (128, NQT, NB)); gate_b = P.buf("gate")
        gsel = [sb(f"gsel{i}", (128, GW)) for i in range(2)]; gsel_b = P.bufs(2, "gsel")
        mx8 = [sb(f"mx8{i}", (128, 8)) for i in range(2)]; mx8_b = P.bufs(2, "mx8")
        mb = [sb(f"mb{i}", (128, GW)) for i in range(2)]; mb_b = P.bufs(2, "mb")
        bm = [sb(f"bm{i}", (128, NB + 1)) for i in range(2)]; bm_b = P.bufs(2, "bm")
        mrow = [sb(f"mrow{i}", (128, 1)) for i in range(2)]; mrow_b = P.bufs(2, "mrow")
        rsum = [sb(f"rsum{i}", (128, 1)) for i in range(2)]; rsum_b = P.bufs(2, "rsum")
        Sb = [sb(f"Sb{i}", (128, S)) for i in range(2)]; Sb_b = P.bufs(2, "Sb")
        Pb = [sb(f"Pb{i}", (128, S), BF16) for i in range(2)]; Pb_b = P.bufs(2, "Pb")
        PT = [sb(f"PT{i}", (128, 8 * 128), BF16) for i in range(2)]; PT_b = P.bufs(2, "PT")
        ost = [sb(f"ost{i}", (128, OG, DH), BF16) for i in range(2)]; ost_b = P.bufs(2, "ost")
        oTs = [sb(f"oTs{i}", (128, OG * 128), BF16) for i in range(2)]; oTs_b = P.bufs(2, "oTs")

        ps_S = [ps(f"ps_S{i}", (128, 512)) for i in range(2)]; ps_S_b = P.bufs(2, "psS")
        ps_T = [ps(f"ps_T{i}", (128, 1024), BF16) for i in range(2)]; ps_T_b = P.bufs(2, "psT")
        ps_o = [ps(f"ps_o{i}", (128, DH)) for i in range(2)]; ps_o_b = P.bufs(2, "pso")
        ps_g = ps("ps_g", (128, GW)); ps_g_b = P.buf("psg")
        psO = ps("psO", (128, OG * 128), BF16); psO_b = P.buf("psO")

        cnt = {"st": 0, "S": 0, "T": 0}

        def next_stage():
            i = cnt["st"] % 2
            cnt["st"] += 1
            return i

        def prologue(p):
            for pc in range(NPC):
                i = next_stage()
                t0 = pc * PIECE
                P.op("sp", lambda e, i=i, t0=t0: e.dma_start(out=stage[i][:, :], in_=kT[p, :, t0:t0 + PIECE]), writes=[stage_b[i]], dma=True)
                P.op("act", lambda e, i=i, t0=t0: e.copy(out=kb[:, t0:t0 + PIECE], in_=stage[i][:, :]), reads=[stage_b[i]], writes=[kb_b])
                nb0 = t0 // BS
                nbp = PIECE // BS
                P.op("dve", lambda e, i=i, nb0=nb0, nbp=nbp: e.tensor_reduce(
                    out=ksum[:, nb0:nb0 + nbp], in_=stage[i][:, :].rearrange("d (n t) -> d n t", t=BS), axis=AX.X, op=ALU.add),
                    reads=[stage_b[i]], writes=[km_b])
            P.op("dve", lambda e: e.tensor_scalar(out=km[:, :], in0=ksum[:, :], scalar1=1.0 / BS, scalar2=None, op0=ALU.mult),
                 reads=[km_b], writes=[km_b])
            for pc in range(NPC):
                i = next_stage()
                t0 = pc * PIECE
                P.op("sp", lambda e, i=i, t0=t0: e.dma_start(out=stage[i][:, :].rearrange("i (n e) -> i n e", e=DH),
                                                             in_=vtm.ap()[t0:t0 + PIECE, p * DH:(p + 1) * DH].rearrange("(n i) e -> i n e", i=128)),
                     writes=[stage_b[i]], dma=True)
                c0 = t0 // 128
                P.op("pool", lambda e, i=i, c0=c0: e.tensor_copy(out=vb[:, c0:c0 + PIECE // 128, :],
                                                                 in_=stage[i][:, :].rearrange("i (n e) -> i n e", e=DH)),
                     reads=[stage_b[i]], writes=[vb_b])
            for pc in range(NPC):
                i = next_stage()
                t0 = pc * PIECE
                P.op("sp", lambda e, i=i, t0=t0: e.dma_start(out=stage[i][:, :], in_=qT[p, :, t0:t0 + PIECE]), writes=[stage_b[i]], dma=True)
                P.op("act", lambda e, i=i, t0=t0: e.copy(out=qb[:, t0:t0 + PIECE], in_=stage[i][:, :]), reads=[stage_b[i]], writes=[qb_b])
                for tl in range(PIECE // 128):
                    qt = t0 // 128 + tl
                    P.op("pe", lambda e, i=i, tl=tl: e.matmul(ps_g[:, 0:NB], stage[i][:, tl * 128:(tl + 1) * 128], km[:, :], start=True, stop=True),
                         reads=[stage_b[i], km_b], writes=[ps_g_b])
                    P.op("act", lambda e, qt=qt: e.copy(out=gate_all[:, qt, :], in_=ps_g[:, 0:NB]), reads=[ps_g_b], writes=[gate_b])

        def front(p, qt):
            bi, o = divmod(qt, 2)
            s = qt % 2
            nblk = bi + 1
            P.op("pool", lambda e: e.memset(gsel[s][:, :], -1e30), writes=[gsel_b[s]])
            if bi > 0:
                P.op("pool", lambda e: e.tensor_copy(out=gsel[s][:, 0:bi], in_=gate_all[:, qt, 0:bi]), reads=[gate_b], writes=[gsel_b[s]])
            P.op("dve", lambda e: e.max(out=mx8[s][:, :], in_=gsel[s][:, :]), reads=[gsel_b[s]], writes=[mx8_b[s]])
            P.op("dve", lambda e: e.tensor_scalar(out=mb[s][:, :], in0=gsel[s][:, :], scalar1=mx8[s][:, 2:3], scalar2=-BIG,
                                                  op0=ALU.is_lt, op1=ALU.mult),
                 reads=[gsel_b[s], mx8_b[s]], writes=[mb_b[s]])
            for n0 in range(0, nblk, 2):
                k = cnt["S"] % 2
                cnt["S"] += 1
                nbl = min(2, nblk - n0)
                ncols = nbl * BS
                P.op("pe", lambda e, k=k, n0=n0, ncols=ncols: e.matmul(ps_S[k][:, 0:ncols], qb[:, qt * 128:(qt + 1) * 128],
                                                                       kb[:, n0 * BS:n0 * BS + ncols], start=True, stop=True),
                     reads=[qb_b, kb_b], writes=[ps_S_b[k]])
                for h in range(nbl):
                    n = n0 + h
                    if n < bi:
                        P.op("dve", lambda e, k=k, n=n, h=h: e.tensor_scalar(
                            out=Sb[s][:, n * BS:(n + 1) * BS], in0=ps_S[k][:, h * BS:(h + 1) * BS], scalar1=mb[s][:, n:n + 1], scalar2=None,
                            op0=ALU.add, op1=ALU.max, accum_out=bm[s][:, n:n + 1]),
                            reads=[ps_S_b[k], mb_b[s]], writes=[Sb_b[s], bm_b[s]])
                    else:
                        P.op("dve", lambda e, k=k, n=n, h=h: e.tensor_tensor(
                            out=Sb[s][:, n * BS:(n + 1) * BS], in0=ps_S[k][:, h * BS:(h + 1) * BS], in1=cm[:, o * BS:(o + 1) * BS], op=ALU.add),
                            reads=[ps_S_b[k], cm_b], writes=[Sb_b[s]])
                        P.op("dve", lambda e, n=n: e.reduce_max(out=bm[s][:, n:n + 1], in_=Sb[s][:, n * BS:(n + 1) * BS], axis=AX.X),
                             reads=[Sb_b[s]], writes=[bm_b[s]])
            P.op("dve", lambda e: e.reduce_max(out=mrow[s][:, :], in_=bm[s][:, 0:nblk], axis=AX.X), reads=[bm_b[s]], writes=[mrow_b[s]])
            P.op("dve", lambda e: e.tensor_scalar(out=mrow[s][:, :], in0=mrow[s][:, :], scalar1=-SC, scalar2=None, op0=ALU.mult),
                 reads=[mrow_b[s]], writes=[mrow_b[s]])

        def front_exp(p, qt):
            bi, o = divmod(qt, 2)
            s = qt % 2
            nk = (bi + 1) * BS
            P.op("act", lambda e: e.activation(out=Pb[s][:, 0:nk], in_=Sb[s][:, 0:nk], func=AF.Exp, bias=mrow[s][:, 0:1], scale=SC,
                                               accum_out=rsum[s][:, :]),
                 reads=[Sb_b[s], mrow_b[s]], writes=[Pb_b[s], rsum_b[s]])

        def back(p, qt):
            bi, o = divmod(qt, 2)
            s = qt % 2
            nch = (bi + 1) * 2
            groups = [(g0, min(8, nch - g0)) for g0 in range(0, nch, 8)]
            oi = qt % 2
            slot = qt % OG
            osl = (qt // OG) % 2

            def emit_T(g0, ng):
                k = cnt["T"] % 2
                cnt["T"] += 1
                for j in range(ng):
                    kc = g0 + j
                    P.op("pe", lambda e, j=j, kc=kc: e.transpose(ps_T[k][:, j * 128:(j + 1) * 128], Pb[s][:, kc * 128:(kc + 1) * 128], idb[:, :]),
                         reads=[Pb_b[s], idb_b], writes=[ps_T_b[k]])
                P.op("act", lambda e: e.copy(out=PT[k][:, 0:ng * 128], in_=ps_T[k][:, 0:ng * 128]), reads=[ps_T_b[k]], writes=[PT_b[k]])
                return k

            def emit_PV(g0, ng, k):
                for j in range(ng):
                    kc = g0 + j
                    P.op("pe", lambda e, j=j, kc=kc: e.matmul(ps_o[oi][:, :], PT[k][:, j * 128:(j + 1) * 128], vb[:, kc, :],
                                                              start=(kc == 0), stop=(kc == nch - 1)),
                         reads=[PT_b[k], vb_b], writes=[ps_o_b[oi]])

            ks = [None] * len(groups)
            ks[0] = emit_T(*groups[0])
            for gi_, (g0, ng) in enumerate(groups):
                if gi_ + 1 < len(groups):
                    ks[gi_ + 1] = emit_T(*groups[gi_ + 1])
                emit_PV(g0, ng, ks[gi_])
            P.op("dve", lambda e: e.reciprocal(out=rsum[s][:, :], in_=rsum[s][:, :]), reads=[rsum_b[s]], writes=[rsum_b[s]])
            P.op("dve", lambda e: e.tensor_scalar(out=ost[osl][:, slot, :], in0=ps_o[oi][:, :], scalar1=rsum[s][:, 0:1], scalar2=None, op0=ALU.mult),
                 reads=[ps_o_b[oi], rsum_b[s]], writes=[ost_b[osl]])
            if slot == OG - 1:
                q0 = (qt - OG + 1) * 128
                for j in range(OG):
                    P.op("pe", lambda e, j=j: e.transpose(psO[:, j * 128:(j + 1) * 128], ost[osl][:, j, :], idb[:, :]),
                         reads=[ost_b[osl], idb_b], writes=[psO_b])
                P.op("act", lambda e: e.copy(out=oTs[osl][:, :], in_=psO[:, :]), reads=[psO_b], writes=[oTs_b[osl]])
                piece, c0 = divmod(q0, TT)
                P.op("pool", lambda e: e.dma_start(out=srcA.ap()[piece, p * DH:(p + 1) * DH, c0:c0 + OG * 128], in_=oTs[osl][:, :]),
                     reads=[oTs_b[osl]], writes=[srcA_b[piece]], dma=True)
                if p == NP - 1 and (q0 + OG * 128) % TT == 0:
                    P.op("pool", lambda e: e.collective_compute("AllGather", ALU.bypass, replica_groups=GROUPS,
                                                                ins=[srcA.ap()[piece]], outs=[dstA.ap()[(piece + 1) * KCD * 128:(piece + 2) * KCD * 128, :]]),
                         reads=[srcA_b[piece]], writes=[dstA_b], dma=True, inc=1)

        for p in range(NP):
            prologue(p)
            front(p, 0)
            front_exp(p, 0)
            for qt in range(NQT):
                if qt + 1 < NQT:
                    front(p, qt + 1)
                back(p, qt)
                if qt + 1 < NQT:
                    front_exp(p, qt + 1)
        P.end_phase()


def build_fused(cfg):
    S, D, F, TT, NG = cfg["S"], cfg["D"], cfg["F"], cfg["TT"], cfg["NG"]
    KC = D // 128
    FC = F // 128
    Q = S // 4
    NTQ = Q // TT
    NPCS = S // TT
    DQ = D // 4
    KQ = KC // 4
    DK, DV, DH = 256, 512, 128
    NP2 = DQ // DV
    NP4 = DQ // DH
    HL = 2
    PH = cfg.get("phases", "ABCDEF")
    nc = bass.Bass("TRN2", target_bir_lowering=False)

    def din(name, shape, dt=F32):
        return nc.dram_tensor(name, list(shape), dt, kind="ExternalInput")

    def dint(name, shape, dt=F32):
        return nc.dram_tensor(name, list(shape), dt, kind="Internal")

    xT_all = din("xT_all", (NPCS, D, TT))
    xq = din("xq", (NTQ, D, TT + HL))
    gA = din("gA", (128, KC))
    NFA = NP2 * 4 + 1
    NGA = NP2 * 3
    wA_fm = din("wA_fm", (NFA, 128, KC, 128))
    wA_tm = din("wA_tm", (NGA, 128, KC, 512))
    gbias = din("gbias", (128, NP2 * 2))
    gain = din("gain", (64, NP2 * DV))
    negmask = din("negmask", (64, 64))
    ident = din("ident", (128, 128))
    cmask = din("cmask", (128, 512))
    w_mix = [din(f"w_mix{i}", (KC, 128, KC, 128)) for i in range(2)]
    ffn = [dict(g=din(f"g_ffn{i}", (128, KC)), w_up=din(f"w_up{i}", (2 * FC, 128, KC, 128)),
                cw=din(f"cw{i}", (128, 3, 2 * FC)), cb=din(f"cb{i}", (128, 2 * FC)),
                w_dn=din(f"w_dn{i}", (KC, 128, FC, 128)), F=F, NG=NG) for i in range(2)]
    gD = din("gD", (128, KC))
    NGD = NP4 * DH // 512
    wD_fm = din("wD_fm", (2 * NP4, 128, KC, 128))
    wD_tm = din("wD_tm", (NGD, 128, KC, 512))
    g_fin = din("g_fin", (128, KC))
    xoT = nc.dram_tensor("xoT", [D, Q], F32, kind="ExternalOutput")

    qT2 = dint("qT2", (NP2, DK, S)); kT2 = dint("kT2", (NP2, DK, S))
    ktm2 = dint("ktm2", (NP2, S, DK)); vtm2 = dint("vtm2", (NP2, S, DV)); otm2 = dint("otm2", (NP2, S, DV))
    gi = dint("gi", (NP2, S)); gf = dint("gf", (NP2, S))
    gs_u = dint("gs_u", (NP2, S)); gs_a = dint("gs_a", (NP2, S)); gs_ao = dint("gs_ao", (NP2, S // 64))
    srcH = dint("srcH", (NPCS, DQ, TT), BF16); dstH = dint("dstH", ((NPCS + 1) * D, TT), BF16)
    x1s = dint("x1s", (D, HL + Q)); srcB = dint("srcB", (D, HL)); dstB = dint("dstB", (5 * D, HL))
    srcX = dint("srcX", (NTQ * 4, DQ, TT), BF16); dstX = dint("dstX", (NTQ * 4, 4 * DQ, TT), BF16)
    qT4 = dint("qT4", (NP4, DH, S)); kT4 = dint("kT4", (NP4, DH, S)); vtm4 = dint("vtm4", (S, NP4 * DH))
    srcA = dint("srcA", (NPCS, DQ, TT), BF16); dstA = dint("dstA", ((NPCS + 1) * D, TT), BF16)

    def w2d(t, i):
        return t[i].rearrange("p k j -> p (k j)")

    def gathered_mix(dst, w):
        return dict(
            w=w,
            main=lambda e, tt: dst.ap()[bass.ds((P.rank(e) * NTQ + 1 + tt) * D, D), :],
            halo=lambda e, tt: dst.ap()[bass.ds((P.rank(e) * NTQ + tt) * D, D), TT - HL:TT])

    with ExitStack() as gst:
        P = Prog(nc, gst)

        fmA = []
        for p in range(NP2):
            for j in range(2):
                fmA.append((w2d(wA_fm, p * 4 + j),
                            lambda tt, ev, p=p, j=j: [(qT2.ap()[p, j * 128:(j + 1) * 128, tt * TT:(tt + 1) * TT], ev[:, 0:TT])]))
            for j in range(2):
                fmA.append((w2d(wA_fm, p * 4 + 2 + j),
                            lambda tt, ev, p=p, j=j: [(kT2.ap()[p, j * 128:(j + 1) * 128, tt * TT:(tt + 1) * TT], ev[:, 0:TT])]))
        fmA.append((w2d(wA_fm, NP2 * 4),
                    lambda tt, ev: [(gi.ap()[0:NP2, tt * TT:(tt + 1) * TT], ev[0:NP2, 0:TT]),
                                    (gf.ap()[0:NP2, tt * TT:(tt + 1) * TT], ev[NP2:2 * NP2, 0:TT])]))
        tmA = []
        for p in range(NP2):
            for j, (dt_, ncols) in enumerate(((ktm2, DK), (vtm2, DV), (otm2, DV))):
                tmA.append((w2d(wA_tm, p * 3 + j), ncols,
                            lambda tt, tb, p=p, dt_=dt_: dt_.ap()[p, tt * TT + tb * 128:tt * TT + (tb + 1) * 128, :]))
        if "A" in PH: emit_dense(nc, P, dict(name="A", NT=NPCS, TT=TT, HL=0, D=D, src="x",
                               x_ap=lambda tt, kc: xT_all[tt, kc * 128:(kc + 1) * 128, :],
                               nxt=dict(g=gA, fm=fmA, tm=tmA)))

        if "B" in PH: emit_mlstm(nc, P, dict(S=S, NP=NP2, TT=TT, KC=KC),
                   dict(qT=qT2, kT=kT2, ktm=ktm2, vtm=vtm2, otm=otm2, gi=gi, gf=gf, gbias=gbias, gain=gain,
                        negmask=negmask, ident=ident, gs_u=gs_u, gs_a=gs_a, gs_ao=gs_ao, srcH=srcH, dstH=dstH, dstB=dstB))

        x1s_b = P.buf("x1s")
        srcB_b = P.buf("srcB")
        dstB_b = P.buf("dstB")
        srcX_b = P.bufs(NTQ * 4, "srcX")
        dstX_b = P.buf("dstX")

        def after_ffn_C(tt, x1, x1_b, HL_, TW):
            for kc in range(KC):
                P.op("sp", lambda e, kc=kc: e.dma_start(out=x1s.ap()[kc * 128:(kc + 1) * 128, HL + tt * TT:HL + (tt + 1) * TT],
                                                        in_=x1[:, kc, HL_:TW]),
                     reads=[x1_b[kc]], writes=[x1s_b], dma=True)
            if tt == NTQ - 1:
                P.op("sp", lambda e: e.dma_start(out=srcB.ap().rearrange("(k p) c -> p k c", p=128), in_=x1[:, :, TW - HL:TW]),
                     reads=x1_b, writes=[srcB_b], dma=True)
                P.op("pool", lambda e: e.collective_compute("AllGather", ALU.bypass, replica_groups=GROUPS,
                                                            ins=[srcB.ap()], outs=[dstB.ap()[D:5 * D, :]]),
                     reads=[srcB_b], writes=[dstB_b], dma=True, inc=1)
                P.op("sp", lambda e: e.dma_start(out=x1s.ap()[:, 0:HL], in_=dstB.ap()[bass.ds(P.rank(e) * D, D), :]),
                     reads=[dstB_b], writes=[x1s_b], dma=True)

        def hn_store_C(tt, hn, hn_b, HL_, TW):
            for fb in range(4):
                c = tt * 4 + fb
                P.op("sp", lambda e, c=c, fb=fb: e.dma_start(out=srcX.ap()[c].rearrange("(k p) t -> p k t", p=128),
                                                             in_=hn[:, fb * KQ:(fb + 1) * KQ, HL_:TW]),
                     reads=hn_b[fb * KQ:(fb + 1) * KQ], writes=[srcX_b[c]], dma=True)
                P.op("pool", lambda e, c=c: e.collective_compute("AllGather", ALU.bypass, replica_groups=GROUPS,
                                                                 ins=[srcX.ap()[c]], outs=[dstX.ap()[c]]),
                     reads=[srcX_b[c]], writes=[dstX_b], dma=True, inc=1)

        if "C" in PH: emit_dense(nc, P, dict(name="C", NT=NTQ, TT=TT, HL=HL, D=D, src="x",
                               x_ap=lambda tt, kc: xq[tt, kc * 128:(kc + 1) * 128, :],
                               mix=gathered_mix(dstH, w_mix[0]), ffn=ffn[0], after_ffn=after_ffn_C, NWB=6,
                               nxt=dict(g=gD, hn_store=hn_store_C)))

        def hn_load_D(tg):
            r, tt = divmod(tg, NTQ)
            return [(fb * KQ, KQ, dstX.ap()[tt * 4 + fb, r * DQ:(r + 1) * DQ, :]) for fb in range(4)]

        fmD = []
        for p in range(NP4):
            fmD.append((w2d(wD_fm, p), lambda tg, ev, p=p: [(qT4.ap()[p, :, tg * TT:(tg + 1) * TT], ev[:, 0:TT])]))
        for p in range(NP4):
            fmD.append((w2d(wD_fm, NP4 + p), lambda tg, ev, p=p: [(kT4.ap()[p, :, tg * TT:(tg + 1) * TT], ev[:, 0:TT])]))
        tmD = [(w2d(wD_tm, g), 512, lambda tg, tb, g=g: vtm4.ap()[tg * TT + tb * 128:tg * TT + (tb + 1) * 128, g * 512:(g + 1) * 512])
               for g in range(NGD)]
        if "D" in PH: emit_dense(nc, P, dict(name="D", NT=NPCS, TT=TT, HL=0, D=D, src="hn", hn_load=hn_load_D, NWB=6,
                               nxt=dict(g=None, fm=fmD, tm=tmD)))

        if "E" in PH: emit_moba(nc, P, dict(S=S, NP=NP4, TT=TT, KC=KC),
                  dict(qT=qT4, kT=kT4, vtm=vtm4, cmask=cmask, ident=ident, srcA=srcA, dstA=dstA))

        if "F" in PH: emit_dense(nc, P, dict(name="F", NT=NTQ, TT=TT, HL=HL, D=D, src="x",
                               x_ap=lambda tt, kc: x1s.ap()[kc * 128:(kc + 1) * 128, tt * TT:(tt + 1) * TT + HL],
                               mix=gathered_mix(dstA, w_mix[1]), ffn=ffn[1], NWB=6,
                               final=dict(g=g_fin, out=lambda tt, kc: xoT.ap()[kc * 128:(kc + 1) * 128, tt * TT:(tt + 1) * TT])))
        if SINGLE_BLOCK:
            P.emit()
        print("fused program built; dma semaphores:", P.ndsem, flush=True)
    return nc


_progs = {}


def _prog(key, cfg):
    if key not in _progs:
        _progs[key] = build_fused(cfg)
    return _progs[key]


def run_fused(cfg, x, norm_mix, norm_ffn, a_w_in, a_gate_bias, a_head_norm, a_w_out,
              b_w_qkv, b_w_out, ffn_w_up, ffn_conv_w, ffn_conv_b, ffn_w_down, final_norm, trace=False):
    f32 = np.float32
    S, D, F, TT = cfg["S"], cfg["D"], cfg["F"], cfg["TT"]
    B = x.shape[0]
    assert B * 4 == N_CORES
    Q = S // 4
    NTQ = Q // TT
    NPCS = S // TT
    DQ = D // 4
    DK, DV, DH = 256, 512, 128
    MLH, MBH = D // DV, D // DH
    NP2, NP4 = DQ // DV, DQ // DH
    HL = 2
    nc = _prog((S, D, F, TT, cfg.get("phases", "ABCDEF")), cfg)

    negmask = np.where(np.arange(64)[None, :] >= np.arange(64)[:, None], 0.0, -30000.0).astype(f32)
    ident = np.eye(128, dtype=f32)
    cmask = np.zeros((128, 512), f32)
    for o in range(2):
        cmask[:, o * 256:(o + 1) * 256] = np.where(np.arange(256)[None, :] <= o * 128 + np.arange(128)[:, None], 0.0, -30000.0)
    shared = dict(gA=lay_vec(norm_mix[0]), gD=lay_vec(norm_mix[1]), g_fin=lay_vec(final_norm),
                  negmask=negmask, ident=ident, cmask=cmask,
                  w_mix0=lay_w(a_w_out[0]), w_mix1=lay_w(b_w_out[0]))
    for i in range(2):
        shared[f"g_ffn{i}"] = lay_vec(norm_ffn[i])
        shared[f"w_up{i}"] = lay_w(ffn_w_up[i])
        shared[f"cw{i}"] = np.ascontiguousarray(np.stack([lay_vec(ffn_conv_w[i, j]) for j in range(3)], axis=1))
        shared[f"cb{i}"] = lay_vec(ffn_conv_b[i])
        shared[f"w_dn{i}"] = lay_w(ffn_w_down[i])

    s1, s2 = MLH * DK, 2 * MLH * DK
    s3 = s2 + MLH * DV
    s4 = s3 + D
    Wa = a_w_in[0]
    Wq = b_w_qkv[0]
    per_rank = []
    for r in range(4):
        heads = [r * NP2 + p for p in range(NP2)]
        cols = []
        for h in heads:
            cols.append(Wa[:, h * DK:(h + 1) * DK])
            cols.append(Wa[:, s1 + h * DK:s1 + (h + 1) * DK])
        gate_cols = np.zeros((D, 128), f32)
        for p, h in enumerate(heads):
            gate_cols[:, p] = Wa[:, s4 + h]
            gate_cols[:, NP2 + p] = Wa[:, s4 + MLH + h]
        cols.append(gate_cols)
        wA_fm = lay_w(np.concatenate(cols, axis=1))
        tcols = []
        for h in heads:
            kpad = np.zeros((D, 512), f32)
            kpad[:, :DK] = Wa[:, s1 + h * DK:s1 + (h + 1) * DK]
            tcols += [kpad, Wa[:, s2 + h * DV:s2 + (h + 1) * DV], Wa[:, s3 + h * DV:s3 + (h + 1) * DV]]
        wA_tm = lay_w_tm(np.concatenate(tcols, axis=1))
        gb = np.array([[a_gate_bias[0, h], a_gate_bias[0, MLH + h]] for h in heads], f32).reshape(1, -1)
        gn = np.concatenate([a_head_norm[0, h * DV:(h + 1) * DV] for h in heads]).reshape(1, -1)
        mh = [r * NP4 + p for p in range(NP4)]
        wD_fm = lay_w(np.concatenate([Wq[:, h * DH:(h + 1) * DH] for h in mh] +
                                     [Wq[:, D + h * DH:D + (h + 1) * DH] for h in mh], axis=1))
        wD_tm = lay_w_tm(np.concatenate([Wq[:, 2 * D + h * DH:2 * D + (h + 1) * DH] for h in mh], axis=1))
        per_rank.append(dict(wA_fm=wA_fm, wA_tm=wA_tm,
                             gbias=np.ascontiguousarray(np.broadcast_to(gb, (128, gb.shape[1]))),
                             gain=np.ascontiguousarray(np.broadcast_to(gn, (64, gn.shape[1]))),
                             wD_fm=wD_fm, wD_tm=wD_tm))
    maps = []
    for b in range(B):
        xTb = np.ascontiguousarray(x[b].T)
        xT_all = np.ascontiguousarray(xTb.reshape(D, NPCS, TT).transpose(1, 0, 2))
        for r in range(4):
            xq = np.zeros((NTQ, D, TT + HL), f32)
            for tt in range(NTQ):
                s0 = r * Q + tt * TT
                if s0 == 0:
                    xq[tt, :, HL:] = xTb[:, 0:TT]
                else:
                    xq[tt] = xTb[:, s0 - HL:s0 + TT]
            maps.append(dict(shared, **per_rank[r], xT_all=xT_all, xq=xq))
        del xTb
    res = run_bass_kernel_spmd(nc, maps, core_ids=list(range(N_CORES)), trace=trace)
    out = np.empty((B, S, D), f32)
    for c in range(N_CORES):
        b, r = divmod(c, 4)
        out[b, r * Q:(r + 1) * Q, :] = res.results[c]["xoT"].T
    return out, res


def kernel(x, norm_mix, norm_ffn, a_w_in, a_gate_bias, a_head_norm, a_w_out,
           b_w_qkv, b_w_out, ffn_w_up, ffn_conv_w, ffn_conv_b, ffn_w_down, final_norm):
    f32 = np.float32
    args = [x, norm_mix, norm_ffn, a_w_in, a_gate_bias, a_head_norm, a_w_out, b_w_qkv, b_w_out,
            ffn_w_up, ffn_conv_w, ffn_conv_b, ffn_w_down, final_norm]
    args = [np.asarray(a, f32) for a in args]
    cfg = dict(S=8192, D=4096, F=11008, TT=512, NG=4)
    out, _ = run_fused(cfg, *args)
    return out
```
